# Optimizing a Trainium2 kernel written in Bass

```python
import math, functools
import jax, jax.numpy as jnp
from jax import lax
import numpy as np

D_MODEL = 1024
BATCH = 8
SEQ = 2048
DEPTH = 1
DEC_BATCH = 128
DEC_SEQ = 1
PAST_LEN = 2048
PAGE_SIZE = 128

HEAD_DIM = 64
N_DIFF_HEADS = 4
N_FOX_HEADS = 8
DIFF_QK_W = N_DIFF_HEADS * 2 * HEAD_DIM
DIFF_V_W = N_DIFF_HEADS * 2 * HEAD_DIM
FOX_W = N_FOX_HEADS * HEAD_DIM
N_BRANCHES = 2
IN_COLS = 2 * DIFF_QK_W + DIFF_V_W + 3 * FOX_W + N_FOX_HEADS + N_BRANCHES * D_MODEL
D_FF = -(-8 * D_MODEL // (3 * 256)) * 256
N_BUCKETS = 32
MAX_DISTANCE = 128
Q_BLOCK = 128
EPS = 1e-6
ATTN_SCALE = HEAD_DIM ** -0.5
NEG_INF = -1e30

kernel_name = 'hybrid_diff_fox_decoder_step'


def rms_norm(x, g):
    xf = x.astype(jnp.float32)
    y = xf * lax.rsqrt(jnp.mean(xf * xf, axis=-1, keepdims=True) + EPS)
    return (y * g.astype(jnp.float32)).astype(x.dtype)


def rel_bucket(q_pos, k_pos):
    n = jnp.maximum(q_pos[:, None] - k_pos[None, :], 0)
    max_exact = N_BUCKETS // 2
    nf = jnp.maximum(n, 1).astype(jnp.float32)
    large = max_exact + (jnp.log(nf / max_exact) / math.log(MAX_DISTANCE / max_exact)
                         * (N_BUCKETS - max_exact)).astype(jnp.int32)
    return jnp.where(n < max_exact, n, jnp.minimum(large, N_BUCKETS - 1))


def lambda_value(lam_p, lambda_init):
    lp = lam_p.astype(jnp.float32)
    return jnp.exp(jnp.sum(lp[0] * lp[1])) - jnp.exp(jnp.sum(lp[2] * lp[3])) + lambda_init


def diff_attend(q, k, v, q_pos, k_pos, rel_bias, lam):
    s = jnp.einsum('bqhmd,bkhmd->bmhqk', q, k, preferred_element_type=jnp.float32) * ATTN_SCALE
    bias = jnp.transpose(rel_bias[rel_bucket(q_pos, k_pos)], (2, 0, 1)).astype(jnp.float32)
    mask = k_pos[None, :] <= q_pos[:, None]
    p = jax.nn.softmax(jnp.where(mask, s + bias, NEG_INF), axis=-1)
    a = p[:, 0] - lam * p[:, 1]
    return jnp.einsum('bhqk,bkhe->bqhe', a.astype(v.dtype), v)


def fox_attend(q, k, v, cq, ck, q_pos, k_pos):
    s = jnp.einsum('bqhd,bkhd->bhqk', q, k, preferred_element_type=jnp.float32) * ATTN_SCALE
    decay = jnp.swapaxes(cq, 1, 2)[:, :, :, None] - jnp.swapaxes(ck, 1, 2)[:, :, None, :]
    mask = k_pos[None, :] <= q_pos[:, None]
    p = jax.nn.softmax(jnp.where(mask, s + decay, NEG_INF), axis=-1)
    return jnp.einsum('bhqk,bkhe->bqhe', p.astype(v.dtype), v)


def project_inputs(h, w_in, diff_q_g, diff_k_g, fox_q_g, fox_k_g, fox_f_b, gate_b):
    B, T, _ = h.shape
    z = jnp.einsum('btd,dc->btc', h, w_in)
    o = [int(v) for v in np.cumsum([0, DIFF_QK_W, DIFF_QK_W, DIFF_V_W, FOX_W, FOX_W, FOX_W,
                                    N_FOX_HEADS, N_BRANCHES * D_MODEL])]
    dq = z[..., o[0]:o[1]].reshape(B, T, N_DIFF_HEADS, 2, HEAD_DIM)
    dk = z[..., o[1]:o[2]].reshape(B, T, N_DIFF_HEADS, 2, HEAD_DIM)
    dv = z[..., o[2]:o[3]].reshape(B, T, N_DIFF_HEADS, 2 * HEAD_DIM)
    fq = z[..., o[3]:o[4]].reshape(B, T, N_FOX_HEADS, HEAD_DIM)
    fk = z[..., o[4]:o[5]].reshape(B, T, N_FOX_HEADS, HEAD_DIM)
    fv = z[..., o[5]:o[6]].reshape(B, T, N_FOX_HEADS, HEAD_DIM)
    logf = jax.nn.log_sigmoid((z[..., o[6]:o[7]] + fox_f_b).astype(jnp.float32))
    gates = jax.nn.sigmoid(z[..., o[7]:o[8]].reshape(B, T, N_BRANCHES, D_MODEL) + gate_b)
    return (rms_norm(dq, diff_q_g), rms_norm(dk, diff_k_g), dv,
            rms_norm(fq, fox_q_g), rms_norm(fk, fox_k_g), fv, logf, gates)


def attend_prompt(dq, dk, dv, fq, fk, fv, logf, lam, rel_bias):
    B, S = dq.shape[:2]
    nblk = S // Q_BLOCK
    pos = jnp.arange(S, dtype=jnp.int32)
    c = jnp.cumsum(logf, axis=1)

    def to_blocks(a):
        return jnp.swapaxes(a.reshape((B, nblk, Q_BLOCK) + a.shape[2:]), 0, 1)

    def from_blocks(a):
        return jnp.swapaxes(a, 0, 1).reshape((B, S) + a.shape[3:])

    def block(args):
        qd, qf, cq, qpos = args
        return (diff_attend(qd, dk, dv, qpos, pos, rel_bias, lam),
                fox_attend(qf, fk, fv, cq, c, qpos, pos))

    od, of = lax.map(block, (to_blocks(dq), to_blocks(fq), to_blocks(c), pos.reshape(nblk, Q_BLOCK)))
    return from_blocks(od), from_blocks(of)


def attend_sample(past_dk, past_dv, past_fk, past_fv, past_logf,
                  dq, dk, dv, fq, fk, fv, logf, lam, rel_bias):
    P, T = past_dk.shape[1], dq.shape[1]
    kd = jnp.concatenate([past_dk.astype(dk.dtype), dk], axis=1)
    vd = jnp.concatenate([past_dv.astype(dv.dtype), dv], axis=1)
    kf = jnp.concatenate([past_fk.astype(fk.dtype), fk], axis=1)
    vf = jnp.concatenate([past_fv.astype(fv.dtype), fv], axis=1)
    c = jnp.cumsum(jnp.concatenate([past_logf.astype(jnp.float32), logf], axis=1), axis=1)
    kpos = jnp.arange(P + T, dtype=jnp.int32)
    qpos = P + jnp.arange(T, dtype=jnp.int32)
    od = diff_attend(dq, kd, vd, qpos, kpos, rel_bias, lam)
    of = fox_attend(fq, kf, vf, c[:, P:], c, qpos, kpos)
    return od, of


def gather_pages(cache, page_table):
    g = jnp.take(cache, page_table, axis=0)
    return g.reshape((page_table.shape[0], page_table.shape[1] * cache.shape[1]) + cache.shape[2:])


def swiglu(h, w_gate_up, w_down):
    gu = h @ w_gate_up
    return (jax.nn.silu(gu[..., :D_FF]) * gu[..., D_FF:]) @ w_down


def block_forward(x, attend, lw, rel_bias, lambda_init):
    (norm1_g, w_in, dqg, dkg, fqg, fkg, lam_p, fox_f_b, gate_b, subln_g,
     w_a, w_b, w_out, norm2_g, w_gu, w_dn) = lw
    B, T = x.shape[:2]
    h = rms_norm(x, norm1_g)
    dq, dk, dv, fq, fk, fv, logf, gates = project_inputs(h, w_in, dqg, dkg, fqg, fkg, fox_f_b, gate_b)
    lam = lambda_value(lam_p, lambda_init)
    od, of = attend(dq, dk, dv, fq, fk, fv, logf, lam, rel_bias)
    od = rms_norm(od, subln_g) * (1.0 - lambda_init)
    ya = od.reshape(B, T, DIFF_V_W) @ w_a
    yb = of.reshape(B, T, FOX_W) @ w_b
    x = x + (gates[:, :, 0] * ya + gates[:, :, 1] * yb) @ w_out
    x = x + swiglu(rms_norm(x, norm2_g), w_gu, w_dn)
    return x, (dk, dv, fk, fv, logf)


def stack_rows(rows, i):
    return jnp.stack([r[i] for r in rows], axis=0)


def setup_inputs(seed: int = 0) -> dict:
    key = jax.random.key(seed)
    ks = jax.random.split(key, 32)
    n_pages = PAST_LEN // PAGE_SIZE
    n_used = DEC_BATCH * n_pages
    n_phys = n_used + max(1, n_used // 4)

    def nrm(k, shape, s):
        return s * jax.random.normal(k, shape, jnp.float32)

    def gain(k, shape):
        return 1.0 + nrm(k, shape, 0.02)

    pool = (DEPTH, n_phys, PAGE_SIZE)
    return {
        'x_prompt': nrm(ks[0], (BATCH, SEQ, D_MODEL), 1.0),
        'x_sample': nrm(ks[1], (DEC_BATCH, DEC_SEQ, D_MODEL), 1.0),
        'cache_diff_k': nrm(ks[2], pool + (N_DIFF_HEADS, 2, HEAD_DIM), 1.0),
        'cache_diff_v': nrm(ks[3], pool + (N_DIFF_HEADS, 2 * HEAD_DIM), 1.0),
        'cache_fox_k': nrm(ks[4], pool + (N_FOX_HEADS, HEAD_DIM), 1.0),
        'cache_fox_v': nrm(ks[5], pool + (N_FOX_HEADS, HEAD_DIM), 1.0),
        'cache_fox_logf': jax.nn.log_sigmoid(2.0 + nrm(ks[6], pool + (N_FOX_HEADS,), 0.5)),
        'page_table': jax.random.permutation(ks[7], n_phys)[:n_used].reshape(DEC_BATCH, n_pages).astype(jnp.int32),
        'rel_bias': nrm(ks[8], (N_BUCKETS, N_DIFF_HEADS), 0.5),
        'norm1_g': gain(ks[9], (DEPTH, D_MODEL)),
        'w_in': nrm(ks[10], (DEPTH, D_MODEL, IN_COLS), D_MODEL ** -0.5),
        'diff_q_g': gain(ks[11], (DEPTH, HEAD_DIM)),
        'diff_k_g': gain(ks[12], (DEPTH, HEAD_DIM)),
        'fox_q_g': gain(ks[13], (DEPTH, HEAD_DIM)),
        'fox_k_g': gain(ks[14], (DEPTH, HEAD_DIM)),
        'diff_lambda': nrm(ks[15], (DEPTH, 4, HEAD_DIM), 0.1),
        'fox_f_b': 2.0 + nrm(ks[16], (DEPTH, N_FOX_HEADS), 0.5),
        'gate_b': nrm(ks[17], (DEPTH, N_BRANCHES, D_MODEL), 0.1),
        'diff_subln_g': gain(ks[18], (DEPTH, 2 * HEAD_DIM)),
        'w_branch_a': nrm(ks[19], (DEPTH, DIFF_V_W, D_MODEL), DIFF_V_W ** -0.5),
        'w_branch_b': nrm(ks[20], (DEPTH, FOX_W, D_MODEL), FOX_W ** -0.5),
        'w_out': nrm(ks[21], (DEPTH, D_MODEL, D_MODEL), D_MODEL ** -0.5),
        'norm2_g': gain(ks[22], (DEPTH, D_MODEL)),
        'w_gate_up': nrm(ks[23], (DEPTH, D_MODEL, 2 * D_FF), D_MODEL ** -0.5),
        'w_down': nrm(ks[24], (DEPTH, D_FF, D_MODEL), D_FF ** -0.5),
    }


def reference(x_prompt, x_sample, cache_diff_k, cache_diff_v, cache_fox_k, cache_fox_v, cache_fox_logf,
              page_table, rel_bias, norm1_g, w_in, diff_q_g, diff_k_g, fox_q_g, fox_k_g, diff_lambda,
              fox_f_b, gate_b, diff_subln_g, w_branch_a, w_branch_b, w_out, norm2_g, w_gate_up, w_down):
    yp, ys = x_prompt, x_sample
    rows_p, rows_s = [], []
    for l in range(DEPTH):
        lambda_init = 0.8 - 0.6 * math.exp(-0.3 * l)
        lw = (norm1_g[l], w_in[l], diff_q_g[l], diff_k_g[l], fox_q_g[l], fox_k_g[l], diff_lambda[l],
              fox_f_b[l], gate_b[l], diff_subln_g[l], w_branch_a[l], w_branch_b[l], w_out[l],
              norm2_g[l], w_gate_up[l], w_down[l])
        yp, rp = block_forward(yp, attend_prompt, lw, rel_bias, lambda_init)
        samp = functools.partial(attend_sample,
                                 gather_pages(cache_diff_k[l], page_table),
                                 gather_pages(cache_diff_v[l], page_table),
                                 gather_pages(cache_fox_k[l], page_table),
                                 gather_pages(cache_fox_v[l], page_table),
                                 gather_pages(cache_fox_logf[l], page_table))
        ys, rs = block_forward(ys, samp, lw, rel_bias, lambda_init)
        rows_p.append(rp)
        rows_s.append(rs)
    return (yp, ys,
            stack_rows(rows_p, 0), stack_rows(rows_p, 1), stack_rows(rows_p, 2), stack_rows(rows_p, 3), stack_rows(rows_p, 4),
            stack_rows(rows_s, 0), stack_rows(rows_s, 1), stack_rows(rows_s, 2), stack_rows(rows_s, 3), stack_rows(rows_s, 4))
```

```python
import math
import numpy as np
import concourse.bass as bass
import concourse.mybir as mybir
from concourse.bass_utils import run_bass_kernel_spmd
from contextlib import ExitStack

F32 = mybir.dt.float32
BF16 = mybir.dt.bfloat16
I32 = mybir.dt.int32
AF = mybir.ActivationFunctionType
ALU = mybir.AluOpType
AX = mybir.AxisListType

NCORES = 8
D = 1024
SEQ = 2048
NT = 16
NS = 16
HD = 64
QKVW = 3080
INC = 5128
DFF = 2816
NFC = 22
NPHYS = 2560
NPG = 16
EPS = 1e-6
LAMBDA_INIT = 0.8 - 0.6 * math.exp(-0.3 * 0)
NEG = -30000.0
DBG_TILES = list(range(NT + 1))
DBG_STAGE = 99
DBG_OUT = False

O_DQ, O_DK, O_DV, O_FQ, O_FK, O_FV, O_LF, O_G = 0, 512, 1024, 1536, 2048, 2560, 3072, 3080

C_ID, C_LTRI, C_ONES, C_SEL64, C_UTRI, C_MTRI = 0, 128, 256, 384, 512, 640
C_J = 768
C_MAIN = 896
C_DM8, C_DM4, C_ES, C_OH, C_MROW = 896, 1408, 1920, 2176, 2560
C_PIO = 2560
C_N = 2944


def _rel_bucket(n):
    n = max(n, 0)
    if n < 16:
        return n
    large = 16 + int(np.float32(np.log(np.float32(max(n, 1)) / np.float32(16)) / np.float32(math.log(128 / 16)) * np.float32(16)))
    return min(large, 31)


def make_consts():
    c = np.zeros((128, C_N), np.float32)
    i = np.arange(128)
    c[:, C_ID:C_ID + 128] = np.eye(128)
    c[:, C_LTRI:C_LTRI + 128] = (i[:, None] <= i[None, :])
    c[:, C_ONES:C_ONES + 128] = 1.0
    c[64, C_SEL64:C_SEL64 + 128] = 1.0
    c[:, C_UTRI:C_UTRI + 128] = (i[:, None] > i[None, :])
    c[:, C_MTRI:C_MTRI + 128] = np.where(i[:, None] > i[None, :], NEG, 0.0)
    c[:, C_J:C_J + 128] = np.eye(128)[::-1]
    c[:, C_PIO] = np.arange(128)
    for h in range(8):
        c[h, C_DM8 + h * 64:C_DM8 + (h + 1) * 64] = 1.0
    for h in range(4):
        c[h, C_DM4 + h * 128:C_DM4 + (h + 1) * 128] = 1.0
    for s in range(16):
        c[0:8, C_ES + s * 16 + s] = 1.0
    for idx in range(383):
        n = idx - 127
        if n >= 0:
            b = _rel_bucket(n)
            c[b, C_OH + idx] += 1.0
            c[31, C_OH + idx] -= 1.0
        else:
            c[32, C_OH + idx] = NEG
    return c


class Buf:
    __slots__ = ("name", "w", "readers", "dsem", "dtotal")

    def __init__(self, name):
        self.name = name
        self.w = None
        self.readers = []
        self.dsem = None
        self.dtotal = 0


class Prog:
    ENGS = ("pe", "act", "dve", "pool", "sp")

    def __init__(self, nc, stack):
        self.nc = nc
        self.stack = stack
        self.q = {e: [] for e in self.ENGS}
        self.esem = {e: stack.enter_context(nc.semaphore("es_" + e)) for e in self.ENGS}
        self.ecnt = {e: 0 for e in self.ENGS}
        self.waited = {e: {} for e in self.ENGS}
        self.dsems = []
        self.nd = 0
        self.phase_sem = stack.enter_context(nc.semaphore("phase"))
        self.nphase = 0
        self.ninstr = 0

    def _dsem(self, b):
        if b.dsem is None:
            b.dsem = self.stack.enter_context(self.nc.semaphore("ds%d" % self.nd))
            self.nd += 1
            self.dsems.append(b)
        return b.dsem

    def _wait(self, E, tok):
        if tok is None:
            return
        if tok[0] == "e":
            name, val = tok[1], tok[2]
            if name == E and E == "pe":
                return
            key = ("e", name)
            sem = self.esem[name]
        else:
            sb = tok[1]
            key = ("d", id(sb))
            sem = sb.dsem
            val = sb.dtotal
        if self.waited[E].get(key, 0) >= val:
            return
        self.waited[E][key] = val
        self.q[E].append(lambda h, sem=sem, val=val: h.wait_ge(sem, val))

    def _deps(self, E, reads, writes):
        for b in reads:
            self._wait(E, b.w)
        for b in writes:
            if not (b.w is not None and b.w[0] == "e" and b.w[1] == E):
                self._wait(E, b.w)
            for r in b.readers:
                if r[0] == "e" and r[1] == E:
                    continue
                self._wait(E, r)

    def op(self, E, fn, reads=(), writes=()):
        self._deps(E, reads, writes)
        self.ecnt[E] += 1
        tok = ("e", E, self.ecnt[E])
        sem = self.esem[E]
        self.q[E].append(lambda h, fn=fn, sem=sem: fn(h).then_inc(sem, 1))
        for b in reads:
            b.readers.append(tok)
        for b in writes:
            b.w = tok
            b.readers = []
        self.ninstr += 1

    def dma(self, fn, sembuf, reads=(), writes=(), E="sp"):
        self._deps(E, reads, writes)
        sem = self._dsem(sembuf)
        sembuf.dtotal += 16
        tok = ("d", sembuf)
        self.q[E].append(lambda h, fn=fn, sem=sem: fn(h).then_inc(sem, 16))
        for b in reads:
            b.readers.append(tok)
        for b in writes:
            b.w = tok
            b.readers = []
        self.ninstr += 1

    def barrier(self):
        sp = "sp"
        for e in self.ENGS:
            if e != sp and self.ecnt[e] > 0:
                self._wait(sp, ("e", e, self.ecnt[e]))
        for b in self.dsems:
            self._wait(sp, ("d", b))
        self.nphase += 1
        n = self.nphase
        ps = self.phase_sem
        self.q[sp].append(lambda h: h.sem_inc(ps, 1))
        for e in self.ENGS:
            if e != sp:
                self.q[e].append(lambda h, n=n: h.wait_ge(ps, n))
        for e in self.ENGS:
            for e2 in self.ENGS:
                self.waited[e][("e", e2)] = self.ecnt[e2]
            for b in self.dsems:
                self.waited[e][("d", id(b))] = b.dtotal

    def finish(self):
        sp = "sp"
        for e in self.ENGS:
            if e != sp and self.ecnt[e] > 0:
                self._wait(sp, ("e", e, self.ecnt[e]))
        for b in self.dsems:
            self._wait(sp, ("d", b))

    def emit(self):
        with self.nc.Block() as block:
            @block.tensor
            def _(h):
                for f in self.q["pe"]:
                    f(h)

            @block.scalar
            def _(h):
                for f in self.q["act"]:
                    f(h)

            @block.vector
            def _(h):
                for f in self.q["dve"]:
                    f(h)

            @block.gpsimd
            def _(h):
                for f in self.q["pool"]:
                    f(h)

            @block.sync
            def _(h):
                for f in self.q["sp"]:
                    f(h)


class Arena:
    def __init__(self, t, words):
        self.t = t
        self.words = words
        self.top = 0
        self.marks = []

    def mark(self):
        self.marks.append(self.top)

    def release(self):
        self.top = self.marks.pop()

    def alloc(self, shape, dtype):
        n = int(np.prod(shape))
        w = n if dtype in (F32, I32) else (n + 1) // 2
        w = (w + 1) // 2 * 2
        assert self.top + w <= self.words, ("arena overflow", self.top, w, self.words)
        v = self.t[:, self.top:self.top + w]
        self.top += w
        if dtype not in (F32,):
            v = v.bitcast(dtype)
        v = v[:, 0:n]
        if len(shape) == 2:
            return v.rearrange("p (a b) -> p a b", b=shape[1])
        if len(shape) == 3:
            return v.rearrange("p (a b c) -> p a b c", b=shape[1], c=shape[2])
        return v


def build(phases="ABCDS"):
    nc = bass.Bass("TRN2", target_bir_lowering=False)
    dt = nc.dram_tensor

    def din(name, shape, dtype=F32):
        return dt(name, list(shape), dtype, kind="ExternalInput").ap()

    def dout(name, shape):
        return dt(name, list(shape), F32, kind="ExternalOutput").ap()

    xp = din("xp", [SEQ, D])
    xsm = din("xs", [NS, D])
    w_in = din("w_in", [D, INC])
    if "C" in phases:
        w_a = din("w_a", [512, D])
        w_b = din("w_b", [512, D])
        w_out = din("w_out", [D, D])
        gate_b = din("gate_b", [1, 2 * D])
    if "D" in phases:
        w_gu = din("w_gu", [D, 2 * DFF])
        w_dn = din("w_dn", [DFF, D])
    cst = din("cst", [128, C_N])
    g1T = din("g1T", [128, 8])
    g2T = din("g2T", [128, 8])
    V_GQ, V_GK, V_FQ, V_FK, V_SG, V_FB, V_B31, V_B0, V_LAM, V_N = 0, 64, 128, 192, 256, 384, 392, 396, 400, 656
    vec = din("vec", [1, V_N])
    rel_bias = din("rel_bias", [32, 4])
    has_S = "S" in phases
    if has_S:
        c_all = din("c_all", [NPHYS * 128, 2056])
        ptab = din("ptab", [1, NS * NPG], I32)

    y_p = dout("y_p", [SEQ, D])
    y_s = dout("y_s", [NS, D])
    o_dk = dout("o_dk", [SEQ, 512])
    o_dv = dout("o_dv", [SEQ, 512])
    o_fk = dout("o_fk", [SEQ, 512])
    o_fv = dout("o_fv", [SEQ, 512])
    o_lf = dout("o_lf", [SEQ, 8])
    s_dk = dout("s_dk", [NS, 512])
    s_dv = dout("s_dv", [NS, 512])
    s_fk = dout("s_fk", [NS, 512])
    s_fv = dout("s_fv", [NS, 512])
    s_lf = dout("s_lf", [NS, 8])
    gtab_h = dt("gtab", [4, 384], F32, kind=("ExternalOutput" if DBG_OUT else "Internal"))
    gtab = gtab_h.ap()
    x1s = dt("x1s", [SEQ + 128, D], F32, kind=("ExternalOutput" if DBG_OUT else "Internal")).ap()
    if DBG_OUT:
        dbg_odt = dt("dbg_odt", [128, 8 * SEQ], BF16, kind="ExternalOutput").ap()

    stack = ExitStack()
    with stack:
        P = Prog(nc, stack)
        sb = lambda name, shape, dtype=F32: stack.enter_context(nc.sbuf_tensor(name, list(shape), dtype))
        cstt = sb("cstt", [128, C_MAIN])
        vect = sb("vect", [128, V_N])
        g1t = sb("g1t", [128, 8])
        g2t = sb("g2t", [128, 8])
        identb = sb("identb", [128, 128], BF16)
        mtrib = sb("mtrib", [128, 128], BF16)
        jb = sb("jb", [128, 128], BF16)
        gq8 = sb("gq8", [128, 64])
        fq8 = sb("fq8", [128, 64])
        sgs = sb("sgs", [128, 128])
        lamt = sb("lamt", [128, 8])
        TTb = sb("TTb", [128, 4, 256], BF16)
        LF = sb("LF", [128, NT + 1, 8])
        CC = sb("CC", [128, NT, 8])
        CR = sb("CR", [128, NT, 8])
        small = sb("small", [128, 64])
        onesb = sb("onesb", [128, 128], BF16)
        sel64b = sb("sel64b", [128, 128], BF16)
        odt16 = sb("odt16", [128, 8, 128], BF16)
        q16 = sb("q16", [128, 1024], BF16)
        k16 = sb("k16", [128, 1024])
        v16 = sb("v16", [128, 1024], BF16)
        b_16 = Buf("t16")
        b_odt16 = Buf("odt16")
        b_x1s = [Buf("x1s%d" % i) for i in range(NT + 1)]

        def MM(out, lhsT, rhs, start, stop):
            return lambda h: h.matmul(out, lhsT, rhs, start=start, stop=stop)

        def TR(out, in_):
            return lambda h: h.transpose(out=out, in_=in_, identity=identb[:])

        def ACTV(out, in_, func, **kw):
            return lambda h: h.activation(out=out, in_=in_, func=func, **kw)

        def TT(out, in0, in1, op):
            return lambda h: h.tensor_tensor(out=out, in0=in0, in1=in1, op=op)

        def TS(out, in0, s1, op0):
            return lambda h: h.tensor_scalar(out=out, in0=in0, scalar1=s1, scalar2=None, op0=op0)

        def CP(out, in_):
            return lambda h: h.tensor_copy(out=out, in_=in_)

        def RCP(out, in_):
            return lambda h: h.reciprocal(out=out, in_=in_)

        def RED(out, in_):
            return lambda h: h.tensor_reduce(out=out, in_=in_, axis=AX.X, op=ALU.add)

        def DMA(out, in_):
            return lambda h: h.dma_start(out=out, in_=in_)

        def MSET(ap, v):
            return lambda h: h.memset(ap, v)
        ARW = 47500
        art = sb("arena", [128, ARW])
        AR = Arena(art, ARW)
        ps = [stack.enter_context(nc.psum_tensor("ps%d" % i, [128, 512], F32)) for i in range(8)]
        psb = [P_buf for P_buf in [Buf("ps%d" % i) for i in range(8)]]

        ident_f = cstt[:, C_ID:C_ID + 128]
        b_cst = Buf("cst")
        b_vec = Buf("vec")
        b_misc = Buf("misc")

        P.dma(lambda h: h.dma_start(out=cstt[:], in_=cst[:, 0:C_MAIN]), b_cst, writes=[b_cst])
        AR.mark()
        caux = AR.alloc([384], F32)
        b_caux = Buf("caux")
        P.dma(lambda h: h.dma_start(out=caux, in_=cst[:, C_OH:C_OH + 384]), b_caux, writes=[b_caux])
        P.dma(lambda h: h.dma_start(out=vect[:], in_=vec.partition_broadcast(128)), b_vec, writes=[b_vec])
        P.dma(lambda h: h.dma_start(out=g1t[:], in_=g1T), b_vec, writes=[b_vec])
        P.dma(lambda h: h.dma_start(out=g2t[:], in_=g2T), b_vec, writes=[b_vec])
        P.op("dve", lambda h: h.tensor_copy(out=identb[:], in_=cstt[:, C_ID:C_ID + 128]), reads=[b_cst], writes=[b_misc])
        P.op("dve", lambda h: h.tensor_copy(out=mtrib[:], in_=cstt[:, C_MTRI:C_MTRI + 128]), reads=[b_cst], writes=[b_misc])
        P.op("dve", lambda h: h.tensor_copy(out=jb[:], in_=cstt[:, C_J:C_J + 128]), reads=[b_cst], writes=[b_misc])
        P.op("dve", CP(onesb[:], cstt[:, C_ONES:C_ONES + 128]), reads=[b_cst], writes=[b_misc])
        P.op("dve", CP(sel64b[:], cstt[:, C_SEL64:C_SEL64 + 128]), reads=[b_cst], writes=[b_misc])
        P.op("pool", MSET(odt16[:], 0.0), writes=[b_odt16])
        P.op("dve", lambda h: h.tensor_scalar(out=gq8[:], in0=vect[:, V_GQ:V_GQ + 64], scalar1=0.125, scalar2=None, op0=ALU.mult), reads=[b_vec], writes=[b_misc])
        P.op("dve", lambda h: h.tensor_scalar(out=fq8[:], in0=vect[:, V_FQ:V_FQ + 64], scalar1=0.125, scalar2=None, op0=ALU.mult), reads=[b_vec], writes=[b_misc])
        P.op("dve", lambda h: h.tensor_scalar(out=sgs[:], in0=vect[:, V_SG:V_SG + 128], scalar1=float(1.0 - LAMBDA_INIT), scalar2=None, op0=ALU.mult), reads=[b_vec], writes=[b_misc])
        lm = vect[:, V_LAM:V_LAM + 256]
        P.op("dve", lambda h: h.tensor_tensor(out=small[:, 0:64], in0=lm[:, 0:64], in1=lm[:, 64:128], op=ALU.mult), reads=[b_vec], writes=[b_misc])
        P.op("dve", lambda h: h.tensor_reduce(out=lamt[:, 0:1], in_=small[:, 0:64], axis=AX.X, op=ALU.add), reads=[b_misc], writes=[b_misc])
        P.op("dve", lambda h: h.tensor_tensor(out=small[:, 0:64], in0=lm[:, 128:192], in1=lm[:, 192:256], op=ALU.mult), reads=[b_vec, b_misc], writes=[b_misc])
        P.op("dve", lambda h: h.tensor_reduce(out=lamt[:, 1:2], in_=small[:, 0:64], axis=AX.X, op=ALU.add), reads=[b_misc], writes=[b_misc])
        P.op("act", lambda h: h.activation(out=lamt[:, 2:4], in_=lamt[:, 0:2], func=AF.Exp), reads=[b_misc], writes=[b_misc])
        P.op("dve", lambda h: h.tensor_tensor(out=lamt[:, 4:5], in0=lamt[:, 3:4], in1=lamt[:, 2:3], op=ALU.subtract), reads=[b_misc], writes=[b_misc])
        P.op("dve", lambda h: h.tensor_scalar(out=lamt[:, 4:5], in0=lamt[:, 4:5], scalar1=float(-LAMBDA_INIT), scalar2=None, op0=ALU.add), reads=[b_misc], writes=[b_misc])
        neglam = lamt[:, 4:5]

        rbx = AR.alloc([4], F32)
        b_rb = Buf("rb")
        P.op("dve", MSET(rbx[32:64, :], 1.0), writes=[b_rb])
        P.dma(lambda h: h.dma_start(out=rbx[0:32, :], in_=rel_bias), b_rb, writes=[b_rb])
        P.op("pe", lambda h: h.matmul(ps[0][0:4, 0:384], rbx[0:33, :], caux[0:33, 0:384], start=True, stop=True), reads=[b_rb, b_caux], writes=[psb[0]])
        gt_s = AR.alloc([384], F32)
        TTf = AR.alloc([4, 256], F32)
        b_gt = Buf("gt")
        b_gtd = Buf("gtd")
        b_tt = Buf("tt")
        P.op("dve", lambda h: h.tensor_copy(out=gt_s[0:4, :], in_=ps[0][0:4, 0:384]), reads=[psb[0]], writes=[b_gt])
        P.dma(lambda h: h.dma_start(out=gtab, in_=gt_s[0:4, :]), b_gt, reads=[b_gt], writes=[b_gtd])
        tt_src = bass.AP(tensor=gtab_h, offset=0, ap=[[1, 128], [384, 4], [1, 256]])
        P.dma(lambda h: h.dma_start(out=TTf, in_=tt_src), b_tt, reads=[b_gtd], writes=[b_tt])
        P.op("dve", lambda h: h.tensor_copy(out=TTb[:], in_=TTf), reads=[b_tt], writes=[b_misc])
        P.barrier()
        AR.release()

        AR.mark()
        QT = AR.alloc([8, SEQ], BF16)
        KT = AR.alloc([8, SEQ], BF16)
        Vd = AR.alloc([NT, 4 * 130], BF16)
        Vf = AR.alloc([NT, 8 * 66], BF16)
        b_QT = [[Buf("QT%d_%d" % (p_, i)) for i in range(NT)] for p_ in range(8)]
        b_KT = [[Buf("KT%d_%d" % (p_, i)) for i in range(NT)] for p_ in range(8)]
        b_V = [Buf("V%d" % i) for i in range(NT)]
        b_LF = [Buf("LF%d" % i) for i in range(NT + 1)]
        Vd4 = Vd.rearrange("p t (h e) -> p t h e", e=130)
        Vf4 = Vf.rearrange("p t (h e) -> p t h e", e=66)
        for i in range(NT):
            P.op("pool", lambda h, i=i: h.memset(Vd4[:, i, :, 128:129], 1.0), writes=[b_V[i]])
            P.op("pool", lambda h, i=i: h.memset(Vf4[:, i, :, 64:65], 1.0), writes=[b_V[i]])

        if "A" in phases:
            AR.mark()
            Wq = AR.alloc([8, QKVW], BF16)
            wst = [AR.alloc([QKVW // 2], F32) for _ in range(2)]
            b_wst = [Buf("wst0"), Buf("wst1")]
            b_W = Buf("Wq")
            w_in_v = w_in.rearrange("(kc p) c -> p kc c", p=128)
            HW = QKVW // 2
            for kc in range(8):
                for hf in range(2):
                    s = hf
                    P.dma(lambda h, kc=kc, s=s, hf=hf: h.dma_start(out=wst[s], in_=w_in_v[:, kc, hf * HW:(hf + 1) * HW]), b_wst[s], writes=[b_wst[s]])
                    eng = ("dve", "pool")[hf]
                    P.op(eng, lambda h, kc=kc, s=s, hf=hf: h.tensor_scalar(out=Wq[:, kc, hf * HW:(hf + 1) * HW], in0=wst[s], scalar1=g1t[:, kc:kc + 1], scalar2=None, op0=ALU.mult),
                         reads=[b_wst[s], b_vec], writes=[b_W])
            xs_t = [AR.alloc([D], F32) for _ in range(2)]
            xn_t = [AR.alloc([D], BF16) for _ in range(2)]
            hT_t = [AR.alloc([8, 128], BF16) for _ in range(2)]
            st4 = [AR.alloc([4], F32) for _ in range(2)]
            sq_t = [AR.alloc([512], F32)] * 2
            kn_t = [AR.alloc([512], F32)] * 2
            kb_t = [AR.alloc([512], BF16) for _ in range(2)]
            og_t = [AR.alloc([512], F32) for _ in range(3)]
            st8 = [AR.alloc([32], F32) for _ in range(2)]
            lg_t = [AR.alloc([40], F32) for _ in range(2)]
            b_xs = [Buf("xs0"), Buf("xs1")]
            b_xn = [Buf("xn0"), Buf("xn1")]
            b_hT = [Buf("hT0"), Buf("hT1")]
            b_st4 = [Buf("st40"), Buf("st41")]
            b_sq = [Buf("sq0")] * 2
            b_kn = [Buf("kn0")] * 2
            b_kb = [Buf("kb0"), Buf("kb1")]
            b_og = [Buf("og%d" % i) for i in range(3)]
            b_st8 = [Buf("st80"), Buf("st81")]
            b_lg = [Buf("lg0"), Buf("lg1")]
            b_junk = Buf("junk")
            nq = [0]
            nog = [0]
            nkn = [0]

            def stage1(i):
                s = i % 2
                rows = 128 if i < NT else NS
                if i == NT:
                    P.op("pool", lambda h, s=s: h.memset(xs_t[s], 0.0), writes=[b_xs[s]])
                    P.dma(lambda h, s=s: h.dma_start(out=xs_t[s][0:NS, :], in_=xsm), b_xs[s], writes=[b_xs[s]])
                else:
                    P.dma(lambda h, s=s, i=i: h.dma_start(out=xs_t[s], in_=xp[i * 128:(i + 1) * 128, :]), b_xs[s], writes=[b_xs[s]])
                P.op("act", lambda h, s=s: h.activation(out=xn_t[s], in_=xs_t[s], func=AF.Square, accum_out=st4[s][:, 0:1]),
                     reads=[b_xs[s]], writes=[b_xn[s], b_st4[s]])
                P.op("act", lambda h, s=s: h.activation(out=st4[s][:, 1:2], in_=st4[s][:, 0:1], func=AF.Sqrt, scale=1.0 / D, bias=float(EPS)),
                     reads=[b_st4[s]], writes=[b_st4[s]])
                P.op("dve", lambda h, s=s: h.reciprocal(out=st4[s][:, 2:3], in_=st4[s][:, 1:2]), reads=[b_st4[s]], writes=[b_st4[s]])
                P.op("dve", lambda h, s=s: h.tensor_scalar(out=xn_t[s], in0=xs_t[s], scalar1=st4[s][:, 2:3], scalar2=None, op0=ALU.mult),
                     reads=[b_xs[s], b_st4[s]], writes=[b_xn[s]])
                if DBG_STAGE < 1:
                    return
                pT = ps[7][:].bitcast(BF16)
                for kc in range(8):
                    P.op("pe", lambda h, s=s, kc=kc: h.transpose(out=pT[:, kc * 128:(kc + 1) * 128], in_=xn_t[s][:, kc * 128:(kc + 1) * 128], identity=identb[:]),
                         reads=[b_xn[s], b_misc], writes=[psb[7]])
                P.op("act", lambda h, s=s: h.activation(out=hT_t[s].rearrange("p a b -> p (a b)"), in_=pT, func=AF.Copy),
                     reads=[psb[7]], writes=[b_hT[s]])

            def stage2(i):
                s = i % 2
                rows = 128 if i < NT else NS

                def proj(bank, c0, n, s=s):
                    for kc in range(8):
                        P.op("pe", lambda h, kc=kc: h.matmul(ps[bank][:, 0:n], hT_t[s][:, kc, :], Wq[:, kc, c0:c0 + n], start=(kc == 0), stop=(kc == 7)),
                             reads=[b_hT[s], b_W], writes=[psb[bank]])

                def qknorm(bank, gains, is_k, dst_cols, out_dram, i=i, s=s, rows=rows):
                    u = nq[0] % 2
                    nq[0] += 1
                    P.op("act", lambda h: h.activation(out=sq_t[u], in_=ps[bank][:], func=AF.Square), reads=[psb[bank]], writes=[b_sq[u]])
                    P.op("dve", lambda h: h.tensor_reduce(out=st8[u][:, 0:8], in_=sq_t[u].rearrange("p (a b) -> p a b", b=64), axis=AX.X, op=ALU.add),
                         reads=[b_sq[u]], writes=[b_st8[u]])
                    P.op("act", lambda h: h.activation(out=st8[u][:, 8:16], in_=st8[u][:, 0:8], func=AF.Sqrt, scale=1.0 / HD, bias=float(EPS)),
                         reads=[b_st8[u]], writes=[b_st8[u]])
                    P.op("dve", lambda h: h.reciprocal(out=st8[u][:, 16:24], in_=st8[u][:, 8:16]), reads=[b_st8[u]], writes=[b_st8[u]])
                    kk = nkn[0] % 2
                    nkn[0] += 1
                    P.op("dve", lambda h: h.tensor_tensor(out=kn_t[kk].rearrange("p (a b) -> p a b", b=64), in0=ps[bank][:].rearrange("p (a b) -> p a b", b=64),
                                                          in1=st8[u][:, 16:24].unsqueeze(2).to_broadcast([128, 8, 64]), op=ALU.mult),
                         reads=[psb[bank], b_st8[u]], writes=[b_kn[kk]])
                    gb = gains.unsqueeze(1).to_broadcast([128, 8, 64])
                    if is_k:
                        o = nog[0] % 3
                        nog[0] += 1
                        dstf = og_t[o] if i < NT else k16[:, dst_cols:dst_cols + 512]
                        bdst = b_og[o] if i < NT else b_16
                        P.op("dve", lambda h: h.tensor_tensor(out=dstf.rearrange("p (a b) -> p a b", b=64), in0=kn_t[kk].rearrange("p (a b) -> p a b", b=64), in1=gb, op=ALU.mult),
                             reads=[b_kn[kk], b_vec, b_misc], writes=[bdst])
                        P.dma(lambda h: h.dma_start(out=out_dram, in_=dstf[0:rows, :]), bdst, reads=[bdst])
                        if i < NT:
                            P.op("act", lambda h: h.activation(out=kb_t[u], in_=dstf, func=AF.Copy), reads=[bdst], writes=[b_kb[u]])
                    else:
                        dstb = kb_t[u] if i < NT else q16[:, dst_cols:dst_cols + 512]
                        bdst = b_kb[u] if i < NT else b_16
                        P.op("dve", lambda h: h.tensor_tensor(out=dstb.rearrange("p (a b) -> p a b", b=64), in0=kn_t[kk].rearrange("p (a b) -> p a b", b=64), in1=gb, op=ALU.mult),
                             reads=[b_kn[kk], b_vec, b_misc], writes=[bdst])
                    if i < NT:
                        pT2 = ps[6][:].bitcast(BF16)
                        for j in range(4):
                            P.op("pe", lambda h, j=j: h.transpose(out=pT2[:, j * 128:(j + 1) * 128], in_=kb_t[u][:, j * 128:(j + 1) * 128], identity=identb[:]),
                                 reads=[b_kb[u], b_misc], writes=[psb[6]])
                        T, bT = (KT, b_KT) if is_k else (QT, b_QT)
                        pair0 = dst_cols // 128
                        P.op("dve", lambda h: h.tensor_copy(out=T[:, pair0:pair0 + 4, i * 128:(i + 1) * 128], in_=pT2[:, 0:512].rearrange("p (a b) -> p a b", b=128)),
                             reads=[psb[6]], writes=[bT[pp][i] for pp in range(pair0, pair0 + 4)])

                def vproj(bank, out_dram, is_d, i=i, rows=rows):
                    o = nog[0] % 3
                    nog[0] += 1
                    P.op("act", lambda h: h.activation(out=og_t[o], in_=ps[bank][:], func=AF.Copy), reads=[psb[bank]], writes=[b_og[o]])
                    P.dma(lambda h: h.dma_start(out=out_dram, in_=og_t[o][0:rows, :]), b_og[o], reads=[b_og[o]])
                    if i < NT:
                        if is_d:
                            P.op("dve", lambda h: h.tensor_copy(out=Vd4[:, i, :, 0:128], in_=og_t[o].rearrange("p (a b) -> p a b", b=128)), reads=[b_og[o]], writes=[b_V[i]])
                        else:
                            P.op("dve", lambda h: h.tensor_copy(out=Vf4[:, i, :, 0:64], in_=og_t[o].rearrange("p (a b) -> p a b", b=64)), reads=[b_og[o]], writes=[b_V[i]])
                    else:
                        c0 = 0 if is_d else 512
                        P.op("dve", lambda h: h.tensor_copy(out=v16[:, c0:c0 + 512], in_=og_t[o]), reads=[b_og[o]], writes=[b_16])

                r0, r1 = i * 128, i * 128 + rows
                pr = (lambda t, ts: t[r0:r1, :] if i < NT else ts[0:rows, :])
                if DBG_STAGE < 2:
                    return
                proj(0, O_DQ, 512)
                if DBG_STAGE < 3:
                    return
                qknorm(0, gq8[:], False, 0, None)
                if DBG_STAGE < 4:
                    return
                proj(1, O_DK, 512)
                qknorm(1, vect[:, V_GK:V_GK + 64], True, 0, pr(o_dk, s_dk))
                if DBG_STAGE < 5:
                    return
                proj(2, O_DV, 512)
                vproj(2, pr(o_dv, s_dv), True)
                if DBG_STAGE < 5.2:
                    return
                proj(3, O_FQ, 512)
                qknorm(3, fq8[:], False, 512, None)
                if DBG_STAGE < 5.4:
                    return
                proj(4, O_FK, 512)
                qknorm(4, vect[:, V_FK:V_FK + 64], True, 512, pr(o_fk, s_fk))
                if DBG_STAGE < 5.6:
                    return
                proj(5, O_FV, 512)
                vproj(5, pr(o_fv, s_fv), False)
                if DBG_STAGE < 6:
                    return
                proj(7, O_LF, 8)
                u = i % 2
                lg = lg_t[u]
                P.op("dve", lambda h, lg=lg: h.tensor_tensor(out=lg[:, 0:8], in0=ps[7][:, 0:8], in1=vect[:, V_FB:V_FB + 8], op=ALU.add), reads=[psb[7], b_vec], writes=[b_lg[u]])
                P.op("dve", lambda h, lg=lg: h.tensor_scalar(out=lg[:, 8:16], in0=lg[:, 0:8], scalar1=-1.0, scalar2=None, op0=ALU.mult), reads=[b_lg[u]], writes=[b_lg[u]])
                P.op("dve", lambda h, lg=lg: h.tensor_tensor(out=lg[:, 8:16], in0=lg[:, 0:8], in1=lg[:, 8:16], op=ALU.min), reads=[b_lg[u]], writes=[b_lg[u]])
                P.op("act", lambda h, lg=lg: h.activation(out=lg[:, 16:24], in_=lg[:, 8:16], func=AF.Exp), reads=[b_lg[u]], writes=[b_lg[u]])
                P.op("act", lambda h, lg=lg: h.activation(out=lg[:, 24:32], in_=lg[:, 16:24], func=AF.Ln, bias=1.0), reads=[b_lg[u]], writes=[b_lg[u]])
                P.op("dve", lambda h, lg=lg: h.tensor_scalar(out=lg[:, 32:40], in0=lg[:, 0:8], scalar1=0.0, scalar2=None, op0=ALU.min), reads=[b_lg[u]], writes=[b_lg[u]])
                P.op("dve", lambda h, lg=lg, i=i: h.tensor_tensor(out=LF[:, i, :], in0=lg[:, 32:40], in1=lg[:, 24:32], op=ALU.subtract), reads=[b_lg[u]], writes=[b_LF[i]])
                lf_dst = pr(o_lf, s_lf)
                P.dma(lambda h, i=i, rows=rows, lf_dst=lf_dst: h.dma_start(out=lf_dst, in_=LF[0:rows, i, :]), b_LF[i], reads=[b_LF[i]])
            stage1(DBG_TILES[0])
            for n_, i_ in enumerate(DBG_TILES):
                if n_ + 1 < len(DBG_TILES):
                    stage1(DBG_TILES[n_ + 1])
                stage2(i_)
            P.barrier()
            AR.release()

        qt_top = None
        if "B" in phases:
            AR.mark()
            PT = [AR.alloc([16, 512], BF16) for _ in range(2)]
            b_PT = [Buf("PT0"), Buf("PT1")]
            FB = AR.alloc([NT, NT * 8], F32)
            WC = AR.alloc([NT * 8], F32)
            TBt = AR.alloc([NT * 8], F32)
            CAR = AR.alloc([8], F32)
            o1s = AR.alloc([4, 128], F32)
            t2s = AR.alloc([4, 128], F32)
            ods = AR.alloc([4, 128], F32)
            sqs = AR.alloc([4, 128], F32)
            odb = [AR.alloc([4, 128], BF16) for _ in range(2)]
            ofs = [AR.alloc([4, 128], BF16) for _ in range(2)]
            rcs = [AR.alloc([16], F32) for _ in range(2)]
            b_c = Buf("cum")
            b_fb = Buf("fb")
            b_o1 = Buf("o1s")
            b_t2 = Buf("t2s")
            b_ods = Buf("ods")
            b_sqs = Buf("sqs")
            b_odb = [Buf("odb0"), Buf("odb1")]
            b_ofs = [Buf("ofs0"), Buf("ofs1")]
            b_rc = [Buf("rc0"), Buf("rc1")]
            LFv = LF[:, 0:NT, :].rearrange("p a b -> p (a b)")
            CCv = CC[:].rearrange("p a b -> p (a b)")
            CRv = CR[:].rearrange("p a b -> p (a b)")
            P.op("pe", MM(ps[0][:, 0:128], cstt[:, C_LTRI:C_LTRI + 128], LFv, True, True), reads=[b_cst] + b_LF[0:NT], writes=[psb[0]])
            P.op("pe", MM(ps[1][:, 0:128], cstt[:, C_ONES:C_ONES + 128], LFv, True, True), reads=[b_cst] + b_LF[0:NT], writes=[psb[1]])
            P.op("dve", CP(WC, ps[0][:, 0:128]), reads=[psb[0]], writes=[b_c])
            P.op("dve", CP(TBt, ps[1][:, 0:128]), reads=[psb[1]], writes=[b_c])
            P.op("dve", CP(CC[:, 0, :], WC[:, 0:8]), reads=[b_c], writes=[b_c])
            P.op("dve", MSET(CAR, 0.0), writes=[b_c])
            for i in range(1, NT):
                P.op("dve", TT(CAR, CAR, TBt[:, (i - 1) * 8:i * 8], ALU.add), reads=[b_c], writes=[b_c])
                P.op("dve", TT(CC[:, i, :], WC[:, i * 8:(i + 1) * 8], CAR, ALU.add), reads=[b_c], writes=[b_c])
            P.op("pe", MM(ps[0][:, 0:128], cstt[:, C_SEL64:C_SEL64 + 128], CCv, True, True), reads=[b_cst, b_c], writes=[psb[0]])
            P.op("dve", CP(CRv, ps[0][:, 0:128]), reads=[psb[0]], writes=[b_c])
            for kt in range(NT):
                n = NT - kt
                P.op("dve", TT(FB[:, kt, kt * 8:NT * 8].rearrange("p (a b) -> p a b", b=8), CR[:, kt:NT, :],
                               CC[:, kt:kt + 1, :].to_broadcast([128, n, 8]), ALU.subtract), reads=[b_c], writes=[b_fb])

            SB_ = (0, 1, 2)
            PO = ((3, 4), (5, 6))
            pT7 = ps[7][:].bitcast(BF16)
            cnt = 0
            scnt = 0
            ofc = 0
            odc = 0
            for B in range(4):
                nk = 4 * B + 4
                for mp in range(16):
                    is_d = mp < 8
                    if is_d:
                        h_, m_ = mp // 2, mp % 2
                        pair, half, vh = h_, m_, h_
                    else:
                        hf = mp - 8
                        pair, half, vh = 4 + hf // 2, hf % 2, hf
                    slot = cnt % 2
                    pset = PO[cnt % 2]
                    cnt += 1
                    p0 = half * 64
                    for kt in range(nk):
                        j0 = max(0, kt - 4 * B)
                        c0 = j0 * 128
                        sbk = SB_[scnt % 3]
                        scnt += 1
                        extra = (is_d and kt >= 4 * B - 1) or ((not is_d) and kt >= 4 * B)
                        P.op("pe", MM(ps[sbk][:, c0:512], KT[p0:p0 + 64, pair, kt * 128:(kt + 1) * 128],
                                      QT[p0:p0 + 64, pair, 512 * B + c0:512 * B + 512], True, not extra),
                             reads=[b_KT[pair][kt]] + [b_QT[pair][4 * B + j] for j in range(j0, 4)], writes=[psb[sbk]])
                        if is_d and extra:
                            if kt >= 4 * B:
                                a0 = c0
                                wd = min(256, 512 - a0)
                                rhs = TTb[:, h_, 0:wd]
                            else:
                                a0, wd = 0, 128
                                rhs = TTb[:, h_, 128:256]
                            P.op("pe", MM(ps[sbk][:, a0:a0 + wd], jb[:], rhs, False, True), reads=[b_misc], writes=[psb[sbk]])
                        if (not is_d) and extra:
                            P.op("pe", MM(ps[sbk][:, c0:c0 + 128], identb[:], mtrib[:], False, True), reads=[b_misc], writes=[psb[sbk]])
                        if is_d:
                            P.op("act", ACTV(PT[slot][:, kt, c0:512], ps[sbk][:, c0:512], AF.Exp, bias=vect[:, V_B31 + h_:V_B31 + h_ + 1]),
                                 reads=[psb[sbk], b_vec], writes=[b_PT[slot]])
                        else:
                            for j in range(j0, 4):
                                qt = 4 * B + j
                                P.op("act", ACTV(PT[slot][:, kt, j * 128:(j + 1) * 128], ps[sbk][:, j * 128:(j + 1) * 128], AF.Exp,
                                                 bias=FB[:, kt, qt * 8 + hf:qt * 8 + hf + 1]),
                                     reads=[psb[sbk], b_fb], writes=[b_PT[slot]])
                    Wc = 129 if is_d else 65
                    V4 = Vd4 if is_d else Vf4
                    for j in range(4):
                        qt = 4 * B + j
                        if is_d:
                            bank, off = pset[j // 2], (j % 2) * 130
                        else:
                            bank, off = pset[0], j * 66
                        for kt in range(qt + 1):
                            P.op("pe", MM(ps[bank][:, off:off + Wc], PT[slot][:, kt, j * 128:(j + 1) * 128], V4[:, kt, vh, 0:Wc], kt == 0, kt == qt),
                                 reads=[b_PT[slot], b_V[kt]], writes=[psb[bank]])
                    if not is_d:
                        bank = pset[0]
                        pv = ps[bank][:, 0:264].rearrange("p (j e) -> p j e", e=66)
                        u = ofc % 2
                        r = rcs[cnt % 2]
                        br = b_rc[cnt % 2]
                        P.op("dve", RCP(r[:, 0:4], pv[:, :, 64]), reads=[psb[bank]], writes=[br])
                        P.op("dve", TT(ofs[u][:, :, half * 64:(half + 1) * 64], pv[:, :, 0:64], r[:, 0:4].unsqueeze(2).to_broadcast([128, 4, 64]), ALU.mult),
                             reads=[psb[bank], br], writes=[b_ofs[u]])
                        if half == 1:
                            for j in range(4):
                                P.op("pe", TR(pT7[:, j * 128:(j + 1) * 128], ofs[u][:, j, :]), reads=[b_ofs[u], b_misc], writes=[psb[7]])
                            P.op("act", ACTV(QT[:, pair, 512 * B:512 * B + 512], pT7[:, 0:512], AF.Copy), reads=[psb[7]],
                                 writes=[b_QT[pair][4 * B + j] for j in range(4)])
                            ofc += 1
                    else:
                        r = rcs[cnt % 2]
                        br = b_rc[cnt % 2]
                        dst, bdst = (o1s, b_o1) if m_ == 0 else (t2s, b_t2)
                        for bi_, jj in ((0, 0), (1, 2)):
                            bank = pset[bi_]
                            pvx = ps[bank][:, 0:260].rearrange("p (j e) -> p j e", e=130)
                            P.op("dve", RCP(r[:, jj:jj + 2], pvx[:, :, 128]), reads=[psb[bank]], writes=[br])
                            P.op("dve", TT(dst[:, jj:jj + 2, :], pvx[:, :, 0:128], r[:, jj:jj + 2].unsqueeze(2).to_broadcast([128, 2, 128]), ALU.mult),
                                 reads=[psb[bank], br], writes=[bdst])
                        if m_ == 1:
                            fl = lambda a: a.rearrange("p a b -> p (a b)")
                            P.op("dve", lambda h: h.scalar_tensor_tensor(out=fl(ods), in0=fl(t2s), scalar=neglam, in1=fl(o1s), op0=ALU.mult, op1=ALU.add),
                                 reads=[b_t2, b_o1, b_misc], writes=[b_ods])
                            P.op("act", ACTV(fl(sqs), fl(ods), AF.Square), reads=[b_ods], writes=[b_sqs])
                            P.op("dve", RED(r[:, 4:8], sqs), reads=[b_sqs], writes=[br])
                            P.op("act", ACTV(r[:, 8:12], r[:, 4:8], AF.Sqrt, scale=1.0 / 128, bias=float(EPS)), reads=[br], writes=[br])
                            P.op("dve", RCP(r[:, 12:16], r[:, 8:12]), reads=[br], writes=[br])
                            P.op("dve", TT(ods, ods, r[:, 12:16].unsqueeze(2).to_broadcast([128, 4, 128]), ALU.mult), reads=[b_ods, br], writes=[b_ods])
                            u = odc % 2
                            P.op("dve", TT(odb[u], ods, sgs[:].unsqueeze(1).to_broadcast([128, 4, 128]), ALU.mult), reads=[b_ods, b_misc], writes=[b_odb[u]])
                            for j in range(4):
                                P.op("pe", TR(pT7[:, j * 128:(j + 1) * 128], odb[u][:, j, :]), reads=[b_odb[u], b_misc], writes=[psb[7]])
                            P.op("act", ACTV(QT[:, pair, 512 * B:512 * B + 512], pT7[:, 0:512], AF.Copy), reads=[psb[7]],
                                 writes=[b_QT[pair][4 * B + j] for j in range(4)])
                            odc += 1
            if DBG_OUT:
                b_dbg = Buf("dbg")
                P.dma(DMA(dbg_odt, QT.rearrange("p a b -> p (a b)")), b_dbg, reads=[b_QT[p_][i_] for p_ in range(8) for i_ in range(NT)])
            P.barrier()
            AR.release()
        ODT = QT
        AR.top = 8192

        if "S" in phases:
            AR.top = 8192
            saux = AR.alloc([1280], F32)
            pio = AR.alloc([2], F32)
            ptb = AR.alloc([NS * NPG], I32)
            idx = AR.alloc([NS * NPG], I32)
            selr = AR.alloc([NS, 128], BF16)
            selrf = [AR.alloc([128], F32) for _ in range(2)]
            BIASD = AR.alloc([NPG, 8], F32)
            pself = AR.alloc([32], F32)
            accd = AR.alloc([512], F32)
            accf = AR.alloc([512], F32)
            NSL = 3
            PGt = [AR.alloc([2056], F32) for _ in range(NSL)]
            b_PG = [Buf("PG%d" % i_) for i_ in range(NSL)]
            LFp = [AR.alloc([NPG * 8], F32) for _ in range(2)]
            Vdb = [AR.alloc([NPG, 512], BF16) for _ in range(2)]
            Vfb = [AR.alloc([NPG, 512], BF16) for _ in range(2)]
            prod = [AR.alloc([512], F32) for _ in range(2)]
            SC = [AR.alloc([2, NPG * 8], F32) for _ in range(2)]
            PR = [AR.alloc([2, NPG * 8], F32) for _ in range(2)]
            DEC = AR.alloc([NPG * 8], F32)
            SUF = AR.alloc([NPG * 8], F32)
            TBs = AR.alloc([NPG * 8], F32)
            sm = [AR.alloc([64], F32) for _ in range(2)]
            ADb = [AR.alloc([NPG + 1, 4], BF16) for _ in range(2)]
            AFb = [AR.alloc([NPG + 1, 8], BF16) for _ in range(2)]
            mdt = AR.alloc([512], F32)
            b_saux = Buf("saux")
            b_idx = Buf("idx")
            b_selr = Buf("selr")
            b_selrf = [Buf("selrf0"), Buf("selrf1")]
            b_biasd = Buf("biasd")
            b_pself = Buf("pself")
            b_acc = Buf("acc")
            b_Kd = [Buf("Kd0"), Buf("Kd1")]
            b_Kf = [Buf("Kf0"), Buf("Kf1")]
            b_Vd = [Buf("Vd0"), Buf("Vd1")]
            b_Vf = [Buf("Vf0"), Buf("Vf1")]
            b_LFp = [Buf("LFp0"), Buf("LFp1")]
            b_Vdb = [Buf("Vdb0"), Buf("Vdb1")]
            b_Vfb = [Buf("Vfb0"), Buf("Vfb1")]
            b_prod = [Buf("prod0"), Buf("prod1")]
            b_SC = [Buf("SC0"), Buf("SC1")]
            b_PR = [Buf("PR0"), Buf("PR1")]
            b_dec = Buf("dec")
            b_sm = [Buf("sm0"), Buf("sm1")]
            b_AD = [Buf("AD0"), Buf("AD1")]
            b_AF = [Buf("AF0"), Buf("AF1")]
            b_md = Buf("md")
            DM8 = saux[0:8, 0:512]
            DM4 = saux[0:4, 512:1024]
            ESv = saux[:, 1024:1280]
            P.dma(DMA(saux, cst[:, C_DM8:C_DM8 + 1280]), b_saux, writes=[b_saux])
            P.dma(DMA(pio[:, 0:2], cst[:, C_PIO:C_PIO + 2]), b_saux, writes=[b_saux])
            P.dma(DMA(ptb, ptab.partition_broadcast(128)), b_idx, writes=[b_idx])
            P.op("dve", lambda h: h.tensor_scalar(out=idx, in0=ptb, scalar1=128.0, scalar2=pio[:, 0:1], op0=ALU.mult, op1=ALU.add), reads=[b_idx, b_saux], writes=[b_idx])
            for s in range(NS):
                P.op("dve", CP(selr[:, s, :], cstt[:, C_ID + s:C_ID + s + 1].to_broadcast([128, 128])), reads=[b_cst], writes=[b_selr])
            P.op("pool", MSET(accd, 0.0), writes=[b_acc])
            P.op("pool", MSET(accf, 0.0), writes=[b_acc])
            B3 = BIASD.rearrange("p g (h m) -> p g h m", m=2)
            for m in range(2):
                P.op("dve", CP(B3[:, :, :, m], vect[:, V_B31:V_B31 + 4].unsqueeze(1).to_broadcast([128, NPG, 4])), reads=[b_vec], writes=[b_biasd])
            P.op("pe", MM(ps[4][:, 0:4], jb[:], TTb[:, :, 128], True, True), reads=[b_misc], writes=[psb[4]])
            for m in range(2):
                P.op("dve", TT(B3[:, NPG - 1, :, m], ps[4][:, 0:4], vect[:, V_B31:V_B31 + 4], ALU.add), reads=[psb[4], b_vec, b_biasd], writes=[b_biasd])
            for c0, dstc in ((0, 0), (512, 8)):
                P.op("dve", TT(prod[0], k16[:, c0:c0 + 512], q16[:, c0:c0 + 512], ALU.mult), reads=[b_16], writes=[b_prod[0]])
                P.op("dve", RED(pself[:, 16 + dstc:16 + dstc + 8], prod[0].rearrange("p (a b) -> p a b", b=64)), reads=[b_prod[0]], writes=[b_pself])
            P.op("dve", TT(pself[:, 16:24].rearrange("p (h m) -> p h m", m=2), pself[:, 16:24].rearrange("p (h m) -> p h m", m=2),
                           vect[:, V_B0:V_B0 + 4].unsqueeze(2).to_broadcast([128, 4, 2]), ALU.add), reads=[b_pself, b_vec], writes=[b_pself])
            P.op("act", ACTV(pself[:, 0:16], pself[:, 16:32], AF.Exp), reads=[b_pself], writes=[b_pself])

            gcn = 0
            for s in range(NS):
                u = s % 2
                qd_bank, qf_bank = (0, 1) if u == 0 else (2, 3)
                P.op("pe", MM(ps[qd_bank][:, :], selr[:, s, :], q16[:, 0:512], True, True), reads=[b_selr, b_16], writes=[psb[qd_bank]])
                P.op("pe", MM(ps[qf_bank][:, :], selr[:, s, :], q16[:, 512:1024], True, True), reads=[b_selr, b_16], writes=[psb[qf_bank]])
                P.op("dve", CP(selrf[u], cstt[:, C_ID + s:C_ID + s + 1].to_broadcast([128, 128])), reads=[b_cst], writes=[b_selrf[u]])
                msk = cstt[:, C_ID + s:C_ID + s + 1]
                for pg in range(NPG):
                    j = s * NPG + pg
                    g = gcn % NSL
                    gcn += 1
                    off = bass.IndirectOffsetOnAxis(ap=idx[:, j:j + 1], axis=0)
                    P.dma(lambda h, dst=PGt[g], off=off: h.indirect_dma_start(out=dst, out_offset=None, in_=c_all, in_offset=off),
                          b_PG[g], reads=[b_idx], writes=[b_PG[g]], E="pool")
                    for col, qb_ in ((0, qd_bank), (1, qf_bank)):
                        P.op("dve", TT(prod[col], PGt[g][:, col * 512:(col + 1) * 512], ps[qb_][:, :], ALU.mult), reads=[b_PG[g], psb[qb_]], writes=[b_prod[col]])
                        P.op("dve", RED(SC[u][:, col, pg * 8:(pg + 1) * 8], prod[col].rearrange("p (a b) -> p a b", b=64)), reads=[b_prod[col]], writes=[b_SC[u]])
                    P.op("act", ACTV(Vdb[u][:, pg, :], PGt[g][:, 1024:1536], AF.Copy), reads=[b_PG[g]], writes=[b_Vdb[u]])
                    P.op("act", ACTV(Vfb[u][:, pg, :], PGt[g][:, 1536:2048], AF.Copy), reads=[b_PG[g]], writes=[b_Vfb[u]])
                    P.op("act", ACTV(LFp[u][:, pg * 8:(pg + 1) * 8], PGt[g][:, 2048:2056], AF.Copy), reads=[b_PG[g]], writes=[b_LFp[u]])
                P.op("pe", MM(ps[4][:, 0:128], cstt[:, C_ONES:C_ONES + 128], LFp[u], True, True), reads=[b_cst, b_LFp[u]], writes=[psb[4]])
                P.op("dve", CP(TBs, ps[4][:, 0:128]), reads=[psb[4]], writes=[b_dec])
                P.op("pe", MM(ps[4][:, 0:8], selrf[u], LF[:, NT, :], True, True), reads=[b_selrf[u], b_LF[NT]], writes=[psb[4]])
                P.op("dve", CP(SUF[:, (NPG - 1) * 8:NPG * 8], ps[4][:, 0:8]), reads=[psb[4]], writes=[b_dec])
                for pg in range(NPG - 2, -1, -1):
                    P.op("dve", TT(SUF[:, pg * 8:(pg + 1) * 8], SUF[:, (pg + 1) * 8:(pg + 2) * 8], TBs[:, (pg + 1) * 8:(pg + 2) * 8], ALU.add), reads=[b_dec], writes=[b_dec])
                P.op("pe", MM(ps[4][:, 0:128], cstt[:, C_UTRI:C_UTRI + 128], LFp[u], True, True), reads=[b_cst, b_LFp[u]], writes=[psb[4]])
                P.op("dve", TT(DEC, ps[4][:, 0:128], SUF, ALU.add), reads=[psb[4], b_dec], writes=[b_dec])
                P.op("dve", TT(SC[u][:, 0, :], SC[u][:, 0, :], BIASD.rearrange("p g c -> p (g c)"), ALU.add), reads=[b_SC[u], b_biasd], writes=[b_SC[u]])
                P.op("dve", TT(SC[u][:, 1, :], SC[u][:, 1, :], DEC, ALU.add), reads=[b_SC[u], b_dec], writes=[b_SC[u]])
                P.op("act", ACTV(PR[u].rearrange("p a b -> p (a b)"), SC[u].rearrange("p a b -> p (a b)"), AF.Exp), reads=[b_SC[u]], writes=[b_PR[u]])
                t_ = sm[u]
                for a_ in range(2):
                    P.op("dve", RED(t_[:, a_ * 8:(a_ + 1) * 8], PR[u][:, a_, :].rearrange("p (g c) -> p c g", c=8)), reads=[b_PR[u]], writes=[b_sm[u]])
                P.op("dve", lambda h, t_=t_, msk=msk: h.scalar_tensor_tensor(out=t_[:, 0:16], in0=pself[:, 0:16], scalar=msk, in1=t_[:, 0:16], op0=ALU.mult, op1=ALU.add),
                     reads=[b_pself, b_cst, b_sm[u]], writes=[b_sm[u]])
                P.op("pe", MM(ps[4][:, 0:16], cstt[:, C_ONES:C_ONES + 128], t_[:, 0:16], True, True), reads=[b_cst, b_sm[u]], writes=[psb[4]])
                P.op("dve", RCP(t_[:, 16:32], ps[4][:, 0:16]), reads=[psb[4]], writes=[b_sm[u]])
                P.op("dve", lambda h, t_=t_, msk=msk: h.scalar_tensor_tensor(out=t_[:, 32:48], in0=pself[:, 0:16], scalar=msk, in1=t_[:, 16:32], op0=ALU.mult, op1=ALU.mult),
                     reads=[b_pself, b_cst, b_sm[u]], writes=[b_sm[u]])
                PRd = PR[u][:, 0, :].rearrange("p (g c) -> p g c", c=8)
                PRf = PR[u][:, 1, :].rearrange("p (g c) -> p g c", c=8)
                P.op("dve", TT(PRd, PRd, t_[:, 16:24].unsqueeze(1).to_broadcast([128, NPG, 8]), ALU.mult), reads=[b_PR[u], b_sm[u]], writes=[b_PR[u]])
                P.op("dve", TT(AFb[u][:, 0:NPG, :], PRf, t_[:, 24:32].unsqueeze(1).to_broadcast([128, NPG, 8]), ALU.mult), reads=[b_PR[u], b_sm[u]], writes=[b_AF[u]])
                P.op("dve", CP(AFb[u][:, NPG, :], t_[:, 40:48]), reads=[b_sm[u]], writes=[b_AF[u]])
                PR4 = PR[u][:, 0, :].rearrange("p (g h m) -> p g h m", h=4, m=2)
                P.op("dve", lambda h, u=u, PR4=PR4: h.scalar_tensor_tensor(out=ADb[u][:, 0:NPG, :], in0=PR4[:, :, :, 1], scalar=neglam, in1=PR4[:, :, :, 0], op0=ALU.mult, op1=ALU.add),
                     reads=[b_PR[u], b_misc], writes=[b_AD[u]])
                s4 = t_[:, 32:40].rearrange("p (h m) -> p h m", m=2)
                P.op("dve", lambda h, u=u, s4=s4: h.scalar_tensor_tensor(out=ADb[u][:, NPG, :], in0=s4[:, :, 1], scalar=neglam, in1=s4[:, :, 0], op0=ALU.mult, op1=ALU.add),
                     reads=[b_sm[u], b_misc], writes=[b_AD[u]])
                for pg in range(NPG + 1):
                    rd = Vdb[u][:, pg, :] if pg < NPG else v16[:, 0:512]
                    rf = Vfb[u][:, pg, :] if pg < NPG else v16[:, 512:1024]
                    P.op("pe", MM(ps[5][0:4, :], ADb[u][:, pg, :], rd, pg == 0, pg == NPG), reads=[b_AD[u], b_Vdb[u], b_16], writes=[psb[5]])
                    P.op("pe", MM(ps[6][0:8, :], AFb[u][:, pg, :], rf, pg == 0, pg == NPG), reads=[b_AF[u], b_Vfb[u], b_16], writes=[psb[6]])
                for bank, nh, DMm, acc in ((5, 4, DM4, accd), (6, 8, DM8, accf)):
                    P.op("dve", TT(mdt[0:nh, :], ps[bank][0:nh, :], DMm, ALU.mult), reads=[psb[bank], b_saux], writes=[b_md])
                    P.op("pe", MM(ps[7][0:16, :], ESv[0:nh, s * 16:(s + 1) * 16], mdt[0:nh, :], True, True), reads=[b_saux, b_md], writes=[psb[7]])
                    P.op("dve", TT(acc[0:16, :], ps[7][0:16, :], acc[0:16, :], ALU.add), reads=[psb[7], b_acc], writes=[b_acc])
            r = sm[0]
            a3 = accd.rearrange("p (a b) -> p a b", b=128)
            P.op("act", ACTV(prod[0], accd, AF.Square), reads=[b_acc], writes=[b_prod[0]])
            P.op("dve", RED(r[:, 4:8], prod[0].rearrange("p (a b) -> p a b", b=128)), reads=[b_prod[0]], writes=[b_sm[0]])
            P.op("act", ACTV(r[:, 8:12], r[:, 4:8], AF.Sqrt, scale=1.0 / 128, bias=float(EPS)), reads=[b_sm[0]], writes=[b_sm[0]])
            P.op("dve", RCP(r[:, 12:16], r[:, 8:12]), reads=[b_sm[0]], writes=[b_sm[0]])
            P.op("dve", TT(a3, a3, r[:, 12:16].unsqueeze(2).to_broadcast([128, 4, 128]), ALU.mult), reads=[b_acc, b_sm[0]], writes=[b_acc])
            ob = Vdb[0][:, 0, :]
            fb_ = Vdb[0][:, 1, :]
            P.op("dve", TT(ob.rearrange("p (a b) -> p a b", b=128), a3, sgs[:].unsqueeze(1).to_broadcast([128, 4, 128]), ALU.mult), reads=[b_acc, b_misc], writes=[b_Vdb[0]])
            P.op("dve", CP(fb_, accf), reads=[b_acc], writes=[b_Vdb[0]])
            pT7s = ps[7][:].bitcast(BF16)
            for c in range(8):
                src = ob[:, c * 128:(c + 1) * 128] if c < 4 else fb_[:, (c - 4) * 128:(c - 3) * 128]
                P.op("pe", TR(pT7s[:, c * 128:(c + 1) * 128], src), reads=[b_Vdb[0], b_misc], writes=[psb[7]])
            P.op("act", ACTV(odt16[:], pT7s.rearrange("p (a b) -> p a b", b=128), AF.Copy), reads=[psb[7]], writes=[b_odt16])
            P.barrier()
        AR.top = 8192

        lw_cnt = [0]

        def load_w(dst3, src, nch, c0, ncols, scale_t, wst, b_wst, b_dst):
            srcv = src.rearrange("(c p) n -> p c n", p=128)
            nsl = len(wst)
            for c in range(nch):
                for a in range(0, ncols, 1024):
                    w = min(1024, ncols - a)
                    k = lw_cnt[0]
                    lw_cnt[0] += 1
                    s_ = k % nsl
                    P.dma(DMA(wst[s_][:, 0:w], srcv[:, c, c0 + a:c0 + a + w]), b_wst[s_], writes=[b_wst[s_]])
                    eng = ("dve", "act", "pool")[k % 3]
                    if eng == "act":
                        if scale_t is not None:
                            P.op(eng, ACTV(dst3[:, c, a:a + w], wst[s_][:, 0:w], AF.Copy, scale=scale_t[:, c:c + 1]), reads=[b_wst[s_], b_vec], writes=[b_dst])
                        else:
                            P.op(eng, ACTV(dst3[:, c, a:a + w], wst[s_][:, 0:w], AF.Copy), reads=[b_wst[s_]], writes=[b_dst])
                    elif scale_t is not None:
                        P.op(eng, TS(dst3[:, c, a:a + w], wst[s_][:, 0:w], scale_t[:, c:c + 1], ALU.mult), reads=[b_wst[s_], b_vec], writes=[b_dst])
                    else:
                        P.op(eng, CP(dst3[:, c, a:a + w], wst[s_][:, 0:w]), reads=[b_wst[s_]], writes=[b_dst])

        def norm_T(xs, b_x, xn, b_xn_, st, b_st_, hT3, b_hT_, bank):
            P.op("act", ACTV(xn, xs, AF.Square, accum_out=st[:, 0:1]), reads=[b_x], writes=[b_xn_, b_st_])
            P.op("act", ACTV(st[:, 1:2], st[:, 0:1], AF.Sqrt, scale=1.0 / D, bias=float(EPS)), reads=[b_st_], writes=[b_st_])
            P.op("dve", RCP(st[:, 2:3], st[:, 1:2]), reads=[b_st_], writes=[b_st_])
            P.op("dve", TS(xn, xs, st[:, 2:3], ALU.mult), reads=[b_x, b_st_], writes=[b_xn_])
            pT = ps[bank][:].bitcast(BF16)
            for kc in range(8):
                P.op("pe", TR(pT[:, kc * 128:(kc + 1) * 128], xn[:, kc * 128:(kc + 1) * 128]), reads=[b_xn_, b_misc], writes=[psb[bank]])
            P.op("act", ACTV(hT3, pT.rearrange("p (a b) -> p a b", b=128), AF.Copy), reads=[psb[bank]], writes=[b_hT_])

        if "C" in phases:
            AR.mark()
            Wg = AR.alloc([8, 2048], BF16)
            Wa = AR.alloc([4, 1024], BF16)
            Wb = AR.alloc([4, 1024], BF16)
            Wo = AR.alloc([8, 1024], BF16)
            gbf = AR.alloc([2048], F32)
            gbb = AR.alloc([2048], BF16)
            wst_c = [AR.alloc([1024], F32) for _ in range(4)]
            b_wst_c = [Buf("cwst%d" % i_) for i_ in range(4)]
            b_Wc = Buf("Wc")
            b_gb = Buf("gb")
            load_w(Wg, w_in, 8, O_G, 2048, g1t, wst_c, b_wst_c, b_Wc)
            load_w(Wa, w_a, 4, 0, 1024, None, wst_c, b_wst_c, b_Wc)
            load_w(Wb, w_b, 4, 0, 1024, None, wst_c, b_wst_c, b_Wc)
            load_w(Wo, w_out, 8, 0, 1024, None, wst_c, b_wst_c, b_Wc)
            P.op("pool", MSET(gbf, 0.0), writes=[b_gb])
            P.dma(DMA(gbf[64:65, :], gate_b), b_gb, writes=[b_gb])
            P.op("dve", CP(gbb, gbf), reads=[b_gb], writes=[b_Wc])
            xs_t_c = [AR.alloc([D], F32) for _ in range(2)]
            xn_t_c = [AR.alloc([D], BF16) for _ in range(2)]
            hT_t_c = [AR.alloc([8, 128], BF16) for _ in range(2)]
            st4_c = [AR.alloc([4], F32) for _ in range(2)]
            G = AR.alloc([2048], F32)
            t0 = AR.alloc([D], F32)
            t1 = AR.alloc([D], F32)
            mb = AR.alloc([D], BF16)
            mT = AR.alloc([8, 128], BF16)
            x1t = [AR.alloc([D], F32) for _ in range(2)]
            b_xs_c = [Buf("cxs0"), Buf("cxs1")]
            b_xn_c = [Buf("cxn0"), Buf("cxn1")]
            b_hT_c = [Buf("chT0"), Buf("chT1")]
            b_st4_c = [Buf("cst40"), Buf("cst41")]
            b_G = Buf("G")
            b_t0 = Buf("t0")
            b_t1 = Buf("t1")
            b_mb = Buf("mb")
            b_mT = Buf("mT")
            b_x1 = [Buf("x1t0"), Buf("x1t1")]
            def c_stage1(i):
                s_ = i % 2
                if i == NT:
                    P.op("pool", MSET(xs_t_c[s_], 0.0), writes=[b_xs_c[s_]])
                    P.dma(DMA(xs_t_c[s_][0:NS, :], xsm), b_xs_c[s_], writes=[b_xs_c[s_]])
                else:
                    P.dma(DMA(xs_t_c[s_], xp[i * 128:(i + 1) * 128, :]), b_xs_c[s_], writes=[b_xs_c[s_]])
                norm_T(xs_t_c[s_], b_xs_c[s_], xn_t_c[s_], b_xn_c[s_], st4_c[s_], b_st4_c[s_], hT_t_c[s_], b_hT_c[s_], 7)

            def c_stage2(i):
                s_ = i % 2
                for blk in range(4):
                    for kc in range(8):
                        P.op("pe", MM(ps[blk][:, :], hT_t_c[s_][:, kc, :], Wg[:, kc, blk * 512:(blk + 1) * 512], kc == 0, False),
                             reads=[b_hT_c[s_], b_Wc], writes=[psb[blk]])
                    P.op("pe", MM(ps[blk][:, :], sel64b[:], gbb[:, blk * 512:(blk + 1) * 512], False, True), reads=[b_misc, b_Wc], writes=[psb[blk]])
                    P.op("act", ACTV(G[:, blk * 512:(blk + 1) * 512], ps[blk][:, :], AF.Sigmoid), reads=[psb[blk]], writes=[b_G])
                for blk in range(2):
                    for c in range(4):
                        if i < NT:
                            la, lb = ODT[:, c, i * 128:(i + 1) * 128], ODT[:, 4 + c, i * 128:(i + 1) * 128]
                            ra, rb_ = [b_QT[c][i]], [b_QT[4 + c][i]]
                        else:
                            la, lb = odt16[:, c, :], odt16[:, 4 + c, :]
                            ra = rb_ = [b_odt16]
                        P.op("pe", MM(ps[4 + blk][:, :], la, Wa[:, c, blk * 512:(blk + 1) * 512], c == 0, c == 3), reads=ra + [b_Wc], writes=[psb[4 + blk]])
                        P.op("pe", MM(ps[6 + blk][:, :], lb, Wb[:, c, blk * 512:(blk + 1) * 512], c == 0, c == 3), reads=rb_ + [b_Wc], writes=[psb[6 + blk]])
                    P.op("dve", TT(t0[:, blk * 512:(blk + 1) * 512], ps[4 + blk][:, :], G[:, blk * 512:(blk + 1) * 512], ALU.mult), reads=[psb[4 + blk], b_G], writes=[b_t0])
                    P.op("dve", TT(t1[:, blk * 512:(blk + 1) * 512], ps[6 + blk][:, :], G[:, 1024 + blk * 512:1024 + (blk + 1) * 512], ALU.mult), reads=[psb[6 + blk], b_G], writes=[b_t1])
                P.op("pool", TT(mb, t0, t1, ALU.add), reads=[b_t0, b_t1], writes=[b_mb])
                pT0 = ps[0][:].bitcast(BF16)
                for kc in range(8):
                    P.op("pe", TR(pT0[:, kc * 128:(kc + 1) * 128], mb[:, kc * 128:(kc + 1) * 128]), reads=[b_mb, b_misc], writes=[psb[0]])
                P.op("act", ACTV(mT, pT0.rearrange("p (a b) -> p a b", b=128), AF.Copy), reads=[psb[0]], writes=[b_mT])
                for blk in range(2):
                    for kc in range(8):
                        P.op("pe", MM(ps[1 + blk][:, :], mT[:, kc, :], Wo[:, kc, blk * 512:(blk + 1) * 512], kc == 0, kc == 7), reads=[b_mT, b_Wc], writes=[psb[1 + blk]])
                    P.op("dve", TT(x1t[s_][:, blk * 512:(blk + 1) * 512], ps[1 + blk][:, :], xs_t_c[s_][:, blk * 512:(blk + 1) * 512], ALU.add),
                         reads=[psb[1 + blk], b_xs_c[s_]], writes=[b_x1[s_]])
                P.dma(DMA(x1s[i * 128:(i + 1) * 128, :], x1t[s_]), b_x1[s_], reads=[b_x1[s_]], writes=[b_x1s[i]])
            c_stage1(0)
            for i in range(NT + 1):
                if i + 1 <= NT:
                    c_stage1(i + 1)
                c_stage2(i)
            P.barrier()
            AR.release()

        if "D" in phases:
            AR.top = 0
            Wgu = AR.alloc([8, 2 * DFF], BF16)
            Wdn = AR.alloc([NFC, 1024], BF16)
            wst_d = [AR.alloc([1024], F32) for _ in range(2)]
            b_wst_d = [Buf("dwst%d" % i_) for i_ in range(5)]
            b_Wd = Buf("Wd")
            top_d = AR.top
            wst_d = wst_d + [AR.alloc([1024], F32) for _ in range(3)]
            load_w(Wgu, w_gu, 8, 0, 2 * DFF, g2t, wst_d, b_wst_d, b_Wd)
            load_w(Wdn, w_dn, NFC, 0, 1024, None, wst_d, b_wst_d, b_Wd)
            P.barrier()
            AR.top = top_d
            x1b = [[AR.alloc([D], F32) for _ in range(2)] for _ in range(2)]
            b_x1b = [[Buf("x1b%d%d" % (a, b)) for b in range(2)] for a in range(2)]
            xn_t_d = [AR.alloc([D], BF16) for _ in range(2)]
            b_xn_d = [Buf("dxn0"), Buf("dxn1")]
            st4_d = [AR.alloc([4], F32) for _ in range(2)]
            b_st4_d = [Buf("dst40"), Buf("dst41")]
            h2T = [AR.alloc([8, 256], BF16) for _ in range(2)]
            b_h2 = [[Buf("h2T%d%d" % (a, b)) for b in range(2)] for a in range(2)]
            actT = AR.alloc([NFC, 256], BF16)
            b_act = Buf("actT")
            sgt = [AR.alloc([256], F32) for _ in range(2)]
            b_sg = [Buf("sg0"), Buf("sg1")]
            blocks = [[2 * b, 2 * b + 1] for b in range(8)] + [[NT]]
            gcnt = 0
            ncn = 0
            for bi, T in enumerate(blocks):
                sl = bi % 2
                for ti, i in enumerate(T):
                    xt = x1b[sl][ti]
                    P.dma(DMA(xt, x1s[i * 128:(i + 1) * 128, :]), b_x1b[sl][ti], reads=[b_x1s[i]], writes=[b_x1b[sl][ti]])
                    u = ncn % 2
                    ncn += 1
                    norm_T(xt, b_x1b[sl][ti], xn_t_d[u], b_xn_d[u], st4_d[u], b_st4_d[u], h2T[sl][:, :, ti * 128:(ti + 1) * 128], b_h2[sl][ti], 7)
                ntok = 128 * len(T)
                rh = [b_h2[sl][ti] for ti in range(len(T))]
                for fc in range(NFC):
                    a, b = ((0, 1), (2, 3), (4, 5))[gcnt % 3]
                    u = gcnt % 2
                    gcnt += 1
                    for kc in range(8):
                        P.op("pe", MM(ps[a][:, 0:ntok], Wgu[:, kc, fc * 128:(fc + 1) * 128], h2T[sl][:, kc, 0:ntok], kc == 0, kc == 7), reads=rh + [b_Wd], writes=[psb[a]])
                    for kc in range(8):
                        P.op("pe", MM(ps[b][:, 0:ntok], Wgu[:, kc, DFF + fc * 128:DFF + (fc + 1) * 128], h2T[sl][:, kc, 0:ntok], kc == 0, kc == 7), reads=rh + [b_Wd], writes=[psb[b]])
                    P.op("act", ACTV(sgt[u][:, 0:ntok], ps[a][:, 0:ntok], AF.Silu), reads=[psb[a]], writes=[b_sg[u]])
                    P.op("dve", TT(actT[:, fc, 0:ntok], ps[b][:, 0:ntok], sgt[u][:, 0:ntok], ALU.mult), reads=[psb[b], b_sg[u]], writes=[b_act])
                for ti, i in enumerate(T):
                    xt = x1b[sl][ti]
                    for blk in range(2):
                        bank = 6 + blk
                        for fc in range(NFC):
                            P.op("pe", MM(ps[bank][:, :], actT[:, fc, ti * 128:(ti + 1) * 128], Wdn[:, fc, blk * 512:(blk + 1) * 512], fc == 0, fc == NFC - 1),
                                 reads=[b_act, b_Wd], writes=[psb[bank]])
                        P.op("dve", TT(xt[:, blk * 512:(blk + 1) * 512], ps[bank][:, :], xt[:, blk * 512:(blk + 1) * 512], ALU.add),
                             reads=[psb[bank], b_x1b[sl][ti]], writes=[b_x1b[sl][ti]])
                    if i < NT:
                        P.dma(DMA(y_p[i * 128:(i + 1) * 128, :], xt), b_x1b[sl][ti], reads=[b_x1b[sl][ti]])
                    else:
                        P.dma(DMA(y_s, xt[0:NS, :]), b_x1b[sl][ti], reads=[b_x1b[sl][ti]])
            P.barrier()

        P.finish()
        P.emit()
    return nc, P


def _host_inputs(inp):
    f = lambda a: np.ascontiguousarray(np.asarray(a, dtype=np.float32))
    common = {
        "w_in": f(inp["w_in"][0]), "w_a": f(inp["w_branch_a"][0]), "w_b": f(inp["w_branch_b"][0]),
        "w_out": f(inp["w_out"][0]), "w_gu": f(inp["w_gate_up"][0]), "w_dn": f(inp["w_down"][0]),
        "cst": make_consts(),
        "g1T": f(np.asarray(inp["norm1_g"])[0].reshape(8, 128).T),
        "g2T": f(np.asarray(inp["norm2_g"])[0].reshape(8, 128).T),
        "vec": f(np.concatenate([np.asarray(inp["diff_q_g"])[0], np.asarray(inp["diff_k_g"])[0], np.asarray(inp["fox_q_g"])[0],
                                 np.asarray(inp["fox_k_g"])[0], np.asarray(inp["diff_subln_g"])[0], np.asarray(inp["fox_f_b"])[0],
                                 np.asarray(inp["rel_bias"])[31], np.asarray(inp["rel_bias"])[0],
                                 np.asarray(inp["diff_lambda"])[0].reshape(-1)])[None, :]),
        "gate_b": f(np.asarray(inp["gate_b"])[0].reshape(1, -1)),
        "rel_bias": f(inp["rel_bias"]),
    }
    return common


OUT_NAMES = ["y_p", "y_s", "o_dk", "o_dv", "o_fk", "o_fv", "o_lf", "s_dk", "s_dv", "s_fk", "s_fv", "s_lf"]


def run(inp, phases="ABCDS"):
    nc, P = build(phases)
    common = _host_inputs(inp)
    if "C" not in phases:
        for k in ("w_a", "w_b", "w_out", "gate_b"):
            common.pop(k)
    if "D" not in phases:
        for k in ("w_gu", "w_dn"):
            common.pop(k)
    xpr = np.asarray(inp["x_prompt"], dtype=np.float32)
    xsa = np.asarray(inp["x_sample"], dtype=np.float32)
    in_maps = []
    if "S" in phases:
        r2 = lambda k, w: np.asarray(inp[k], dtype=np.float32).reshape(NPHYS * 128, w)
        c_all_host = np.concatenate([r2("cache_diff_k", 512), r2("cache_fox_k", 512), r2("cache_diff_v", 512),
                                     r2("cache_fox_v", 512), r2("cache_fox_logf", 8)], axis=1)
    for c in range(NCORES):
        m = dict(common)
        m["xp"] = np.ascontiguousarray(xpr[c])
        m["xs"] = np.ascontiguousarray(xsa[c * NS:(c + 1) * NS, 0])
        if "S" in phases:
            m["c_all"] = c_all_host
            m["ptab"] = np.ascontiguousarray(np.asarray(inp["page_table"], dtype=np.int32)[c * NS:(c + 1) * NS].reshape(1, -1))
        in_maps.append(m)
    res = run_bass_kernel_spmd(nc, in_maps, core_ids=list(range(NCORES)))
    R = res.results
    if DBG_OUT:
        global DBG_RES
        DBG_RES = R
    cat = lambda k: np.concatenate([np.asarray(r[k]) for r in R], axis=0)
    yp = cat("y_p").reshape(8, SEQ, D)
    ys = cat("y_s").reshape(128, 1, D)
    outs = [yp, ys,
            cat("o_dk").reshape(1, 8, SEQ, 4, 2, 64), cat("o_dv").reshape(1, 8, SEQ, 4, 128),
            cat("o_fk").reshape(1, 8, SEQ, 8, 64), cat("o_fv").reshape(1, 8, SEQ, 8, 64), cat("o_lf").reshape(1, 8, SEQ, 8),
            cat("s_dk").reshape(1, 128, 1, 4, 2, 64), cat("s_dv").reshape(1, 128, 1, 4, 128),
            cat("s_fk").reshape(1, 128, 1, 8, 64), cat("s_fv").reshape(1, 128, 1, 8, 64), cat("s_lf").reshape(1, 128, 1, 8)]
    return tuple(np.ascontiguousarray(o, dtype=np.float32) for o in outs)


def kernel(**inputs):
    return run(inputs, "ABCDS")
```

```python
import math
import numpy as np
import concourse.bass as bass
import concourse.mybir as mybir
from concourse.bass_utils import run_bass_kernel_spmd
from contextlib import ExitStack

F32 = mybir.dt.float32
BF16 = mybir.dt.bfloat16
I32 = mybir.dt.int32
AF = mybir.ActivationFunctionType
ALU = mybir.AluOpType
AX = mybir.AxisListType

NCORES = 8
D = 1024
SEQ = 2048
NT = 16
NS = 16
HD = 64
QKVW = 3080
INC = 5128
DFF = 2816
NFC = 22
NPHYS = 2560
NPG = 16
EPS = 1e-6
LAMBDA_INIT = 0.8 - 0.6 * math.exp(-0.3 * 0)
NEG = -30000.0
DBG_TILES = list(range(NT + 1))
DBG_STAGE = 99
DBG_OUT = False

O_DQ, O_DK, O_DV, O_FQ, O_FK, O_FV, O_LF, O_G = 0, 512, 1024, 1536, 2048, 2560, 3072, 3080

C_ID, C_LTRI, C_ONES, C_SEL64, C_UTRI, C_MTRI = 0, 128, 256, 384, 512, 640
C_J = 768
C_MAIN = 896
C_DM8, C_DM4, C_ES, C_OH, C_MROW = 896, 1408, 1920, 2176, 2560
C_PIO = 2560
C_N = 2944


def _rel_bucket(n):
    n = max(n, 0)
    if n < 16:
        return n
    large = 16 + int(np.float32(np.log(np.float32(max(n, 1)) / np.float32(16)) / np.float32(math.log(128 / 16)) * np.float32(16)))
    return min(large, 31)


def make_consts():
    c = np.zeros((128, C_N), np.float32)
    i = np.arange(128)
    c[:, C_ID:C_ID + 128] = np.eye(128)
    c[:, C_LTRI:C_LTRI + 128] = (i[:, None] <= i[None, :])
    c[:, C_ONES:C_ONES + 128] = 1.0
    c[64, C_SEL64:C_SEL64 + 128] = 1.0
    c[:, C_UTRI:C_UTRI + 128] = (i[:, None] > i[None, :])
    c[:, C_MTRI:C_MTRI + 128] = np.where(i[:, None] > i[None, :], NEG, 0.0)
    c[:, C_J:C_J + 128] = np.eye(128)[::-1]
    c[:, C_PIO] = np.arange(128)
    for h in range(8):
        c[h, C_DM8 + h * 64:C_DM8 + (h + 1) * 64] = 1.0
    for h in range(4):
        c[h, C_DM4 + h * 128:C_DM4 + (h + 1) * 128] = 1.0
    for s in range(16):
        c[0:8, C_ES + s * 16 + s] = 1.0
    for idx in range(383):
        n = idx - 127
        if n >= 0:
            b = _rel_bucket(n)
            c[b, C_OH + idx] += 1.0
            c[31, C_OH + idx] -= 1.0
        else:
            c[32, C_OH + idx] = NEG
    return c


class Buf:
    __slots__ = ("name", "w", "readers", "dsem", "dtotal")

    def __init__(self, name):
        self.name = name
        self.w = None
        self.readers = []
        self.dsem = None
        self.dtotal = 0


class Prog:
    ENGS = ("pe", "act", "dve", "pool", "sp")

    def __init__(self, nc, stack):
        self.nc = nc
        self.stack = stack
        self.q = {e: [] for e in self.ENGS}
        self.esem = {e: stack.enter_context(nc.semaphore("es_" + e)) for e in self.ENGS}
        self.ecnt = {e: 0 for e in self.ENGS}
        self.waited = {e: {} for e in self.ENGS}
        self.dsems = []
        self.nd = 0
        self.phase_sem = stack.enter_context(nc.semaphore("phase"))
        self.nphase = 0
        self.ninstr = 0

    def _dsem(self, b):
        if b.dsem is None:
            b.dsem = self.stack.enter_context(self.nc.semaphore("ds%d" % self.nd))
            self.nd += 1
            self.dsems.append(b)
        return b.dsem

    def _wait(self, E, tok):
        if tok is None:
            return
        if tok[0] == "e":
            name, val = tok[1], tok[2]
            if name == E and E == "pe":
                return
            key = ("e", name)
            sem = self.esem[name]
        else:
            sb = tok[1]
            key = ("d", id(sb))
            sem = sb.dsem
            val = sb.dtotal
        if self.waited[E].get(key, 0) >= val:
            return
        self.waited[E][key] = val
        self.q[E].append(lambda h, sem=sem, val=val: h.wait_ge(sem, val))

    def _deps(self, E, reads, writes):
        for b in reads:
            self._wait(E, b.w)
        for b in writes:
            if not (b.w is not None and b.w[0] == "e" and b.w[1] == E):
                self._wait(E, b.w)
            for r in b.readers:
                if r[0] == "e" and r[1] == E:
                    continue
                self._wait(E, r)

    def op(self, E, fn, reads=(), writes=()):
        self._deps(E, reads, writes)
        self.ecnt[E] += 1
        tok = ("e", E, self.ecnt[E])
        sem = self.esem[E]
        self.q[E].append(lambda h, fn=fn, sem=sem: fn(h).then_inc(sem, 1))
        for b in reads:
            b.readers.append(tok)
        for b in writes:
            b.w = tok
            b.readers = []
        self.ninstr += 1

    def dma(self, fn, sembuf, reads=(), writes=(), E="sp"):
        self._deps(E, reads, writes)
        sem = self._dsem(sembuf)
        sembuf.dtotal += 16
        tok = ("d", sembuf)
        self.q[E].append(lambda h, fn=fn, sem=sem: fn(h).then_inc(sem, 16))
        for b in reads:
            b.readers.append(tok)
        for b in writes:
            b.w = tok
            b.readers = []
        self.ninstr += 1

    def barrier(self):
        sp = "sp"
        for e in self.ENGS:
            if e != sp and self.ecnt[e] > 0:
                self._wait(sp, ("e", e, self.ecnt[e]))
        for b in self.dsems:
            self._wait(sp, ("d", b))
        self.nphase += 1
        n = self.nphase
        ps = self.phase_sem
        self.q[sp].append(lambda h: h.sem_inc(ps, 1))
        for e in self.ENGS:
            if e != sp:
                self.q[e].append(lambda h, n=n: h.wait_ge(ps, n))
        for e in self.ENGS:
            for e2 in self.ENGS:
                self.waited[e][("e", e2)] = self.ecnt[e2]
            for b in self.dsems:
                self.waited[e][("d", id(b))] = b.dtotal

    def finish(self):
        sp = "sp"
        for e in self.ENGS:
            if e != sp and self.ecnt[e] > 0:
                self._wait(sp, ("e", e, self.ecnt[e]))
        for b in self.dsems:
            self._wait(sp, ("d", b))

    def emit(self):
        with self.nc.Block() as block:
            @block.tensor
            def _(h):
                for f in self.q["pe"]:
                    f(h)

            @block.scalar
            def _(h):
                for f in self.q["act"]:
                    f(h)

            @block.vector
            def _(h):
                for f in self.q["dve"]:
                    f(h)

            @block.gpsimd
            def _(h):
                for f in self.q["pool"]:
                    f(h)

            @block.sync
            def _(h):
                for f in self.q["sp"]:
                    f(h)


class Arena:
    def __init__(self, t, words):
        self.t = t
        self.words = words
        self.top = 0
        self.marks = []

    def mark(self):
        self.marks.append(self.top)

    def release(self):
        self.top = self.marks.pop()

    def alloc(self, shape, dtype):
        n = int(np.prod(shape))
        w = n if dtype in (F32, I32) else (n + 1) // 2
        w = (w + 1) // 2 * 2
        assert self.top + w <= self.words, ("arena overflow", self.top, w, self.words)
        v = self.t[:, self.top:self.top + w]
        self.top += w
        if dtype not in (F32,):
            v = v.bitcast(dtype)
        v = v[:, 0:n]
        if len(shape) == 2:
            return v.rearrange("p (a b) -> p a b", b=shape[1])
        if len(shape) == 3:
            return v.rearrange("p (a b c) -> p a b c", b=shape[1], c=shape[2])
        return v


def build(phases="ABCDS"):
    nc = bass.Bass("TRN2", target_bir_lowering=False)
    dt = nc.dram_tensor

    def din(name, shape, dtype=F32):
        return dt(name, list(shape), dtype, kind="ExternalInput").ap()

    def dout(name, shape):
        return dt(name, list(shape), F32, kind="ExternalOutput").ap()

    xp = din("xp", [SEQ, D])
    xsm = din("xs", [NS, D])
    w_in = din("w_in", [D, INC])
    if "C" in phases:
        w_a = din("w_a", [512, D])
        w_b = din("w_b", [512, D])
        w_out = din("w_out", [D, D])
        gate_b = din("gate_b", [1, 2 * D])
    if "D" in phases:
        w_gu = din("w_gu", [D, 2 * DFF])
        w_dn = din("w_dn", [DFF, D])
    cst = din("cst", [128, C_N])
    g1T = din("g1T", [128, 8])
    g2T = din("g2T", [128, 8])
    V_GQ, V_GK, V_FQ, V_FK, V_SG, V_FB, V_B31, V_B0, V_LAM, V_N = 0, 64, 128, 192, 256, 384, 392, 396, 400, 656
    vec = din("vec", [1, V_N])
    rel_bias = din("rel_bias", [32, 4])
    has_S = "S" in phases
    if has_S:
        c_all = din("c_all", [NPHYS * 128, 2056])
        ptab = din("ptab", [1, NS * NPG], I32)

    y_p = dout("y_p", [SEQ, D])
    y_s = dout("y_s", [NS, D])
    o_dk = dout("o_dk", [SEQ, 512])
    o_dv = dout("o_dv", [SEQ, 512])
    o_fk = dout("o_fk", [SEQ, 512])
    o_fv = dout("o_fv", [SEQ, 512])
    o_lf = dout("o_lf", [SEQ, 8])
    s_dk = dout("s_dk", [NS, 512])
    s_dv = dout("s_dv", [NS, 512])
    s_fk = dout("s_fk", [NS, 512])
    s_fv = dout("s_fv", [NS, 512])
    s_lf = dout("s_lf", [NS, 8])
    gtab_h = dt("gtab", [4, 384], F32, kind=("ExternalOutput" if DBG_OUT else "Internal"))
    gtab = gtab_h.ap()
    x1s = dt("x1s", [SEQ + 128, D], F32, kind=("ExternalOutput" if DBG_OUT else "Internal")).ap()
    if DBG_OUT:
        dbg_odt = dt("dbg_odt", [128, 8 * SEQ], BF16, kind="ExternalOutput").ap()

    stack = ExitStack()
    with stack:
        P = Prog(nc, stack)
        sb = lambda name, shape, dtype=F32: stack.enter_context(nc.sbuf_tensor(name, list(shape), dtype))
        cstt = sb("cstt", [128, C_MAIN])
        vect = sb("vect", [128, V_N])
        g1t = sb("g1t", [128, 8])
        g2t = sb("g2t", [128, 8])
        identb = sb("identb", [128, 128], BF16)
        mtrib = sb("mtrib", [128, 128], BF16)
        jb = sb("jb", [128, 128], BF16)
        gq8 = sb("gq8", [128, 64])
        fq8 = sb("fq8", [128, 64])
        sgs = sb("sgs", [128, 128])
        lamt = sb("lamt", [128, 8])
        TTb = sb("TTb", [128, 4, 256], BF16)
        LF = sb("LF", [128, NT + 1, 8])
        CC = sb("CC", [128, NT, 8])
        CR = sb("CR", [128, NT, 8])
        small = sb("small", [128, 64])
        onesb = sb("onesb", [128, 128], BF16)
        sel64b = sb("sel64b", [128, 128], BF16)
        odt16 = sb("odt16", [128, 8, 128], BF16)
        q16 = sb("q16", [128, 1024], BF16)
        k16 = sb("k16", [128, 1024])
        v16 = sb("v16", [128, 1024], BF16)
        b_16 = Buf("t16")
        b_odt16 = Buf("odt16")
        b_x1s = [Buf("x1s%d" % i) for i in range(NT + 1)]

        def MM(out, lhsT, rhs, start, stop):
            return lambda h: h.matmul(out, lhsT, rhs, start=start, stop=stop)

        def TR(out, in_):
            return lambda h: h.transpose(out=out, in_=in_, identity=identb[:])

        def ACTV(out, in_, func, **kw):
            return lambda h: h.activation(out=out, in_=in_, func=func, **kw)

        def TT(out, in0, in1, op):
            return lambda h: h.tensor_tensor(out=out, in0=in0, in1=in1, op=op)

        def TS(out, in0, s1, op0):
            return lambda h: h.tensor_scalar(out=out, in0=in0, scalar1=s1, scalar2=None, op0=op0)

        def CP(out, in_):
            return lambda h: h.tensor_copy(out=out, in_=in_)

        def RCP(out, in_):
            return lambda h: h.reciprocal(out=out, in_=in_)

        def RED(out, in_):
            return lambda h: h.tensor_reduce(out=out, in_=in_, axis=AX.X, op=ALU.add)

        def DMA(out, in_):
            return lambda h: h.dma_start(out=out, in_=in_)

        def MSET(ap, v):
            return lambda h: h.memset(ap, v)
        ARW = 47500
        art = sb("arena", [128, ARW])
        AR = Arena(art, ARW)
        ps = [stack.enter_context(nc.psum_tensor("ps%d" % i, [128, 512], F32)) for i in range(8)]
        psb = [P_buf for P_buf in [Buf("ps%d" % i) for i in range(8)]]

        ident_f = cstt[:, C_ID:C_ID + 128]
        b_cst = Buf("cst")
        b_vec = Buf("vec")
        b_misc = Buf("misc")

        P.dma(lambda h: h.dma_start(out=cstt[:], in_=cst[:, 0:C_MAIN]), b_cst, writes=[b_cst])
        AR.mark()
        caux = AR.alloc([384], F32)
        b_caux = Buf("caux")
        P.dma(lambda h: h.dma_start(out=caux, in_=cst[:, C_OH:C_OH + 384]), b_caux, writes=[b_caux])
        P.dma(lambda h: h.dma_start(out=vect[:], in_=vec.partition_broadcast(128)), b_vec, writes=[b_vec])
        P.dma(lambda h: h.dma_start(out=g1t[:], in_=g1T), b_vec, writes=[b_vec])
        P.dma(lambda h: h.dma_start(out=g2t[:], in_=g2T), b_vec, writes=[b_vec])
        P.op("dve", lambda h: h.tensor_copy(out=identb[:], in_=cstt[:, C_ID:C_ID + 128]), reads=[b_cst], writes=[b_misc])
        P.op("dve", lambda h: h.tensor_copy(out=mtrib[:], in_=cstt[:, C_MTRI:C_MTRI + 128]), reads=[b_cst], writes=[b_misc])
        P.op("dve", lambda h: h.tensor_copy(out=jb[:], in_=cstt[:, C_J:C_J + 128]), reads=[b_cst], writes=[b_misc])
        P.op("dve", CP(onesb[:], cstt[:, C_ONES:C_ONES + 128]), reads=[b_cst], writes=[b_misc])
        P.op("dve", CP(sel64b[:], cstt[:, C_SEL64:C_SEL64 + 128]), reads=[b_cst], writes=[b_misc])
        P.op("pool", MSET(odt16[:], 0.0), writes=[b_odt16])
        P.op("dve", lambda h: h.tensor_scalar(out=gq8[:], in0=vect[:, V_GQ:V_GQ + 64], scalar1=0.125, scalar2=None, op0=ALU.mult), reads=[b_vec], writes=[b_misc])
        P.op("dve", lambda h: h.tensor_scalar(out=fq8[:], in0=vect[:, V_FQ:V_FQ + 64], scalar1=0.125, scalar2=None, op0=ALU.mult), reads=[b_vec], writes=[b_misc])
        P.op("dve", lambda h: h.tensor_scalar(out=sgs[:], in0=vect[:, V_SG:V_SG + 128], scalar1=float(1.0 - LAMBDA_INIT), scalar2=None, op0=ALU.mult), reads=[b_vec], writes=[b_misc])
        lm = vect[:, V_LAM:V_LAM + 256]
        P.op("dve", lambda h: h.tensor_tensor(out=small[:, 0:64], in0=lm[:, 0:64], in1=lm[:, 64:128], op=ALU.mult), reads=[b_vec], writes=[b_misc])
        P.op("dve", lambda h: h.tensor_reduce(out=lamt[:, 0:1], in_=small[:, 0:64], axis=AX.X, op=ALU.add), reads=[b_misc], writes=[b_misc])
        P.op("dve", lambda h: h.tensor_tensor(out=small[:, 0:64], in0=lm[:, 128:192], in1=lm[:, 192:256], op=ALU.mult), reads=[b_vec, b_misc], writes=[b_misc])
        P.op("dve", lambda h: h.tensor_reduce(out=lamt[:, 1:2], in_=small[:, 0:64], axis=AX.X, op=ALU.add), reads=[b_misc], writes=[b_misc])
        P.op("act", lambda h: h.activation(out=lamt[:, 2:4], in_=lamt[:, 0:2], func=AF.Exp), reads=[b_misc], writes=[b_misc])
        P.op("dve", lambda h: h.tensor_tensor(out=lamt[:, 4:5], in0=lamt[:, 3:4], in1=lamt[:, 2:3], op=ALU.subtract), reads=[b_misc], writes=[b_misc])
        P.op("dve", lambda h: h.tensor_scalar(out=lamt[:, 4:5], in0=lamt[:, 4:5], scalar1=float(-LAMBDA_INIT), scalar2=None, op0=ALU.add), reads=[b_misc], writes=[b_misc])
        neglam = lamt[:, 4:5]

        rbx = AR.alloc([4], F32)
        b_rb = Buf("rb")
        P.op("dve", MSET(rbx[32:64, :], 1.0), writes=[b_rb])
        P.dma(lambda h: h.dma_start(out=rbx[0:32, :], in_=rel_bias), b_rb, writes=[b_rb])
        P.op("pe", lambda h: h.matmul(ps[0][0:4, 0:384], rbx[0:33, :], caux[0:33, 0:384], start=True, stop=True), reads=[b_rb, b_caux], writes=[psb[0]])
        gt_s = AR.alloc([384], F32)
        TTf = AR.alloc([4, 256], F32)
        b_gt = Buf("gt")
        b_gtd = Buf("gtd")
        b_tt = Buf("tt")
        P.op("dve", lambda h: h.tensor_copy(out=gt_s[0:4, :], in_=ps[0][0:4, 0:384]), reads=[psb[0]], writes=[b_gt])
        P.dma(lambda h: h.dma_start(out=gtab, in_=gt_s[0:4, :]), b_gt, reads=[b_gt], writes=[b_gtd])
        tt_src = bass.AP(tensor=gtab_h, offset=0, ap=[[1, 128], [384, 4], [1, 256]])
        P.dma(lambda h: h.dma_start(out=TTf, in_=tt_src), b_tt, reads=[b_gtd], writes=[b_tt])
        P.op("dve", lambda h: h.tensor_copy(out=TTb[:], in_=TTf), reads=[b_tt], writes=[b_misc])
        P.barrier()
        AR.release()

        AR.mark()
        QT = AR.alloc([8, SEQ], BF16)
        KT = AR.alloc([8, SEQ], BF16)
        Vd = AR.alloc([NT, 4 * 130], BF16)
        Vf = AR.alloc([NT, 8 * 66], BF16)
        b_QT = [[Buf("QT%d_%d" % (p_, i)) for i in range(NT)] for p_ in range(8)]
        b_KT = [[Buf("KT%d_%d" % (p_, i)) for i in range(NT)] for p_ in range(8)]
        b_V = [Buf("V%d" % i) for i in range(NT)]
        b_LF = [Buf("LF%d" % i) for i in range(NT + 1)]
        Vd4 = Vd.rearrange("p t (h e) -> p t h e", e=130)
        Vf4 = Vf.rearrange("p t (h e) -> p t h e", e=66)
        for i in range(NT):
            P.op("pool", lambda h, i=i: h.memset(Vd4[:, i, :, 128:129], 1.0), writes=[b_V[i]])
            P.op("pool", lambda h, i=i: h.memset(Vf4[:, i, :, 64:65], 1.0), writes=[b_V[i]])

        if "A" in phases:
            AR.mark()
            Wq = AR.alloc([8, QKVW], BF16)
            top_a = AR.top
            wstA = [AR.alloc([QKVW // 2], F32) for _ in range(4)]
            b_wstA = [Buf("wstA%d" % i_) for i_ in range(4)]
            b_W = Buf("Wq")
            w_in_v = w_in.rearrange("(kc p) c -> p kc c", p=128)
            HW = QKVW // 2
            for kc in range(8):
                for hf in range(2):
                    k_ = kc * 2 + hf
                    s4_ = k_ % 4
                    P.dma(DMA(wstA[s4_], w_in_v[:, kc, hf * HW:(hf + 1) * HW]), b_wstA[s4_], writes=[b_wstA[s4_]])
                    if k_ % 2 == 0:
                        P.op("dve", TS(Wq[:, kc, hf * HW:(hf + 1) * HW], wstA[s4_], g1t[:, kc:kc + 1], ALU.mult), reads=[b_wstA[s4_], b_vec], writes=[b_W])
                    else:
                        P.op("act", ACTV(Wq[:, kc, hf * HW:(hf + 1) * HW], wstA[s4_], AF.Copy, scale=g1t[:, kc:kc + 1]), reads=[b_wstA[s4_], b_vec], writes=[b_W])
            P.barrier()
            AR.top = top_a
            xs_t = [AR.alloc([D], F32) for _ in range(2)]
            xn_t = [AR.alloc([D], BF16) for _ in range(2)]
            hT_t = [AR.alloc([8, 128], BF16) for _ in range(2)]
            st4 = [AR.alloc([4], F32) for _ in range(2)]
            sq_t = [AR.alloc([512], F32) for _ in range(2)]
            kn_t = [AR.alloc([512], F32) for _ in range(2)]
            kb_t = [AR.alloc([512], BF16) for _ in range(2)]
            og_t = [AR.alloc([512], F32) for _ in range(4)]
            st8 = [AR.alloc([32], F32) for _ in range(2)]
            lg_t = [AR.alloc([40], F32) for _ in range(2)]
            b_xs = [Buf("xs0"), Buf("xs1")]
            b_xn = [Buf("xn0"), Buf("xn1")]
            b_hT = [Buf("hT0"), Buf("hT1")]
            b_st4 = [Buf("st40"), Buf("st41")]
            b_sq = [Buf("sq0"), Buf("sq1")]
            b_kn = [Buf("kn0"), Buf("kn1")]
            b_kb = [Buf("kb0"), Buf("kb1")]
            b_og = [Buf("og%d" % i) for i in range(4)]
            b_st8 = [Buf("st80"), Buf("st81")]
            b_lg = [Buf("lg0"), Buf("lg1")]
            b_junk = Buf("junk")
            nq = [0]
            nog = [0]
            nkn = [0]

            def stage1(i):
                s = i % 2
                rows = 128 if i < NT else NS
                if i == NT:
                    P.op("pool", lambda h, s=s: h.memset(xs_t[s], 0.0), writes=[b_xs[s]])
                    P.dma(lambda h, s=s: h.dma_start(out=xs_t[s][0:NS, :], in_=xsm), b_xs[s], writes=[b_xs[s]])
                else:
                    P.dma(lambda h, s=s, i=i: h.dma_start(out=xs_t[s], in_=xp[i * 128:(i + 1) * 128, :]), b_xs[s], writes=[b_xs[s]])
                P.op("act", lambda h, s=s: h.activation(out=xn_t[s], in_=xs_t[s], func=AF.Square, accum_out=st4[s][:, 0:1]),
                     reads=[b_xs[s]], writes=[b_xn[s], b_st4[s]])
                P.op("act", lambda h, s=s: h.activation(out=st4[s][:, 1:2], in_=st4[s][:, 0:1], func=AF.Sqrt, scale=1.0 / D, bias=float(EPS)),
                     reads=[b_st4[s]], writes=[b_st4[s]])
                P.op("dve", lambda h, s=s: h.reciprocal(out=st4[s][:, 2:3], in_=st4[s][:, 1:2]), reads=[b_st4[s]], writes=[b_st4[s]])
                P.op("dve", lambda h, s=s: h.tensor_scalar(out=xn_t[s], in0=xs_t[s], scalar1=st4[s][:, 2:3], scalar2=None, op0=ALU.mult),
                     reads=[b_xs[s], b_st4[s]], writes=[b_xn[s]])
                if DBG_STAGE < 1:
                    return
                pT = ps[7][:].bitcast(BF16)
                for kc in range(8):
                    P.op("pe", lambda h, s=s, kc=kc: h.transpose(out=pT[:, kc * 128:(kc + 1) * 128], in_=xn_t[s][:, kc * 128:(kc + 1) * 128], identity=identb[:]),
                         reads=[b_xn[s], b_misc], writes=[psb[7]])
                P.op("act", lambda h, s=s: h.activation(out=hT_t[s].rearrange("p a b -> p (a b)"), in_=pT, func=AF.Copy),
                     reads=[psb[7]], writes=[b_hT[s]])

            def stage2(i):
                s = i % 2
                rows = 128 if i < NT else NS

                def proj(bank, c0, n, s=s):
                    for kc in range(8):
                        P.op("pe", lambda h, kc=kc: h.matmul(ps[bank][:, 0:n], hT_t[s][:, kc, :], Wq[:, kc, c0:c0 + n], start=(kc == 0), stop=(kc == 7)),
                             reads=[b_hT[s], b_W], writes=[psb[bank]])

                def qknorm(bank, gains, is_k, dst_cols, out_dram, i=i, s=s, rows=rows):
                    u = nq[0] % 2
                    nq[0] += 1
                    P.op("act", lambda h: h.activation(out=sq_t[u], in_=ps[bank][:], func=AF.Square), reads=[psb[bank]], writes=[b_sq[u]])
                    P.op("dve", lambda h: h.tensor_reduce(out=st8[u][:, 0:8], in_=sq_t[u].rearrange("p (a b) -> p a b", b=64), axis=AX.X, op=ALU.add),
                         reads=[b_sq[u]], writes=[b_st8[u]])
                    P.op("act", lambda h: h.activation(out=st8[u][:, 8:16], in_=st8[u][:, 0:8], func=AF.Sqrt, scale=1.0 / HD, bias=float(EPS)),
                         reads=[b_st8[u]], writes=[b_st8[u]])
                    P.op("dve", lambda h: h.reciprocal(out=st8[u][:, 16:24], in_=st8[u][:, 8:16]), reads=[b_st8[u]], writes=[b_st8[u]])
                    kk = nkn[0] % 2
                    nkn[0] += 1
                    P.op("dve", lambda h: h.tensor_tensor(out=kn_t[kk].rearrange("p (a b) -> p a b", b=64), in0=ps[bank][:].rearrange("p (a b) -> p a b", b=64),
                                                          in1=st8[u][:, 16:24].unsqueeze(2).to_broadcast([128, 8, 64]), op=ALU.mult),
                         reads=[psb[bank], b_st8[u]], writes=[b_kn[kk]])
                    gb = gains.unsqueeze(1).to_broadcast([128, 8, 64])
                    if is_k:
                        o = nog[0] % 4
                        nog[0] += 1
                        dstf = og_t[o] if i < NT else k16[:, dst_cols:dst_cols + 512]
                        bdst = b_og[o] if i < NT else b_16
                        P.op("dve", lambda h: h.tensor_tensor(out=dstf.rearrange("p (a b) -> p a b", b=64), in0=kn_t[kk].rearrange("p (a b) -> p a b", b=64), in1=gb, op=ALU.mult),
                             reads=[b_kn[kk], b_vec, b_misc], writes=[bdst])
                        P.dma(lambda h: h.dma_start(out=out_dram, in_=dstf[0:rows, :]), bdst, reads=[bdst])
                        if i < NT:
                            P.op("act", lambda h: h.activation(out=kb_t[u], in_=dstf, func=AF.Copy), reads=[bdst], writes=[b_kb[u]])
                    else:
                        dstb = kb_t[u] if i < NT else q16[:, dst_cols:dst_cols + 512]
                        bdst = b_kb[u] if i < NT else b_16
                        P.op("dve", lambda h: h.tensor_tensor(out=dstb.rearrange("p (a b) -> p a b", b=64), in0=kn_t[kk].rearrange("p (a b) -> p a b", b=64), in1=gb, op=ALU.mult),
                             reads=[b_kn[kk], b_vec, b_misc], writes=[bdst])
                    if i < NT:
                        pT2 = ps[6][:].bitcast(BF16)
                        for j in range(4):
                            P.op("pe", lambda h, j=j: h.transpose(out=pT2[:, j * 128:(j + 1) * 128], in_=kb_t[u][:, j * 128:(j + 1) * 128], identity=identb[:]),
                                 reads=[b_kb[u], b_misc], writes=[psb[6]])
                        T, bT = (KT, b_KT) if is_k else (QT, b_QT)
                        pair0 = dst_cols // 128
                        P.op("dve", lambda h: h.tensor_copy(out=T[:, pair0:pair0 + 4, i * 128:(i + 1) * 128], in_=pT2[:, 0:512].rearrange("p (a b) -> p a b", b=128)),
                             reads=[psb[6]], writes=[bT[pp][i] for pp in range(pair0, pair0 + 4)])

                def vproj(bank, out_dram, is_d, i=i, rows=rows):
                    o = nog[0] % 4
                    nog[0] += 1
                    P.op("act", lambda h: h.activation(out=og_t[o], in_=ps[bank][:], func=AF.Copy), reads=[psb[bank]], writes=[b_og[o]])
                    P.dma(lambda h: h.dma_start(out=out_dram, in_=og_t[o][0:rows, :]), b_og[o], reads=[b_og[o]])
                    if i < NT:
                        if is_d:
                            P.op("dve", lambda h: h.tensor_copy(out=Vd4[:, i, :, 0:128], in_=og_t[o].rearrange("p (a b) -> p a b", b=128)), reads=[b_og[o]], writes=[b_V[i]])
                        else:
                            P.op("dve", lambda h: h.tensor_copy(out=Vf4[:, i, :, 0:64], in_=og_t[o].rearrange("p (a b) -> p a b", b=64)), reads=[b_og[o]], writes=[b_V[i]])
                    else:
                        c0 = 0 if is_d else 512
                        P.op("dve", lambda h: h.tensor_copy(out=v16[:, c0:c0 + 512], in_=og_t[o]), reads=[b_og[o]], writes=[b_16])

                r0, r1 = i * 128, i * 128 + rows
                pr = (lambda t, ts: t[r0:r1, :] if i < NT else ts[0:rows, :])
                if DBG_STAGE < 2:
                    return
                proj(0, O_DQ, 512)
                if DBG_STAGE < 3:
                    return
                qknorm(0, gq8[:], False, 0, None)
                if DBG_STAGE < 4:
                    return
                proj(1, O_DK, 512)
                qknorm(1, vect[:, V_GK:V_GK + 64], True, 0, pr(o_dk, s_dk))
                if DBG_STAGE < 5:
                    return
                proj(2, O_DV, 512)
                vproj(2, pr(o_dv, s_dv), True)
                if DBG_STAGE < 5.2:
                    return
                proj(3, O_FQ, 512)
                qknorm(3, fq8[:], False, 512, None)
                if DBG_STAGE < 5.4:
                    return
                proj(4, O_FK, 512)
                qknorm(4, vect[:, V_FK:V_FK + 64], True, 512, pr(o_fk, s_fk))
                if DBG_STAGE < 5.6:
                    return
                proj(5, O_FV, 512)
                vproj(5, pr(o_fv, s_fv), False)
                if DBG_STAGE < 6:
                    return
                proj(7, O_LF, 8)
                u = i % 2
                lg = lg_t[u]
                P.op("dve", lambda h, lg=lg: h.tensor_tensor(out=lg[:, 0:8], in0=ps[7][:, 0:8], in1=vect[:, V_FB:V_FB + 8], op=ALU.add), reads=[psb[7], b_vec], writes=[b_lg[u]])
                P.op("dve", lambda h, lg=lg: h.tensor_scalar(out=lg[:, 8:16], in0=lg[:, 0:8], scalar1=-1.0, scalar2=None, op0=ALU.mult), reads=[b_lg[u]], writes=[b_lg[u]])
                P.op("dve", lambda h, lg=lg: h.tensor_tensor(out=lg[:, 8:16], in0=lg[:, 0:8], in1=lg[:, 8:16], op=ALU.min), reads=[b_lg[u]], writes=[b_lg[u]])
                P.op("act", lambda h, lg=lg: h.activation(out=lg[:, 16:24], in_=lg[:, 8:16], func=AF.Exp), reads=[b_lg[u]], writes=[b_lg[u]])
                P.op("act", lambda h, lg=lg: h.activation(out=lg[:, 24:32], in_=lg[:, 16:24], func=AF.Ln, bias=1.0), reads=[b_lg[u]], writes=[b_lg[u]])
                P.op("dve", lambda h, lg=lg: h.tensor_scalar(out=lg[:, 32:40], in0=lg[:, 0:8], scalar1=0.0, scalar2=None, op0=ALU.min), reads=[b_lg[u]], writes=[b_lg[u]])
                P.op("dve", lambda h, lg=lg, i=i: h.tensor_tensor(out=LF[:, i, :], in0=lg[:, 32:40], in1=lg[:, 24:32], op=ALU.subtract), reads=[b_lg[u]], writes=[b_LF[i]])
                lf_dst = pr(o_lf, s_lf)
                P.dma(lambda h, i=i, rows=rows, lf_dst=lf_dst: h.dma_start(out=lf_dst, in_=LF[0:rows, i, :]), b_LF[i], reads=[b_LF[i]])
            stage1(DBG_TILES[0])
            for n_, i_ in enumerate(DBG_TILES):
                if n_ + 1 < len(DBG_TILES):
                    stage1(DBG_TILES[n_ + 1])
                stage2(i_)
            P.barrier()
            AR.release()

        qt_top = None
        if "B" in phases:
            AR.mark()
            PT = [AR.alloc([16, 512], BF16) for _ in range(2)]
            b_PT = [Buf("PT0"), Buf("PT1")]
            FB = AR.alloc([NT, NT * 8], F32)
            WC = AR.alloc([NT * 8], F32)
            TBt = AR.alloc([NT * 8], F32)
            CAR = AR.alloc([8], F32)
            o1s = AR.alloc([4, 128], F32)
            t2s = AR.alloc([4, 128], F32)
            ods = AR.alloc([4, 128], F32)
            sqs = AR.alloc([4, 128], F32)
            odb = [AR.alloc([4, 128], BF16) for _ in range(2)]
            ofs = [AR.alloc([4, 128], BF16) for _ in range(2)]
            rcs = [AR.alloc([16], F32) for _ in range(2)]
            b_c = Buf("cum")
            b_fb = Buf("fb")
            b_o1 = Buf("o1s")
            b_t2 = Buf("t2s")
            b_ods = Buf("ods")
            b_sqs = Buf("sqs")
            b_odb = [Buf("odb0"), Buf("odb1")]
            b_ofs = [Buf("ofs0"), Buf("ofs1")]
            b_rc = [Buf("rc0"), Buf("rc1")]
            LFv = LF[:, 0:NT, :].rearrange("p a b -> p (a b)")
            CCv = CC[:].rearrange("p a b -> p (a b)")
            CRv = CR[:].rearrange("p a b -> p (a b)")
            P.op("pe", MM(ps[0][:, 0:128], cstt[:, C_LTRI:C_LTRI + 128], LFv, True, True), reads=[b_cst] + b_LF[0:NT], writes=[psb[0]])
            P.op("pe", MM(ps[1][:, 0:128], cstt[:, C_ONES:C_ONES + 128], LFv, True, True), reads=[b_cst] + b_LF[0:NT], writes=[psb[1]])
            P.op("dve", CP(WC, ps[0][:, 0:128]), reads=[psb[0]], writes=[b_c])
            P.op("dve", CP(TBt, ps[1][:, 0:128]), reads=[psb[1]], writes=[b_c])
            P.op("dve", CP(CC[:, 0, :], WC[:, 0:8]), reads=[b_c], writes=[b_c])
            P.op("dve", MSET(CAR, 0.0), writes=[b_c])
            for i in range(1, NT):
                P.op("dve", TT(CAR, CAR, TBt[:, (i - 1) * 8:i * 8], ALU.add), reads=[b_c], writes=[b_c])
                P.op("dve", TT(CC[:, i, :], WC[:, i * 8:(i + 1) * 8], CAR, ALU.add), reads=[b_c], writes=[b_c])
            P.op("pe", MM(ps[0][:, 0:128], cstt[:, C_SEL64:C_SEL64 + 128], CCv, True, True), reads=[b_cst, b_c], writes=[psb[0]])
            P.op("dve", CP(CRv, ps[0][:, 0:128]), reads=[psb[0]], writes=[b_c])
            for kt in range(NT):
                n = NT - kt
                P.op("dve", TT(FB[:, kt, kt * 8:NT * 8].rearrange("p (a b) -> p a b", b=8), CR[:, kt:NT, :],
                               CC[:, kt:kt + 1, :].to_broadcast([128, n, 8]), ALU.subtract), reads=[b_c], writes=[b_fb])

            SB_ = (0, 1, 2)
            PO = ((3, 4), (5, 6))
            pT7 = ps[7][:].bitcast(BF16)
            cnt = 0
            scnt = 0
            ofc = 0
            odc = 0
            for B in range(4):
                nk = 4 * B + 4
                for mp in range(16):
                    is_d = mp < 8
                    if is_d:
                        h_, m_ = mp // 2, mp % 2
                        pair, half, vh = h_, m_, h_
                    else:
                        hf = mp - 8
                        pair, half, vh = 4 + hf // 2, hf % 2, hf
                    slot = cnt % 2
                    pset = PO[cnt % 2]
                    cnt += 1
                    p0 = half * 64
                    for kt in range(nk):
                        j0 = max(0, kt - 4 * B)
                        c0 = j0 * 128
                        sbk = SB_[scnt % 3]
                        scnt += 1
                        extra = (is_d and kt >= 4 * B - 1) or ((not is_d) and kt >= 4 * B)
                        P.op("pe", MM(ps[sbk][:, c0:512], KT[p0:p0 + 64, pair, kt * 128:(kt + 1) * 128],
                                      QT[p0:p0 + 64, pair, 512 * B + c0:512 * B + 512], True, not extra),
                             reads=[b_KT[pair][kt]] + [b_QT[pair][4 * B + j] for j in range(j0, 4)], writes=[psb[sbk]])
                        if is_d and extra:
                            if kt >= 4 * B:
                                a0 = c0
                                wd = min(256, 512 - a0)
                                rhs = TTb[:, h_, 0:wd]
                            else:
                                a0, wd = 0, 128
                                rhs = TTb[:, h_, 128:256]
                            P.op("pe", MM(ps[sbk][:, a0:a0 + wd], jb[:], rhs, False, True), reads=[b_misc], writes=[psb[sbk]])
                        if (not is_d) and extra:
                            P.op("pe", MM(ps[sbk][:, c0:c0 + 128], identb[:], mtrib[:], False, True), reads=[b_misc], writes=[psb[sbk]])
                        if is_d:
                            P.op("act", ACTV(PT[slot][:, kt, c0:512], ps[sbk][:, c0:512], AF.Exp, bias=vect[:, V_B31 + h_:V_B31 + h_ + 1]),
                                 reads=[psb[sbk], b_vec], writes=[b_PT[slot]])
                        else:
                            for j in range(j0, 4):
                                qt = 4 * B + j
                                P.op("act", ACTV(PT[slot][:, kt, j * 128:(j + 1) * 128], ps[sbk][:, j * 128:(j + 1) * 128], AF.Exp,
                                                 bias=FB[:, kt, qt * 8 + hf:qt * 8 + hf + 1]),
                                     reads=[psb[sbk], b_fb], writes=[b_PT[slot]])
                    Wc = 129 if is_d else 65
                    V4 = Vd4 if is_d else Vf4
                    for j in range(4):
                        qt = 4 * B + j
                        if is_d:
                            bank, off = pset[j // 2], (j % 2) * 130
                        else:
                            bank, off = pset[0], j * 66
                        for kt in range(qt + 1):
                            P.op("pe", MM(ps[bank][:, off:off + Wc], PT[slot][:, kt, j * 128:(j + 1) * 128], V4[:, kt, vh, 0:Wc], kt == 0, kt == qt),
                                 reads=[b_PT[slot], b_V[kt]], writes=[psb[bank]])
                    if not is_d:
                        bank = pset[0]
                        pv = ps[bank][:, 0:264].rearrange("p (j e) -> p j e", e=66)
                        u = ofc % 2
                        r = rcs[cnt % 2]
                        br = b_rc[cnt % 2]
                        P.op("dve", RCP(r[:, 0:4], pv[:, :, 64]), reads=[psb[bank]], writes=[br])
                        P.op("dve", TT(ofs[u][:, :, half * 64:(half + 1) * 64], pv[:, :, 0:64], r[:, 0:4].unsqueeze(2).to_broadcast([128, 4, 64]), ALU.mult),
                             reads=[psb[bank], br], writes=[b_ofs[u]])
                        if half == 1:
                            for j in range(4):
                                P.op("pe", TR(pT7[:, j * 128:(j + 1) * 128], ofs[u][:, j, :]), reads=[b_ofs[u], b_misc], writes=[psb[7]])
                            P.op("act", ACTV(QT[:, pair, 512 * B:512 * B + 512], pT7[:, 0:512], AF.Copy), reads=[psb[7]],
                                 writes=[b_QT[pair][4 * B + j] for j in range(4)])
                            ofc += 1
                    else:
                        r = rcs[cnt % 2]
                        br = b_rc[cnt % 2]
                        dst, bdst = (o1s, b_o1) if m_ == 0 else (t2s, b_t2)
                        for bi_, jj in ((0, 0), (1, 2)):
                            bank = pset[bi_]
                            pvx = ps[bank][:, 0:260].rearrange("p (j e) -> p j e", e=130)
                            P.op("dve", RCP(r[:, jj:jj + 2], pvx[:, :, 128]), reads=[psb[bank]], writes=[br])
                            P.op("dve", TT(dst[:, jj:jj + 2, :], pvx[:, :, 0:128], r[:, jj:jj + 2].unsqueeze(2).to_broadcast([128, 2, 128]), ALU.mult),
                                 reads=[psb[bank], br], writes=[bdst])
                        if m_ == 1:
                            fl = lambda a: a.rearrange("p a b -> p (a b)")
                            P.op("dve", lambda h: h.scalar_tensor_tensor(out=fl(ods), in0=fl(t2s), scalar=neglam, in1=fl(o1s), op0=ALU.mult, op1=ALU.add),
                                 reads=[b_t2, b_o1, b_misc], writes=[b_ods])
                            P.op("act", ACTV(fl(sqs), fl(ods), AF.Square), reads=[b_ods], writes=[b_sqs])
                            P.op("dve", RED(r[:, 4:8], sqs), reads=[b_sqs], writes=[br])
                            P.op("act", ACTV(r[:, 8:12], r[:, 4:8], AF.Sqrt, scale=1.0 / 128, bias=float(EPS)), reads=[br], writes=[br])
                            P.op("dve", RCP(r[:, 12:16], r[:, 8:12]), reads=[br], writes=[br])
                            P.op("dve", TT(ods, ods, r[:, 12:16].unsqueeze(2).to_broadcast([128, 4, 128]), ALU.mult), reads=[b_ods, br], writes=[b_ods])
                            u = odc % 2
                            P.op("dve", TT(odb[u], ods, sgs[:].unsqueeze(1).to_broadcast([128, 4, 128]), ALU.mult), reads=[b_ods, b_misc], writes=[b_odb[u]])
                            for j in range(4):
                                P.op("pe", TR(pT7[:, j * 128:(j + 1) * 128], odb[u][:, j, :]), reads=[b_odb[u], b_misc], writes=[psb[7]])
                            P.op("act", ACTV(QT[:, pair, 512 * B:512 * B + 512], pT7[:, 0:512], AF.Copy), reads=[psb[7]],
                                 writes=[b_QT[pair][4 * B + j] for j in range(4)])
                            odc += 1
            if DBG_OUT:
                b_dbg = Buf("dbg")
                P.dma(DMA(dbg_odt, QT.rearrange("p a b -> p (a b)")), b_dbg, reads=[b_QT[p_][i_] for p_ in range(8) for i_ in range(NT)])
            P.barrier()
            AR.release()
        ODT = QT
        AR.top = 8192

        if "S" in phases:
            AR.top = 8192
            saux = AR.alloc([1280], F32)
            pio = AR.alloc([2], F32)
            ptb = AR.alloc([NS * NPG], I32)
            idx = AR.alloc([NS * NPG], I32)
            selr = AR.alloc([NS, 128], BF16)
            selrf = [AR.alloc([128], F32) for _ in range(2)]
            BIASD = AR.alloc([NPG, 8], F32)
            pself = AR.alloc([32], F32)
            accd = AR.alloc([512], F32)
            accf = AR.alloc([512], F32)
            NSL = 3
            PGt = [AR.alloc([2056], F32) for _ in range(NSL)]
            b_PG = [Buf("PG%d" % i_) for i_ in range(NSL)]
            LFp = [AR.alloc([NPG * 8], F32) for _ in range(2)]
            Vdb = [AR.alloc([NPG, 512], BF16) for _ in range(2)]
            Vfb = [AR.alloc([NPG, 512], BF16) for _ in range(2)]
            prod = [AR.alloc([512], F32) for _ in range(2)]
            SC = [AR.alloc([2, NPG * 8], F32) for _ in range(2)]
            PR = [AR.alloc([2, NPG * 8], F32) for _ in range(2)]
            DEC = AR.alloc([NPG * 8], F32)
            SUF = AR.alloc([NPG * 8], F32)
            TBs = AR.alloc([NPG * 8], F32)
            sm = [AR.alloc([64], F32) for _ in range(2)]
            ADb = [AR.alloc([NPG + 1, 4], BF16) for _ in range(2)]
            AFb = [AR.alloc([NPG + 1, 8], BF16) for _ in range(2)]
            mdt = AR.alloc([512], F32)
            b_saux = Buf("saux")
            b_idx = Buf("idx")
            b_selr = Buf("selr")
            b_selrf = [Buf("selrf0"), Buf("selrf1")]
            b_biasd = Buf("biasd")
            b_pself = Buf("pself")
            b_acc = Buf("acc")
            b_Kd = [Buf("Kd0"), Buf("Kd1")]
            b_Kf = [Buf("Kf0"), Buf("Kf1")]
            b_Vd = [Buf("Vd0"), Buf("Vd1")]
            b_Vf = [Buf("Vf0"), Buf("Vf1")]
            b_LFp = [Buf("LFp0"), Buf("LFp1")]
            b_Vdb = [Buf("Vdb0"), Buf("Vdb1")]
            b_Vfb = [Buf("Vfb0"), Buf("Vfb1")]
            b_prod = [Buf("prod0"), Buf("prod1")]
            b_SC = [Buf("SC0"), Buf("SC1")]
            b_PR = [Buf("PR0"), Buf("PR1")]
            b_dec = Buf("dec")
            b_sm = [Buf("sm0"), Buf("sm1")]
            b_AD = [Buf("AD0"), Buf("AD1")]
            b_AF = [Buf("AF0"), Buf("AF1")]
            b_md = Buf("md")
            DM8 = saux[0:8, 0:512]
            DM4 = saux[0:4, 512:1024]
            ESv = saux[:, 1024:1280]
            P.dma(DMA(saux, cst[:, C_DM8:C_DM8 + 1280]), b_saux, writes=[b_saux])
            P.dma(DMA(pio[:, 0:2], cst[:, C_PIO:C_PIO + 2]), b_saux, writes=[b_saux])
            P.dma(DMA(ptb, ptab.partition_broadcast(128)), b_idx, writes=[b_idx])
            P.op("dve", lambda h: h.tensor_scalar(out=idx, in0=ptb, scalar1=128.0, scalar2=pio[:, 0:1], op0=ALU.mult, op1=ALU.add), reads=[b_idx, b_saux], writes=[b_idx])
            for s in range(NS):
                P.op("dve", CP(selr[:, s, :], cstt[:, C_ID + s:C_ID + s + 1].to_broadcast([128, 128])), reads=[b_cst], writes=[b_selr])
            P.op("pool", MSET(accd, 0.0), writes=[b_acc])
            P.op("pool", MSET(accf, 0.0), writes=[b_acc])
            B3 = BIASD.rearrange("p g (h m) -> p g h m", m=2)
            for m in range(2):
                P.op("dve", CP(B3[:, :, :, m], vect[:, V_B31:V_B31 + 4].unsqueeze(1).to_broadcast([128, NPG, 4])), reads=[b_vec], writes=[b_biasd])
            P.op("pe", MM(ps[4][:, 0:4], jb[:], TTb[:, :, 128], True, True), reads=[b_misc], writes=[psb[4]])
            for m in range(2):
                P.op("dve", TT(B3[:, NPG - 1, :, m], ps[4][:, 0:4], vect[:, V_B31:V_B31 + 4], ALU.add), reads=[psb[4], b_vec, b_biasd], writes=[b_biasd])
            for c0, dstc in ((0, 0), (512, 8)):
                P.op("dve", TT(prod[0], k16[:, c0:c0 + 512], q16[:, c0:c0 + 512], ALU.mult), reads=[b_16], writes=[b_prod[0]])
                P.op("dve", RED(pself[:, 16 + dstc:16 + dstc + 8], prod[0].rearrange("p (a b) -> p a b", b=64)), reads=[b_prod[0]], writes=[b_pself])
            P.op("dve", TT(pself[:, 16:24].rearrange("p (h m) -> p h m", m=2), pself[:, 16:24].rearrange("p (h m) -> p h m", m=2),
                           vect[:, V_B0:V_B0 + 4].unsqueeze(2).to_broadcast([128, 4, 2]), ALU.add), reads=[b_pself, b_vec], writes=[b_pself])
            P.op("act", ACTV(pself[:, 0:16], pself[:, 16:32], AF.Exp), reads=[b_pself], writes=[b_pself])

            gcn = 0

            def pages(s, gen):
                nonlocal gcn
                u = s % 2
                qd_bank, qf_bank = (0, 1) if u == 0 else (2, 3)
                P.op("pe", MM(ps[qd_bank][:, :], selr[:, s, :], q16[:, 0:512], True, True), reads=[b_selr, b_16], writes=[psb[qd_bank]])
                P.op("pe", MM(ps[qf_bank][:, :], selr[:, s, :], q16[:, 512:1024], True, True), reads=[b_selr, b_16], writes=[psb[qf_bank]])
                P.op("dve", CP(selrf[u], cstt[:, C_ID + s:C_ID + s + 1].to_broadcast([128, 128])), reads=[b_cst], writes=[b_selrf[u]])
                msk = cstt[:, C_ID + s:C_ID + s + 1]
                for pg in range(NPG):
                    j = s * NPG + pg
                    g = gcn % NSL
                    gcn += 1
                    off = bass.IndirectOffsetOnAxis(ap=idx[:, j:j + 1], axis=0)
                    P.dma(lambda h, dst=PGt[g], off=off: h.indirect_dma_start(out=dst, out_offset=None, in_=c_all, in_offset=off),
                          b_PG[g], reads=[b_idx], writes=[b_PG[g]], E="pool")
                    for col, qb_ in ((0, qd_bank), (1, qf_bank)):
                        P.op("dve", TT(prod[col], PGt[g][:, col * 512:(col + 1) * 512], ps[qb_][:, :], ALU.mult), reads=[b_PG[g], psb[qb_]], writes=[b_prod[col]])
                        P.op("dve", RED(SC[u][:, col, pg * 8:(pg + 1) * 8], prod[col].rearrange("p (a b) -> p a b", b=64)), reads=[b_prod[col]], writes=[b_SC[u]])
                    P.op("act", ACTV(Vdb[u][:, pg, :], PGt[g][:, 1024:1536], AF.Copy), reads=[b_PG[g]], writes=[b_Vdb[u]])
                    P.op("act", ACTV(Vfb[u][:, pg, :], PGt[g][:, 1536:2048], AF.Copy), reads=[b_PG[g]], writes=[b_Vfb[u]])
                    P.op("act", ACTV(LFp[u][:, pg * 8:(pg + 1) * 8], PGt[g][:, 2048:2056], AF.Copy), reads=[b_PG[g]], writes=[b_LFp[u]])
                    if gen is not None:
                        next(gen, None)
            def post(s):
                u = s % 2
                msk = cstt[:, C_ID + s:C_ID + s + 1]
                P.op("pe", MM(ps[4][:, 0:128], cstt[:, C_ONES:C_ONES + 128], LFp[u], True, True), reads=[b_cst, b_LFp[u]], writes=[psb[4]])
                P.op("dve", CP(TBs, ps[4][:, 0:128]), reads=[psb[4]], writes=[b_dec])
                P.op("pe", MM(ps[4][:, 0:8], selrf[u], LF[:, NT, :], True, True), reads=[b_selrf[u], b_LF[NT]], writes=[psb[4]])
                P.op("dve", CP(SUF[:, (NPG - 1) * 8:NPG * 8], ps[4][:, 0:8]), reads=[psb[4]], writes=[b_dec])
                yield
                for pg in range(NPG - 2, -1, -1):
                    P.op("dve", TT(SUF[:, pg * 8:(pg + 1) * 8], SUF[:, (pg + 1) * 8:(pg + 2) * 8], TBs[:, (pg + 1) * 8:(pg + 2) * 8], ALU.add), reads=[b_dec], writes=[b_dec])
                yield
                P.op("pe", MM(ps[4][:, 0:128], cstt[:, C_UTRI:C_UTRI + 128], LFp[u], True, True), reads=[b_cst, b_LFp[u]], writes=[psb[4]])
                P.op("dve", TT(DEC, ps[4][:, 0:128], SUF, ALU.add), reads=[psb[4], b_dec], writes=[b_dec])
                yield
                P.op("dve", TT(SC[u][:, 0, :], SC[u][:, 0, :], BIASD.rearrange("p g c -> p (g c)"), ALU.add), reads=[b_SC[u], b_biasd], writes=[b_SC[u]])
                P.op("dve", TT(SC[u][:, 1, :], SC[u][:, 1, :], DEC, ALU.add), reads=[b_SC[u], b_dec], writes=[b_SC[u]])
                P.op("act", ACTV(PR[u].rearrange("p a b -> p (a b)"), SC[u].rearrange("p a b -> p (a b)"), AF.Exp), reads=[b_SC[u]], writes=[b_PR[u]])
                yield
                t_ = sm[u]
                for a_ in range(2):
                    P.op("dve", RED(t_[:, a_ * 8:(a_ + 1) * 8], PR[u][:, a_, :].rearrange("p (g c) -> p c g", c=8)), reads=[b_PR[u]], writes=[b_sm[u]])
                P.op("dve", lambda h, t_=t_, msk=msk: h.scalar_tensor_tensor(out=t_[:, 0:16], in0=pself[:, 0:16], scalar=msk, in1=t_[:, 0:16], op0=ALU.mult, op1=ALU.add),
                     reads=[b_pself, b_cst, b_sm[u]], writes=[b_sm[u]])
                P.op("pe", MM(ps[4][:, 0:16], cstt[:, C_ONES:C_ONES + 128], t_[:, 0:16], True, True), reads=[b_cst, b_sm[u]], writes=[psb[4]])
                P.op("dve", RCP(t_[:, 16:32], ps[4][:, 0:16]), reads=[psb[4]], writes=[b_sm[u]])
                yield
                P.op("dve", lambda h, t_=t_, msk=msk: h.scalar_tensor_tensor(out=t_[:, 32:48], in0=pself[:, 0:16], scalar=msk, in1=t_[:, 16:32], op0=ALU.mult, op1=ALU.mult),
                     reads=[b_pself, b_cst, b_sm[u]], writes=[b_sm[u]])
                yield
                PRd = PR[u][:, 0, :].rearrange("p (g c) -> p g c", c=8)
                PRf = PR[u][:, 1, :].rearrange("p (g c) -> p g c", c=8)
                P.op("dve", TT(PRd, PRd, t_[:, 16:24].unsqueeze(1).to_broadcast([128, NPG, 8]), ALU.mult), reads=[b_PR[u], b_sm[u]], writes=[b_PR[u]])
                P.op("dve", TT(AFb[u][:, 0:NPG, :], PRf, t_[:, 24:32].unsqueeze(1).to_broadcast([128, NPG, 8]), ALU.mult), reads=[b_PR[u], b_sm[u]], writes=[b_AF[u]])
                P.op("dve", CP(AFb[u][:, NPG, :], t_[:, 40:48]), reads=[b_sm[u]], writes=[b_AF[u]])
                PR4 = PR[u][:, 0, :].rearrange("p (g h m) -> p g h m", h=4, m=2)
                P.op("dve", lambda h, u=u, PR4=PR4: h.scalar_tensor_tensor(out=ADb[u][:, 0:NPG, :], in0=PR4[:, :, :, 1], scalar=neglam, in1=PR4[:, :, :, 0], op0=ALU.mult, op1=ALU.add),
                     reads=[b_PR[u], b_misc], writes=[b_AD[u]])
                s4 = t_[:, 32:40].rearrange("p (h m) -> p h m", m=2)
                P.op("dve", lambda h, u=u, s4=s4: h.scalar_tensor_tensor(out=ADb[u][:, NPG, :], in0=s4[:, :, 1], scalar=neglam, in1=s4[:, :, 0], op0=ALU.mult, op1=ALU.add),
                     reads=[b_sm[u], b_misc], writes=[b_AD[u]])
                yield
                for pg in range(NPG + 1):
                    rd = Vdb[u][:, pg, :] if pg < NPG else v16[:, 0:512]
                    rf = Vfb[u][:, pg, :] if pg < NPG else v16[:, 512:1024]
                    P.op("pe", MM(ps[5][0:4, :], ADb[u][:, pg, :], rd, pg == 0, pg == NPG), reads=[b_AD[u], b_Vdb[u], b_16], writes=[psb[5]])
                    P.op("pe", MM(ps[6][0:8, :], AFb[u][:, pg, :], rf, pg == 0, pg == NPG), reads=[b_AF[u], b_Vfb[u], b_16], writes=[psb[6]])
                yield
                for bank, nh, DMm, acc in ((5, 4, DM4, accd), (6, 8, DM8, accf)):
                    P.op("dve", TT(mdt[0:nh, :], ps[bank][0:nh, :], DMm, ALU.mult), reads=[psb[bank], b_saux], writes=[b_md])
                    P.op("pe", MM(ps[7][0:16, :], ESv[0:nh, s * 16:(s + 1) * 16], mdt[0:nh, :], True, True), reads=[b_saux, b_md], writes=[psb[7]])
                    P.op("dve", TT(acc[0:16, :], ps[7][0:16, :], acc[0:16, :], ALU.add), reads=[psb[7], b_acc], writes=[b_acc])

            prev = None
            for s in range(NS):
                pages(s, prev)
                if prev is not None:
                    for _ in prev:
                        pass
                prev = post(s)
            for _ in prev:
                pass
            r = sm[0]
            a3 = accd.rearrange("p (a b) -> p a b", b=128)
            P.op("act", ACTV(prod[0], accd, AF.Square), reads=[b_acc], writes=[b_prod[0]])
            P.op("dve", RED(r[:, 4:8], prod[0].rearrange("p (a b) -> p a b", b=128)), reads=[b_prod[0]], writes=[b_sm[0]])
            P.op("act", ACTV(r[:, 8:12], r[:, 4:8], AF.Sqrt, scale=1.0 / 128, bias=float(EPS)), reads=[b_sm[0]], writes=[b_sm[0]])
            P.op("dve", RCP(r[:, 12:16], r[:, 8:12]), reads=[b_sm[0]], writes=[b_sm[0]])
            P.op("dve", TT(a3, a3, r[:, 12:16].unsqueeze(2).to_broadcast([128, 4, 128]), ALU.mult), reads=[b_acc, b_sm[0]], writes=[b_acc])
            ob = Vdb[0][:, 0, :]
            fb_ = Vdb[0][:, 1, :]
            P.op("dve", TT(ob.rearrange("p (a b) -> p a b", b=128), a3, sgs[:].unsqueeze(1).to_broadcast([128, 4, 128]), ALU.mult), reads=[b_acc, b_misc], writes=[b_Vdb[0]])
            P.op("dve", CP(fb_, accf), reads=[b_acc], writes=[b_Vdb[0]])
            pT7s = ps[7][:].bitcast(BF16)
            for c in range(8):
                src = ob[:, c * 128:(c + 1) * 128] if c < 4 else fb_[:, (c - 4) * 128:(c - 3) * 128]
                P.op("pe", TR(pT7s[:, c * 128:(c + 1) * 128], src), reads=[b_Vdb[0], b_misc], writes=[psb[7]])
            P.op("act", ACTV(odt16[:], pT7s.rearrange("p (a b) -> p a b", b=128), AF.Copy), reads=[psb[7]], writes=[b_odt16])
            P.barrier()
        AR.top = 8192

        lw_cnt = [0]

        def load_w(dst3, src, nch, c0, ncols, scale_t, wst, b_wst, b_dst):
            srcv = src.rearrange("(c p) n -> p c n", p=128)
            nsl = len(wst)
            for c in range(nch):
                for a in range(0, ncols, 1024):
                    w = min(1024, ncols - a)
                    k = lw_cnt[0]
                    lw_cnt[0] += 1
                    s_ = k % nsl
                    P.dma(DMA(wst[s_][:, 0:w], srcv[:, c, c0 + a:c0 + a + w]), b_wst[s_], writes=[b_wst[s_]])
                    eng = ("dve", "act")[k % 2]
                    if eng == "act":
                        if scale_t is not None:
                            P.op(eng, ACTV(dst3[:, c, a:a + w], wst[s_][:, 0:w], AF.Copy, scale=scale_t[:, c:c + 1]), reads=[b_wst[s_], b_vec], writes=[b_dst])
                        else:
                            P.op(eng, ACTV(dst3[:, c, a:a + w], wst[s_][:, 0:w], AF.Copy), reads=[b_wst[s_]], writes=[b_dst])
                    elif scale_t is not None:
                        P.op(eng, TS(dst3[:, c, a:a + w], wst[s_][:, 0:w], scale_t[:, c:c + 1], ALU.mult), reads=[b_wst[s_], b_vec], writes=[b_dst])
                    else:
                        P.op(eng, CP(dst3[:, c, a:a + w], wst[s_][:, 0:w]), reads=[b_wst[s_]], writes=[b_dst])

        def norm_T(xs, b_x, xn, b_xn_, st, b_st_, hT3, b_hT_, bank):
            P.op("act", ACTV(xn, xs, AF.Square, accum_out=st[:, 0:1]), reads=[b_x], writes=[b_xn_, b_st_])
            P.op("act", ACTV(st[:, 1:2], st[:, 0:1], AF.Sqrt, scale=1.0 / D, bias=float(EPS)), reads=[b_st_], writes=[b_st_])
            P.op("dve", RCP(st[:, 2:3], st[:, 1:2]), reads=[b_st_], writes=[b_st_])
            P.op("dve", TS(xn, xs, st[:, 2:3], ALU.mult), reads=[b_x, b_st_], writes=[b_xn_])
            pT = ps[bank][:].bitcast(BF16)
            for kc in range(8):
                P.op("pe", TR(pT[:, kc * 128:(kc + 1) * 128], xn[:, kc * 128:(kc + 1) * 128]), reads=[b_xn_, b_misc], writes=[psb[bank]])
            P.op("act", ACTV(hT3, pT.rearrange("p (a b) -> p a b", b=128), AF.Copy), reads=[psb[bank]], writes=[b_hT_])

        if "C" in phases:
            AR.mark()
            Wg = AR.alloc([8, 2048], BF16)
            Wa = AR.alloc([4, 1024], BF16)
            Wb = AR.alloc([4, 1024], BF16)
            Wo = AR.alloc([8, 1024], BF16)
            gbf = AR.alloc([2048], F32)
            gbb = AR.alloc([2048], BF16)
            wst_c = [AR.alloc([1024], F32) for _ in range(4)]
            b_wst_c = [Buf("cwst%d" % i_) for i_ in range(4)]
            b_Wc = Buf("Wc")
            b_gb = Buf("gb")
            load_w(Wg, w_in, 8, O_G, 2048, g1t, wst_c, b_wst_c, b_Wc)
            load_w(Wa, w_a, 4, 0, 1024, None, wst_c, b_wst_c, b_Wc)
            load_w(Wb, w_b, 4, 0, 1024, None, wst_c, b_wst_c, b_Wc)
            load_w(Wo, w_out, 8, 0, 1024, None, wst_c, b_wst_c, b_Wc)
            P.op("pool", MSET(gbf, 0.0), writes=[b_gb])
            P.dma(DMA(gbf[64:65, :], gate_b), b_gb, writes=[b_gb])
            P.op("dve", CP(gbb, gbf), reads=[b_gb], writes=[b_Wc])
            xs_t_c = [AR.alloc([D], F32) for _ in range(2)]
            xn_t_c = [AR.alloc([D], BF16) for _ in range(2)]
            hT_t_c = [AR.alloc([8, 128], BF16) for _ in range(2)]
            st4_c = [AR.alloc([4], F32) for _ in range(2)]
            G = AR.alloc([2048], F32)
            t0 = AR.alloc([D], F32)
            t1 = AR.alloc([D], F32)
            mb = AR.alloc([D], BF16)
            mT = AR.alloc([8, 128], BF16)
            x1t = [AR.alloc([D], F32) for _ in range(2)]
            b_xs_c = [Buf("cxs0"), Buf("cxs1")]
            b_xn_c = [Buf("cxn0"), Buf("cxn1")]
            b_hT_c = [Buf("chT0"), Buf("chT1")]
            b_st4_c = [Buf("cst40"), Buf("cst41")]
            b_G = Buf("G")
            b_t0 = Buf("t0")
            b_t1 = Buf("t1")
            b_mb = Buf("mb")
            b_mT = Buf("mT")
            b_x1 = [Buf("x1t0"), Buf("x1t1")]
            def c_stage1(i):
                s_ = i % 2
                if i == NT:
                    P.op("pool", MSET(xs_t_c[s_], 0.0), writes=[b_xs_c[s_]])
                    P.dma(DMA(xs_t_c[s_][0:NS, :], xsm), b_xs_c[s_], writes=[b_xs_c[s_]])
                else:
                    P.dma(DMA(xs_t_c[s_], xp[i * 128:(i + 1) * 128, :]), b_xs_c[s_], writes=[b_xs_c[s_]])
                norm_T(xs_t_c[s_], b_xs_c[s_], xn_t_c[s_], b_xn_c[s_], st4_c[s_], b_st4_c[s_], hT_t_c[s_], b_hT_c[s_], 7)

            def c_stage2(i):
                s_ = i % 2
                for blk in range(4):
                    for kc in range(8):
                        P.op("pe", MM(ps[blk][:, :], hT_t_c[s_][:, kc, :], Wg[:, kc, blk * 512:(blk + 1) * 512], kc == 0, False),
                             reads=[b_hT_c[s_], b_Wc], writes=[psb[blk]])
                    P.op("pe", MM(ps[blk][:, :], sel64b[:], gbb[:, blk * 512:(blk + 1) * 512], False, True), reads=[b_misc, b_Wc], writes=[psb[blk]])
                    P.op("act", ACTV(G[:, blk * 512:(blk + 1) * 512], ps[blk][:, :], AF.Sigmoid), reads=[psb[blk]], writes=[b_G])
                for blk in range(2):
                    for c in range(4):
                        if i < NT:
                            la, lb = ODT[:, c, i * 128:(i + 1) * 128], ODT[:, 4 + c, i * 128:(i + 1) * 128]
                            ra, rb_ = [b_QT[c][i]], [b_QT[4 + c][i]]
                        else:
                            la, lb = odt16[:, c, :], odt16[:, 4 + c, :]
                            ra = rb_ = [b_odt16]
                        P.op("pe", MM(ps[4 + blk][:, :], la, Wa[:, c, blk * 512:(blk + 1) * 512], c == 0, c == 3), reads=ra + [b_Wc], writes=[psb[4 + blk]])
                        P.op("pe", MM(ps[6 + blk][:, :], lb, Wb[:, c, blk * 512:(blk + 1) * 512], c == 0, c == 3), reads=rb_ + [b_Wc], writes=[psb[6 + blk]])
                    P.op("dve", TT(t0[:, blk * 512:(blk + 1) * 512], ps[4 + blk][:, :], G[:, blk * 512:(blk + 1) * 512], ALU.mult), reads=[psb[4 + blk], b_G], writes=[b_t0])
                    P.op("dve", TT(t1[:, blk * 512:(blk + 1) * 512], ps[6 + blk][:, :], G[:, 1024 + blk * 512:1024 + (blk + 1) * 512], ALU.mult), reads=[psb[6 + blk], b_G], writes=[b_t1])
                P.op("pool", TT(mb, t0, t1, ALU.add), reads=[b_t0, b_t1], writes=[b_mb])
                pT0 = ps[0][:].bitcast(BF16)
                for kc in range(8):
                    P.op("pe", TR(pT0[:, kc * 128:(kc + 1) * 128], mb[:, kc * 128:(kc + 1) * 128]), reads=[b_mb, b_misc], writes=[psb[0]])
                P.op("act", ACTV(mT, pT0.rearrange("p (a b) -> p a b", b=128), AF.Copy), reads=[psb[0]], writes=[b_mT])
                for blk in range(2):
                    for kc in range(8):
                        P.op("pe", MM(ps[1 + blk][:, :], mT[:, kc, :], Wo[:, kc, blk * 512:(blk + 1) * 512], kc == 0, kc == 7), reads=[b_mT, b_Wc], writes=[psb[1 + blk]])
                    P.op("dve", TT(x1t[s_][:, blk * 512:(blk + 1) * 512], ps[1 + blk][:, :], xs_t_c[s_][:, blk * 512:(blk + 1) * 512], ALU.add),
                         reads=[psb[1 + blk], b_xs_c[s_]], writes=[b_x1[s_]])
                P.dma(DMA(x1s[i * 128:(i + 1) * 128, :], x1t[s_]), b_x1[s_], reads=[b_x1[s_]], writes=[b_x1s[i]])
            c_stage1(0)
            for i in range(NT + 1):
                if i + 1 <= NT:
                    c_stage1(i + 1)
                c_stage2(i)
            P.barrier()
            AR.release()

        if "D" in phases:
            AR.top = 0
            Wgu = AR.alloc([8, 2 * DFF], BF16)
            Wdn = AR.alloc([NFC, 1024], BF16)
            wst_d = [AR.alloc([1024], F32) for _ in range(2)]
            b_wst_d = [Buf("dwst%d" % i_) for i_ in range(5)]
            b_Wd = Buf("Wd")
            top_d = AR.top
            wst_d = wst_d + [AR.alloc([1024], F32) for _ in range(3)]
            load_w(Wgu, w_gu, 8, 0, 2 * DFF, g2t, wst_d, b_wst_d, b_Wd)
            load_w(Wdn, w_dn, NFC, 0, 1024, None, wst_d, b_wst_d, b_Wd)
            P.barrier()
            AR.top = top_d
            x1b = [[AR.alloc([D], F32) for _ in range(2)] for _ in range(2)]
            b_x1b = [[Buf("x1b%d%d" % (a, b)) for b in range(2)] for a in range(2)]
            xn_t_d = [AR.alloc([D], BF16) for _ in range(2)]
            b_xn_d = [Buf("dxn0"), Buf("dxn1")]
            st4_d = [AR.alloc([4], F32) for _ in range(2)]
            b_st4_d = [Buf("dst40"), Buf("dst41")]
            h2T = [AR.alloc([8, 256], BF16) for _ in range(2)]
            b_h2 = [[Buf("h2T%d%d" % (a, b)) for b in range(2)] for a in range(2)]
            actT = AR.alloc([NFC, 256], BF16)
            b_act = Buf("actT")
            sgt = [AR.alloc([256], F32) for _ in range(2)]
            b_sg = [Buf("sg0"), Buf("sg1")]
            blocks = [[2 * b, 2 * b + 1] for b in range(8)] + [[NT]]
            gcnt = 0
            ncn = 0
            for bi, T in enumerate(blocks):
                sl = bi % 2
                for ti, i in enumerate(T):
                    xt = x1b[sl][ti]
                    P.dma(DMA(xt, x1s[i * 128:(i + 1) * 128, :]), b_x1b[sl][ti], reads=[b_x1s[i]], writes=[b_x1b[sl][ti]])
                    u = ncn % 2
                    ncn += 1
                    norm_T(xt, b_x1b[sl][ti], xn_t_d[u], b_xn_d[u], st4_d[u], b_st4_d[u], h2T[sl][:, :, ti * 128:(ti + 1) * 128], b_h2[sl][ti], 7)
                ntok = 128 * len(T)
                rh = [b_h2[sl][ti] for ti in range(len(T))]
                for fc in range(NFC):
                    a, b = ((0, 1), (2, 3), (4, 5))[gcnt % 3]
                    u = gcnt % 2
                    gcnt += 1
                    for kc in range(8):
                        P.op("pe", MM(ps[a][:, 0:ntok], Wgu[:, kc, fc * 128:(fc + 1) * 128], h2T[sl][:, kc, 0:ntok], kc == 0, kc == 7), reads=rh + [b_Wd], writes=[psb[a]])
                    for kc in range(8):
                        P.op("pe", MM(ps[b][:, 0:ntok], Wgu[:, kc, DFF + fc * 128:DFF + (fc + 1) * 128], h2T[sl][:, kc, 0:ntok], kc == 0, kc == 7), reads=rh + [b_Wd], writes=[psb[b]])
                    P.op("act", ACTV(sgt[u][:, 0:ntok], ps[a][:, 0:ntok], AF.Silu), reads=[psb[a]], writes=[b_sg[u]])
                    P.op("dve", TT(actT[:, fc, 0:ntok], ps[b][:, 0:ntok], sgt[u][:, 0:ntok], ALU.mult), reads=[psb[b], b_sg[u]], writes=[b_act])
                for ti, i in enumerate(T):
                    xt = x1b[sl][ti]
                    for blk in range(2):
                        bank = 6 + blk
                        for fc in range(NFC):
                            P.op("pe", MM(ps[bank][:, :], actT[:, fc, ti * 128:(ti + 1) * 128], Wdn[:, fc, blk * 512:(blk + 1) * 512], fc == 0, fc == NFC - 1),
                                 reads=[b_act, b_Wd], writes=[psb[bank]])
                        P.op("dve", TT(xt[:, blk * 512:(blk + 1) * 512], ps[bank][:, :], xt[:, blk * 512:(blk + 1) * 512], ALU.add),
                             reads=[psb[bank], b_x1b[sl][ti]], writes=[b_x1b[sl][ti]])
                    if i < NT:
                        P.dma(DMA(y_p[i * 128:(i + 1) * 128, :], xt), b_x1b[sl][ti], reads=[b_x1b[sl][ti]])
                    else:
                        P.dma(DMA(y_s, xt[0:NS, :]), b_x1b[sl][ti], reads=[b_x1b[sl][ti]])
            P.barrier()

        P.finish()
        P.emit()
    return nc, P


def _host_inputs(inp):
    f = lambda a: np.ascontiguousarray(np.asarray(a, dtype=np.float32))
    common = {
        "w_in": f(inp["w_in"][0]), "w_a": f(inp["w_branch_a"][0]), "w_b": f(inp["w_branch_b"][0]),
        "w_out": f(inp["w_out"][0]), "w_gu": f(inp["w_gate_up"][0]), "w_dn": f(inp["w_down"][0]),
        "cst": make_consts(),
        "g1T": f(np.asarray(inp["norm1_g"])[0].reshape(8, 128).T),
        "g2T": f(np.asarray(inp["norm2_g"])[0].reshape(8, 128).T),
        "vec": f(np.concatenate([np.asarray(inp["diff_q_g"])[0], np.asarray(inp["diff_k_g"])[0], np.asarray(inp["fox_q_g"])[0],
                                 np.asarray(inp["fox_k_g"])[0], np.asarray(inp["diff_subln_g"])[0], np.asarray(inp["fox_f_b"])[0],
                                 np.asarray(inp["rel_bias"])[31], np.asarray(inp["rel_bias"])[0],
                                 np.asarray(inp["diff_lambda"])[0].reshape(-1)])[None, :]),
        "gate_b": f(np.asarray(inp["gate_b"])[0].reshape(1, -1)),
        "rel_bias": f(inp["rel_bias"]),
    }
    return common


OUT_NAMES = ["y_p", "y_s", "o_dk", "o_dv", "o_fk", "o_fv", "o_lf", "s_dk", "s_dv", "s_fk", "s_fv", "s_lf"]


def run(inp, phases="ABCDS"):
    nc, P = build(phases)
    common = _host_inputs(inp)
    if "C" not in phases:
        for k in ("w_a", "w_b", "w_out", "gate_b"):
            common.pop(k)
    if "D" not in phases:
        for k in ("w_gu", "w_dn"):
            common.pop(k)
    xpr = np.asarray(inp["x_prompt"], dtype=np.float32)
    xsa = np.asarray(inp["x_sample"], dtype=np.float32)
    in_maps = []
    if "S" in phases:
        r2 = lambda k, w: np.asarray(inp[k], dtype=np.float32).reshape(NPHYS * 128, w)
        c_all_host = np.concatenate([r2("cache_diff_k", 512), r2("cache_fox_k", 512), r2("cache_diff_v", 512),
                                     r2("cache_fox_v", 512), r2("cache_fox_logf", 8)], axis=1)
    for c in range(NCORES):
        m = dict(common)
        m["xp"] = np.ascontiguousarray(xpr[c])
        m["xs"] = np.ascontiguousarray(xsa[c * NS:(c + 1) * NS, 0])
        if "S" in phases:
            m["c_all"] = c_all_host
            m["ptab"] = np.ascontiguousarray(np.asarray(inp["page_table"], dtype=np.int32)[c * NS:(c + 1) * NS].reshape(1, -1))
        in_maps.append(m)
    res = run_bass_kernel_spmd(nc, in_maps, core_ids=list(range(NCORES)))
    R = res.results
    if DBG_OUT:
        global DBG_RES
        DBG_RES = R
    cat = lambda k: np.concatenate([np.asarray(r[k]) for r in R], axis=0)
    yp = cat("y_p").reshape(8, SEQ, D)
    ys = cat("y_s").reshape(128, 1, D)
    outs = [yp, ys,
            cat("o_dk").reshape(1, 8, SEQ, 4, 2, 64), cat("o_dv").reshape(1, 8, SEQ, 4, 128),
            cat("o_fk").reshape(1, 8, SEQ, 8, 64), cat("o_fv").reshape(1, 8, SEQ, 8, 64), cat("o_lf").reshape(1, 8, SEQ, 8),
            cat("s_dk").reshape(1, 128, 1, 4, 2, 64), cat("s_dv").reshape(1, 128, 1, 4, 128),
            cat("s_fk").reshape(1, 128, 1, 8, 64), cat("s_fv").reshape(1, 128, 1, 8, 64), cat("s_lf").reshape(1, 128, 1, 8)]
    return tuple(np.ascontiguousarray(o, dtype=np.float32) for o in outs)


def kernel(**inputs):
    return run(inputs, "ABCDS")
```

```python
import math
import numpy as np
import concourse.bass as bass
import concourse.mybir as mybir
from concourse.bass_utils import run_bass_kernel_spmd
from contextlib import ExitStack

F32 = mybir.dt.float32
BF16 = mybir.dt.bfloat16
I32 = mybir.dt.int32
AF = mybir.ActivationFunctionType
ALU = mybir.AluOpType
AX = mybir.AxisListType

NCORES = 8
D = 1024
SEQ = 2048
NT = 16
NS = 16
HD = 64
QKVW = 3080
INC = 5128
DFF = 2816
NFC = 22
NPHYS = 2560
NPG = 16
EPS = 1e-6
LAMBDA_INIT = 0.8 - 0.6 * math.exp(-0.3 * 0)
NEG = -30000.0
DBG_TILES = list(range(NT + 1))
DBG_STAGE = 99
DBG_OUT = False

O_DQ, O_DK, O_DV, O_FQ, O_FK, O_FV, O_LF, O_G = 0, 512, 1024, 1536, 2048, 2560, 3072, 3080

C_ID, C_LTRI, C_ONES, C_SEL64, C_UTRI, C_MTRI = 0, 128, 256, 384, 512, 640
C_J = 768
C_MAIN = 896
C_DM8, C_DM4, C_ES, C_OH, C_MROW = 896, 1408, 1920, 2176, 2560
C_PIO = 2560
C_N = 2944


def _rel_bucket(n):
    n = max(n, 0)
    if n < 16:
        return n
    large = 16 + int(np.float32(np.log(np.float32(max(n, 1)) / np.float32(16)) / np.float32(math.log(128 / 16)) * np.float32(16)))
    return min(large, 31)


def make_consts():
    c = np.zeros((128, C_N), np.float32)
    i = np.arange(128)
    c[:, C_ID:C_ID + 128] = np.eye(128)
    c[:, C_LTRI:C_LTRI + 128] = (i[:, None] <= i[None, :])
    c[:, C_ONES:C_ONES + 128] = 1.0
    c[64, C_SEL64:C_SEL64 + 128] = 1.0
    c[:, C_UTRI:C_UTRI + 128] = (i[:, None] > i[None, :])
    c[:, C_MTRI:C_MTRI + 128] = np.where(i[:, None] > i[None, :], NEG, 0.0)
    c[:, C_J:C_J + 128] = np.eye(128)[::-1]
    c[:, C_PIO] = np.arange(128)
    for h in range(8):
        c[h, C_DM8 + h * 64:C_DM8 + (h + 1) * 64] = 1.0
    for h in range(4):
        c[h, C_DM4 + h * 128:C_DM4 + (h + 1) * 128] = 1.0
    for s in range(16):
        c[0:8, C_ES + s * 16 + s] = 1.0
    for idx in range(383):
        n = idx - 127
        if n >= 0:
            b = _rel_bucket(n)
            c[b, C_OH + idx] += 1.0
            c[31, C_OH + idx] -= 1.0
        else:
            c[32, C_OH + idx] = NEG
    return c


class Buf:
    __slots__ = ("name", "w", "readers", "dsem", "dtotal")

    def __init__(self, name):
        self.name = name
        self.w = None
        self.readers = []
        self.dsem = None
        self.dtotal = 0


class Prog:
    ENGS = ("pe", "act", "dve", "pool", "sp")

    def __init__(self, nc, stack):
        self.nc = nc
        self.stack = stack
        self.q = {e: [] for e in self.ENGS}
        self.esem = {e: stack.enter_context(nc.semaphore("es_" + e)) for e in self.ENGS}
        self.ecnt = {e: 0 for e in self.ENGS}
        self.waited = {e: {} for e in self.ENGS}
        self.dsems = []
        self.nd = 0
        self.phase_sem = stack.enter_context(nc.semaphore("phase"))
        self.nphase = 0
        self.ninstr = 0

    def _dsem(self, b):
        if b.dsem is None:
            b.dsem = self.stack.enter_context(self.nc.semaphore("ds%d" % self.nd))
            self.nd += 1
            self.dsems.append(b)
        return b.dsem

    def _wait(self, E, tok):
        if tok is None:
            return
        if tok[0] == "e":
            name, val = tok[1], tok[2]
            if name == E and E == "pe":
                return
            key = ("e", name)
            sem = self.esem[name]
        else:
            sb = tok[1]
            key = ("d", id(sb))
            sem = sb.dsem
            val = sb.dtotal
        if self.waited[E].get(key, 0) >= val:
            return
        self.waited[E][key] = val
        self.q[E].append(lambda h, sem=sem, val=val: h.wait_ge(sem, val))

    def _deps(self, E, reads, writes):
        for b in reads:
            self._wait(E, b.w)
        for b in writes:
            if not (b.w is not None and b.w[0] == "e" and b.w[1] == E):
                self._wait(E, b.w)
            for r in b.readers:
                if r[0] == "e" and r[1] == E:
                    continue
                self._wait(E, r)

    def op(self, E, fn, reads=(), writes=()):
        self._deps(E, reads, writes)
        self.ecnt[E] += 1
        tok = ("e", E, self.ecnt[E])
        sem = self.esem[E]
        self.q[E].append(lambda h, fn=fn, sem=sem: fn(h).then_inc(sem, 1))
        for b in reads:
            b.readers.append(tok)
        for b in writes:
            b.w = tok
            b.readers = []
        self.ninstr += 1

    def dma(self, fn, sembuf, reads=(), writes=(), E="sp"):
        self._deps(E, reads, writes)
        sem = self._dsem(sembuf)
        sembuf.dtotal += 16
        tok = ("d", sembuf)
        self.q[E].append(lambda h, fn=fn, sem=sem: fn(h).then_inc(sem, 16))
        for b in reads:
            b.readers.append(tok)
        for b in writes:
            b.w = tok
            b.readers = []
        self.ninstr += 1

    def barrier(self):
        sp = "sp"
        for e in self.ENGS:
            if e != sp and self.ecnt[e] > 0:
                self._wait(sp, ("e", e, self.ecnt[e]))
        for b in self.dsems:
            self._wait(sp, ("d", b))
        self.nphase += 1
        n = self.nphase
        ps = self.phase_sem
        self.q[sp].append(lambda h: h.sem_inc(ps, 1))
        for e in self.ENGS:
            if e != sp:
                self.q[e].append(lambda h, n=n: h.wait_ge(ps, n))
        for e in self.ENGS:
            for e2 in self.ENGS:
                self.waited[e][("e", e2)] = self.ecnt[e2]
            for b in self.dsems:
                self.waited[e][("d", id(b))] = b.dtotal

    def finish(self):
        sp = "sp"
        for e in self.ENGS:
            if e != sp and self.ecnt[e] > 0:
                self._wait(sp, ("e", e, self.ecnt[e]))
        for b in self.dsems:
            self._wait(sp, ("d", b))

    def emit(self):
        with self.nc.Block() as block:
            @block.tensor
            def _(h):
                for f in self.q["pe"]:
                    f(h)

            @block.scalar
            def _(h):
                for f in self.q["act"]:
                    f(h)

            @block.vector
            def _(h):
                for f in self.q["dve"]:
                    f(h)

            @block.gpsimd
            def _(h):
                for f in self.q["pool"]:
                    f(h)

            @block.sync
            def _(h):
                for f in self.q["sp"]:
                    f(h)


class Arena:
    def __init__(self, t, words):
        self.t = t
        self.words = words
        self.top = 0
        self.marks = []

    def mark(self):
        self.marks.append(self.top)

    def release(self):
        self.top = self.marks.pop()

    def alloc(self, shape, dtype):
        n = int(np.prod(shape))
        w = n if dtype in (F32, I32) else (n + 1) // 2
        w = (w + 1) // 2 * 2
        assert self.top + w <= self.words, ("arena overflow", self.top, w, self.words)
        v = self.t[:, self.top:self.top + w]
        self.top += w
        if dtype not in (F32,):
            v = v.bitcast(dtype)
        v = v[:, 0:n]
        if len(shape) == 2:
            return v.rearrange("p (a b) -> p a b", b=shape[1])
        if len(shape) == 3:
            return v.rearrange("p (a b c) -> p a b c", b=shape[1], c=shape[2])
        return v


def build(phases="ABCDS"):
    nc = bass.Bass("TRN2", target_bir_lowering=False)
    dt = nc.dram_tensor

    def din(name, shape, dtype=F32):
        return dt(name, list(shape), dtype, kind="ExternalInput").ap()

    def dout(name, shape):
        return dt(name, list(shape), F32, kind="ExternalOutput").ap()

    xp = din("xp", [SEQ, D])
    xsm = din("xs", [NS, D])
    w_in = din("w_in", [D, INC])
    if "C" in phases:
        w_a = din("w_a", [512, D])
        w_b = din("w_b", [512, D])
        w_out = din("w_out", [D, D])
        gate_b = din("gate_b", [1, 2 * D])
    if "D" in phases:
        w_gu = din("w_gu", [D, 2 * DFF])
        w_dn = din("w_dn", [DFF, D])
    cst = din("cst", [128, C_N])
    g1T = din("g1T", [128, 8])
    g2T = din("g2T", [128, 8])
    V_GQ, V_GK, V_FQ, V_FK, V_SG, V_FB, V_B31, V_B0, V_LAM, V_N = 0, 64, 128, 192, 256, 384, 392, 396, 400, 656
    vec = din("vec", [1, V_N])
    rel_bias = din("rel_bias", [32, 4])
    has_S = "S" in phases
    if has_S:
        c_all = din("c_all", [NPHYS * 128, 2056])
        ptab = din("ptab", [1, NS * NPG], I32)

    y_p = dout("y_p", [SEQ, D])
    y_s = dout("y_s", [NS, D])
    o_dk = dout("o_dk", [SEQ, 512])
    o_dv = dout("o_dv", [SEQ, 512])
    o_fk = dout("o_fk", [SEQ, 512])
    o_fv = dout("o_fv", [SEQ, 512])
    o_lf = dout("o_lf", [SEQ, 8])
    s_dk = dout("s_dk", [NS, 512])
    s_dv = dout("s_dv", [NS, 512])
    s_fk = dout("s_fk", [NS, 512])
    s_fv = dout("s_fv", [NS, 512])
    s_lf = dout("s_lf", [NS, 8])
    gtab_h = dt("gtab", [4, 384], F32, kind=("ExternalOutput" if DBG_OUT else "Internal"))
    gtab = gtab_h.ap()
    x1s = dt("x1s", [SEQ + 128, D], F32, kind=("ExternalOutput" if DBG_OUT else "Internal")).ap()
    if DBG_OUT:
        dbg_odt = dt("dbg_odt", [128, 8 * SEQ], BF16, kind="ExternalOutput").ap()

    stack = ExitStack()
    with stack:
        P = Prog(nc, stack)
        sb = lambda name, shape, dtype=F32: stack.enter_context(nc.sbuf_tensor(name, list(shape), dtype))
        cstt = sb("cstt", [128, C_MAIN])
        vect = sb("vect", [128, V_N])
        g1t = sb("g1t", [128, 8])
        g2t = sb("g2t", [128, 8])
        identb = sb("identb", [128, 128], BF16)
        mtrib = sb("mtrib", [128, 128], BF16)
        jb = sb("jb", [128, 128], BF16)
        gq8 = sb("gq8", [128, 64])
        fq8 = sb("fq8", [128, 64])
        sgs = sb("sgs", [128, 128])
        lamt = sb("lamt", [128, 8])
        TTb = sb("TTb", [128, 4, 256], BF16)
        LF = sb("LF", [128, NT + 1, 8])
        CC = sb("CC", [128, NT, 8])
        CR = sb("CR", [128, NT, 8])
        small = sb("small", [128, 64])
        onesb = sb("onesb", [128, 128], BF16)
        sel64b = sb("sel64b", [128, 128], BF16)
        odt16 = sb("odt16", [128, 8, 128], BF16)
        q16 = sb("q16", [128, 1024], BF16)
        k16 = sb("k16", [128, 1024])
        v16 = sb("v16", [128, 1024], BF16)
        b_16 = Buf("t16")
        b_odt16 = Buf("odt16")
        b_x1s = [Buf("x1s%d" % i) for i in range(NT + 1)]

        def MM(out, lhsT, rhs, start, stop):
            return lambda h: h.matmul(out, lhsT, rhs, start=start, stop=stop)

        def TR(out, in_):
            return lambda h: h.transpose(out=out, in_=in_, identity=identb[:])

        def ACTV(out, in_, func, **kw):
            return lambda h: h.activation(out=out, in_=in_, func=func, **kw)

        def TT(out, in0, in1, op):
            return lambda h: h.tensor_tensor(out=out, in0=in0, in1=in1, op=op)

        def TS(out, in0, s1, op0):
            return lambda h: h.tensor_scalar(out=out, in0=in0, scalar1=s1, scalar2=None, op0=op0)

        def CP(out, in_):
            return lambda h: h.tensor_copy(out=out, in_=in_)

        def RCP(out, in_):
            return lambda h: h.reciprocal(out=out, in_=in_)

        def RED(out, in_):
            return lambda h: h.tensor_reduce(out=out, in_=in_, axis=AX.X, op=ALU.add)

        def DMA(out, in_):
            return lambda h: h.dma_start(out=out, in_=in_)

        def MSET(ap, v):
            return lambda h: h.memset(ap, v)
        ARW = 47500
        art = sb("arena", [128, ARW])
        AR = Arena(art, ARW)
        ps = [stack.enter_context(nc.psum_tensor("ps%d" % i, [128, 512], F32)) for i in range(8)]
        psb = [P_buf for P_buf in [Buf("ps%d" % i) for i in range(8)]]

        ident_f = cstt[:, C_ID:C_ID + 128]
        b_cst = Buf("cst")
        b_vec = Buf("vec")
        b_misc = Buf("misc")

        P.dma(lambda h: h.dma_start(out=cstt[:], in_=cst[:, 0:C_MAIN]), b_cst, writes=[b_cst])
        AR.mark()
        caux = AR.alloc([384], F32)
        b_caux = Buf("caux")
        P.dma(lambda h: h.dma_start(out=caux, in_=cst[:, C_OH:C_OH + 384]), b_caux, writes=[b_caux])
        P.dma(lambda h: h.dma_start(out=vect[:], in_=vec.partition_broadcast(128)), b_vec, writes=[b_vec])
        P.dma(lambda h: h.dma_start(out=g1t[:], in_=g1T), b_vec, writes=[b_vec])
        P.dma(lambda h: h.dma_start(out=g2t[:], in_=g2T), b_vec, writes=[b_vec])
        P.op("dve", lambda h: h.tensor_copy(out=identb[:], in_=cstt[:, C_ID:C_ID + 128]), reads=[b_cst], writes=[b_misc])
        P.op("dve", lambda h: h.tensor_copy(out=mtrib[:], in_=cstt[:, C_MTRI:C_MTRI + 128]), reads=[b_cst], writes=[b_misc])
        P.op("dve", lambda h: h.tensor_copy(out=jb[:], in_=cstt[:, C_J:C_J + 128]), reads=[b_cst], writes=[b_misc])
        P.op("dve", CP(onesb[:], cstt[:, C_ONES:C_ONES + 128]), reads=[b_cst], writes=[b_misc])
        P.op("dve", CP(sel64b[:], cstt[:, C_SEL64:C_SEL64 + 128]), reads=[b_cst], writes=[b_misc])
        P.op("pool", MSET(odt16[:], 0.0), writes=[b_odt16])
        P.op("dve", lambda h: h.tensor_scalar(out=gq8[:], in0=vect[:, V_GQ:V_GQ + 64], scalar1=0.125, scalar2=None, op0=ALU.mult), reads=[b_vec], writes=[b_misc])
        P.op("dve", lambda h: h.tensor_scalar(out=fq8[:], in0=vect[:, V_FQ:V_FQ + 64], scalar1=0.125, scalar2=None, op0=ALU.mult), reads=[b_vec], writes=[b_misc])
        P.op("dve", lambda h: h.tensor_scalar(out=sgs[:], in0=vect[:, V_SG:V_SG + 128], scalar1=float(1.0 - LAMBDA_INIT), scalar2=None, op0=ALU.mult), reads=[b_vec], writes=[b_misc])
        lm = vect[:, V_LAM:V_LAM + 256]
        P.op("dve", lambda h: h.tensor_tensor(out=small[:, 0:64], in0=lm[:, 0:64], in1=lm[:, 64:128], op=ALU.mult), reads=[b_vec], writes=[b_misc])
        P.op("dve", lambda h: h.tensor_reduce(out=lamt[:, 0:1], in_=small[:, 0:64], axis=AX.X, op=ALU.add), reads=[b_misc], writes=[b_misc])
        P.op("dve", lambda h: h.tensor_tensor(out=small[:, 0:64], in0=lm[:, 128:192], in1=lm[:, 192:256], op=ALU.mult), reads=[b_vec, b_misc], writes=[b_misc])
        P.op("dve", lambda h: h.tensor_reduce(out=lamt[:, 1:2], in_=small[:, 0:64], axis=AX.X, op=ALU.add), reads=[b_misc], writes=[b_misc])
        P.op("act", lambda h: h.activation(out=lamt[:, 2:4], in_=lamt[:, 0:2], func=AF.Exp), reads=[b_misc], writes=[b_misc])
        P.op("dve", lambda h: h.tensor_tensor(out=lamt[:, 4:5], in0=lamt[:, 3:4], in1=lamt[:, 2:3], op=ALU.subtract), reads=[b_misc], writes=[b_misc])
        P.op("dve", lambda h: h.tensor_scalar(out=lamt[:, 4:5], in0=lamt[:, 4:5], scalar1=float(-LAMBDA_INIT), scalar2=None, op0=ALU.add), reads=[b_misc], writes=[b_misc])
        neglam = lamt[:, 4:5]

        rbx = AR.alloc([4], F32)
        b_rb = Buf("rb")
        P.op("dve", MSET(rbx[32:64, :], 1.0), writes=[b_rb])
        P.dma(lambda h: h.dma_start(out=rbx[0:32, :], in_=rel_bias), b_rb, writes=[b_rb])
        P.op("pe", lambda h: h.matmul(ps[0][0:4, 0:384], rbx[0:33, :], caux[0:33, 0:384], start=True, stop=True), reads=[b_rb, b_caux], writes=[psb[0]])
        gt_s = AR.alloc([384], F32)
        TTf = AR.alloc([4, 256], F32)
        b_gt = Buf("gt")
        b_gtd = Buf("gtd")
        b_tt = Buf("tt")
        P.op("dve", lambda h: h.tensor_copy(out=gt_s[0:4, :], in_=ps[0][0:4, 0:384]), reads=[psb[0]], writes=[b_gt])
        P.dma(lambda h: h.dma_start(out=gtab, in_=gt_s[0:4, :]), b_gt, reads=[b_gt], writes=[b_gtd])
        tt_src = bass.AP(tensor=gtab_h, offset=0, ap=[[1, 128], [384, 4], [1, 256]])
        P.dma(lambda h: h.dma_start(out=TTf, in_=tt_src), b_tt, reads=[b_gtd], writes=[b_tt])
        P.op("dve", lambda h: h.tensor_copy(out=TTb[:], in_=TTf), reads=[b_tt], writes=[b_misc])
        P.barrier()
        AR.release()

        AR.mark()
        QT = AR.alloc([8, SEQ], BF16)
        KT = AR.alloc([8, SEQ], BF16)
        Vd = AR.alloc([NT, 4 * 130], BF16)
        Vf = AR.alloc([NT, 8 * 66], BF16)
        b_QT = [[Buf("QT%d_%d" % (p_, i)) for i in range(NT)] for p_ in range(8)]
        b_KT = [[Buf("KT%d_%d" % (p_, i)) for i in range(NT)] for p_ in range(8)]
        b_V = [Buf("V%d" % i) for i in range(NT)]
        b_LF = [Buf("LF%d" % i) for i in range(NT + 1)]
        Vd4 = Vd.rearrange("p t (h e) -> p t h e", e=130)
        Vf4 = Vf.rearrange("p t (h e) -> p t h e", e=66)
        for i in range(NT):
            P.op("pool", lambda h, i=i: h.memset(Vd4[:, i, :, 128:129], 1.0), writes=[b_V[i]])
            P.op("pool", lambda h, i=i: h.memset(Vf4[:, i, :, 64:65], 1.0), writes=[b_V[i]])

        if "A" in phases:
            AR.mark()
            Wq = AR.alloc([8, QKVW], BF16)
            top_a = AR.top
            wstA = [AR.alloc([QKVW // 2], F32) for _ in range(4)]
            b_wstA = [Buf("wstA%d" % i_) for i_ in range(4)]
            b_W = Buf("Wq")
            w_in_v = w_in.rearrange("(kc p) c -> p kc c", p=128)
            HW = QKVW // 2
            for kc in range(8):
                for hf in range(2):
                    k_ = kc * 2 + hf
                    s4_ = k_ % 4
                    P.dma(DMA(wstA[s4_], w_in_v[:, kc, hf * HW:(hf + 1) * HW]), b_wstA[s4_], writes=[b_wstA[s4_]])
                    if k_ % 2 == 0:
                        P.op("dve", TS(Wq[:, kc, hf * HW:(hf + 1) * HW], wstA[s4_], g1t[:, kc:kc + 1], ALU.mult), reads=[b_wstA[s4_], b_vec], writes=[b_W])
                    else:
                        P.op("act", ACTV(Wq[:, kc, hf * HW:(hf + 1) * HW], wstA[s4_], AF.Copy, scale=g1t[:, kc:kc + 1]), reads=[b_wstA[s4_], b_vec], writes=[b_W])
            P.barrier()
            AR.top = top_a
            xs_t = [AR.alloc([D], F32) for _ in range(2)]
            xn_t = [AR.alloc([D], BF16) for _ in range(2)]
            hT_t = [AR.alloc([8, 128], BF16) for _ in range(2)]
            st4 = [AR.alloc([4], F32) for _ in range(2)]
            sq_t = [AR.alloc([512], F32) for _ in range(4)]
            kb_t = [AR.alloc([512], BF16) for _ in range(2)]
            og_t = [AR.alloc([512], F32) for _ in range(2)]
            st8 = [AR.alloc([32], F32) for _ in range(4)]
            lg_t = [AR.alloc([40], F32) for _ in range(2)]
            b_xs = [Buf("xs0"), Buf("xs1")]
            b_xn = [Buf("xn0"), Buf("xn1")]
            b_hT = [Buf("hT0"), Buf("hT1")]
            b_st4 = [Buf("st40"), Buf("st41")]
            b_sq = [Buf("sq%d" % i_) for i_ in range(4)]
            b_kb = [Buf("kb0"), Buf("kb1")]
            b_og = [Buf("og%d" % i) for i in range(2)]
            b_st8 = [Buf("st8%d" % i_) for i_ in range(4)]
            b_lg = [Buf("lg0"), Buf("lg1")]
            b_junk = Buf("junk")
            nq = [0]
            nog = [0]
            nkn = [0]

            def stage1(i):
                s = i % 2
                rows = 128 if i < NT else NS
                if i == NT:
                    P.op("pool", lambda h, s=s: h.memset(xs_t[s], 0.0), writes=[b_xs[s]])
                    P.dma(lambda h, s=s: h.dma_start(out=xs_t[s][0:NS, :], in_=xsm), b_xs[s], writes=[b_xs[s]])
                else:
                    P.dma(lambda h, s=s, i=i: h.dma_start(out=xs_t[s], in_=xp[i * 128:(i + 1) * 128, :]), b_xs[s], writes=[b_xs[s]])
                P.op("act", lambda h, s=s: h.activation(out=xn_t[s], in_=xs_t[s], func=AF.Square, accum_out=st4[s][:, 0:1]),
                     reads=[b_xs[s]], writes=[b_xn[s], b_st4[s]])
                P.op("act", lambda h, s=s: h.activation(out=st4[s][:, 1:2], in_=st4[s][:, 0:1], func=AF.Sqrt, scale=1.0 / D, bias=float(EPS)),
                     reads=[b_st4[s]], writes=[b_st4[s]])
                P.op("dve", lambda h, s=s: h.reciprocal(out=st4[s][:, 2:3], in_=st4[s][:, 1:2]), reads=[b_st4[s]], writes=[b_st4[s]])
                P.op("dve", lambda h, s=s: h.tensor_scalar(out=xn_t[s], in0=xs_t[s], scalar1=st4[s][:, 2:3], scalar2=None, op0=ALU.mult),
                     reads=[b_xs[s], b_st4[s]], writes=[b_xn[s]])
                if DBG_STAGE < 1:
                    return
                pT = ps[7][:].bitcast(BF16)
                for kc in range(8):
                    P.op("pe", lambda h, s=s, kc=kc: h.transpose(out=pT[:, kc * 128:(kc + 1) * 128], in_=xn_t[s][:, kc * 128:(kc + 1) * 128], identity=identb[:]),
                         reads=[b_xn[s], b_misc], writes=[psb[7]])
                P.op("act", lambda h, s=s: h.activation(out=hT_t[s].rearrange("p a b -> p (a b)"), in_=pT, func=AF.Copy),
                     reads=[psb[7]], writes=[b_hT[s]])

            def stage2(i):
                s = i % 2
                rows = 128 if i < NT else NS
                r0, r1 = i * 128, i * 128 + rows
                pr = (lambda t, ts: t[r0:r1, :] if i < NT else ts[0:rows, :])
                hT = hT_t[s]
                nblocks = [(0, O_DQ, False, gq8[:], 0, None), (1, O_DK, True, vect[:, V_GK:V_GK + 64], 0, pr(o_dk, s_dk)),
                           (3, O_FQ, False, fq8[:], 512, None), (4, O_FK, True, vect[:, V_FK:V_FK + 64], 512, pr(o_fk, s_fk))]
                vblocks = [(2, O_DV, True, pr(o_dv, s_dv)), (5, O_FV, False, pr(o_fv, s_fv))]
                for bank, c0 in [(0, O_DQ), (1, O_DK), (2, O_DV), (3, O_FQ), (4, O_FK), (5, O_FV)]:
                    for kc in range(8):
                        P.op("pe", MM(ps[bank][:, :], hT[:, kc, :], Wq[:, kc, c0:c0 + 512], kc == 0, kc == 7), reads=[b_hT[s], b_W], writes=[psb[bank]])
                for kc in range(8):
                    P.op("pe", MM(ps[7][:, 0:8], hT[:, kc, :], Wq[:, kc, O_LF:O_LF + 8], kc == 0, kc == 7), reads=[b_hT[s], b_W], writes=[psb[7]])
                v3 = lambda a: a.rearrange("p (a b) -> p a b", b=64)
                for u, (bank, c0, is_k, gains, dc, od_) in enumerate(nblocks):
                    P.op("act", ACTV(sq_t[u], ps[bank][:, :], AF.Square), reads=[psb[bank]], writes=[b_sq[u]])
                for u in range(4):
                    P.op("dve", RED(st8[u][:, 0:8], v3(sq_t[u])), reads=[b_sq[u]], writes=[b_st8[u]])
                for u in range(4):
                    P.op("act", ACTV(st8[u][:, 8:16], st8[u][:, 0:8], AF.Sqrt, scale=1.0 / HD, bias=float(EPS)), reads=[b_st8[u]], writes=[b_st8[u]])
                for u in range(4):
                    P.op("dve", RCP(st8[u][:, 16:24], st8[u][:, 8:16]), reads=[b_st8[u]], writes=[b_st8[u]])
                for u, (bank, c0, is_k, gains, dc, od_) in enumerate(nblocks):
                    P.op("dve", TT(v3(sq_t[u]), v3(ps[bank][:, :]), st8[u][:, 16:24].unsqueeze(2).to_broadcast([128, 8, 64]), ALU.mult),
                         reads=[psb[bank], b_st8[u]], writes=[b_sq[u]])
                for n_, (bank, c0, is_d, od_) in enumerate(vblocks):
                    o = n_
                    P.op("act", ACTV(og_t[o], ps[bank][:, :], AF.Copy), reads=[psb[bank]], writes=[b_og[o]])
                    P.dma(DMA(od_, og_t[o][0:rows, :]), b_og[o], reads=[b_og[o]])
                    if i < NT:
                        if is_d:
                            P.op("dve", CP(Vd4[:, i, :, 0:128], og_t[o].rearrange("p (a b) -> p a b", b=128)), reads=[b_og[o]], writes=[b_V[i]])
                        else:
                            P.op("dve", CP(Vf4[:, i, :, 0:64], og_t[o].rearrange("p (a b) -> p a b", b=64)), reads=[b_og[o]], writes=[b_V[i]])
                    else:
                        cc = 0 if is_d else 512
                        P.op("dve", CP(v16[:, cc:cc + 512], og_t[o]), reads=[b_og[o]], writes=[b_16])
                for u, (bank, c0, is_k, gains, dc, od_) in enumerate(nblocks):
                    gb = gains.unsqueeze(1).to_broadcast([128, 8, 64])
                    kbu = kb_t[u % 2]
                    if is_k:
                        if i < NT:
                            P.op("dve", TT(v3(sq_t[u]), v3(sq_t[u]), gb, ALU.mult), reads=[b_sq[u], b_vec, b_misc], writes=[b_sq[u]])
                            P.dma(DMA(od_, sq_t[u][0:rows, :]), b_sq[u], reads=[b_sq[u]])
                            P.op("act", ACTV(kbu, sq_t[u], AF.Copy), reads=[b_sq[u]], writes=[b_kb[u % 2]])
                        else:
                            P.op("dve", TT(v3(k16[:, dc:dc + 512]), v3(sq_t[u]), gb, ALU.mult), reads=[b_sq[u], b_vec, b_misc], writes=[b_16])
                            P.dma(DMA(od_, k16[0:rows, dc:dc + 512]), b_16, reads=[b_16])
                    else:
                        if i < NT:
                            P.op("dve", TT(v3(kbu), v3(sq_t[u]), gb, ALU.mult), reads=[b_sq[u], b_vec, b_misc], writes=[b_kb[u % 2]])
                        else:
                            P.op("dve", TT(v3(q16[:, dc:dc + 512]), v3(sq_t[u]), gb, ALU.mult), reads=[b_sq[u], b_vec, b_misc], writes=[b_16])
                    if i < NT:
                        pT2 = ps[6][:].bitcast(BF16)
                        hf_ = (u % 2) * 512
                        for j in range(4):
                            P.op("pe", TR(pT2[:, hf_ + j * 128:hf_ + (j + 1) * 128], kbu[:, j * 128:(j + 1) * 128]), reads=[b_kb[u % 2], b_misc], writes=[psb[6]])
                        T, bT = (KT, b_KT) if is_k else (QT, b_QT)
                        pair0 = dc // 128
                        P.op("dve", CP(T[:, pair0:pair0 + 4, i * 128:(i + 1) * 128], pT2[:, hf_:hf_ + 512].rearrange("p (a b) -> p a b", b=128)),
                             reads=[psb[6]], writes=[bT[pp][i] for pp in range(pair0, pair0 + 4)])
                u = i % 2
                lg = lg_t[u]
                P.op("dve", TT(lg[:, 0:8], ps[7][:, 0:8], vect[:, V_FB:V_FB + 8], ALU.add), reads=[psb[7], b_vec], writes=[b_lg[u]])
                P.op("dve", TS(lg[:, 8:16], lg[:, 0:8], -1.0, ALU.mult), reads=[b_lg[u]], writes=[b_lg[u]])
                P.op("dve", TT(lg[:, 8:16], lg[:, 0:8], lg[:, 8:16], ALU.min), reads=[b_lg[u]], writes=[b_lg[u]])
                P.op("act", ACTV(lg[:, 16:24], lg[:, 8:16], AF.Exp), reads=[b_lg[u]], writes=[b_lg[u]])
                P.op("act", ACTV(lg[:, 24:32], lg[:, 16:24], AF.Ln, bias=1.0), reads=[b_lg[u]], writes=[b_lg[u]])
                P.op("dve", TS(lg[:, 32:40], lg[:, 0:8], 0.0, ALU.min), reads=[b_lg[u]], writes=[b_lg[u]])
                P.op("dve", TT(LF[:, i, :], lg[:, 32:40], lg[:, 24:32], ALU.subtract), reads=[b_lg[u]], writes=[b_LF[i]])
                P.dma(DMA(pr(o_lf, s_lf), LF[0:rows, i, :]), b_LF[i], reads=[b_LF[i]])

            stage1(DBG_TILES[0])
            for n_, i_ in enumerate(DBG_TILES):
                if n_ + 1 < len(DBG_TILES):
                    stage1(DBG_TILES[n_ + 1])
                stage2(i_)
            P.barrier()
            AR.release()

        qt_top = None
        if "B" in phases:
            AR.mark()
            PT = [AR.alloc([16, 512], BF16) for _ in range(2)]
            b_PT = [Buf("PT0"), Buf("PT1")]
            FB = AR.alloc([NT, NT * 8], F32)
            WC = AR.alloc([NT * 8], F32)
            TBt = AR.alloc([NT * 8], F32)
            CAR = AR.alloc([8], F32)
            o1s = AR.alloc([4, 128], F32)
            t2s = AR.alloc([4, 128], F32)
            ods = AR.alloc([4, 128], F32)
            sqs = AR.alloc([4, 128], F32)
            odb = [AR.alloc([4, 128], BF16) for _ in range(2)]
            ofs = [AR.alloc([4, 128], BF16) for _ in range(2)]
            rcs = [AR.alloc([16], F32) for _ in range(2)]
            b_c = Buf("cum")
            b_fb = Buf("fb")
            b_o1 = Buf("o1s")
            b_t2 = Buf("t2s")
            b_ods = Buf("ods")
            b_sqs = Buf("sqs")
            b_odb = [Buf("odb0"), Buf("odb1")]
            b_ofs = [Buf("ofs0"), Buf("ofs1")]
            b_rc = [Buf("rc0"), Buf("rc1")]
            LFv = LF[:, 0:NT, :].rearrange("p a b -> p (a b)")
            CCv = CC[:].rearrange("p a b -> p (a b)")
            CRv = CR[:].rearrange("p a b -> p (a b)")
            P.op("pe", MM(ps[0][:, 0:128], cstt[:, C_LTRI:C_LTRI + 128], LFv, True, True), reads=[b_cst] + b_LF[0:NT], writes=[psb[0]])
            P.op("pe", MM(ps[1][:, 0:128], cstt[:, C_ONES:C_ONES + 128], LFv, True, True), reads=[b_cst] + b_LF[0:NT], writes=[psb[1]])
            P.op("dve", CP(WC, ps[0][:, 0:128]), reads=[psb[0]], writes=[b_c])
            P.op("dve", CP(TBt, ps[1][:, 0:128]), reads=[psb[1]], writes=[b_c])
            P.op("dve", CP(CC[:, 0, :], WC[:, 0:8]), reads=[b_c], writes=[b_c])
            P.op("dve", MSET(CAR, 0.0), writes=[b_c])
            for i in range(1, NT):
                P.op("dve", TT(CAR, CAR, TBt[:, (i - 1) * 8:i * 8], ALU.add), reads=[b_c], writes=[b_c])
                P.op("dve", TT(CC[:, i, :], WC[:, i * 8:(i + 1) * 8], CAR, ALU.add), reads=[b_c], writes=[b_c])
            P.op("pe", MM(ps[0][:, 0:128], cstt[:, C_SEL64:C_SEL64 + 128], CCv, True, True), reads=[b_cst, b_c], writes=[psb[0]])
            P.op("dve", CP(CRv, ps[0][:, 0:128]), reads=[psb[0]], writes=[b_c])
            for kt in range(NT):
                n = NT - kt
                P.op("dve", TT(FB[:, kt, kt * 8:NT * 8].rearrange("p (a b) -> p a b", b=8), CR[:, kt:NT, :],
                               CC[:, kt:kt + 1, :].to_broadcast([128, n, 8]), ALU.subtract), reads=[b_c], writes=[b_fb])

            SB_ = (0, 1, 2)
            PO = ((3, 4), (5, 6))
            pT7 = ps[7][:].bitcast(BF16)
            cnt = 0
            scnt = 0
            ofc = 0
            odc = 0
            def minfo(n):
                B, mp = n // 16, n % 16
                is_d = mp < 8
                if is_d:
                    h_, m_ = mp // 2, mp % 2
                    pair, half, vh, hf = h_, m_, h_, 0
                else:
                    hf = mp - 8
                    h_, m_ = 0, 0
                    pair, half, vh = 4 + hf // 2, hf % 2, hf
                return B, mp, is_d, h_, m_, hf, pair, half, vh, n % 2, PO[n % 2], half * 64, 4 * B + 4

            def SE(n):
                nonlocal scnt
                B, mp, is_d, h_, m_, hf, pair, half, vh, slot, pset, p0, nk = minfo(n)
                cnt = n + 1
                for kt in range(nk):
                    j0 = max(0, kt - 4 * B)
                    c0 = j0 * 128
                    sbk = SB_[scnt % 3]
                    scnt += 1
                    extra = (is_d and kt >= 4 * B - 1) or ((not is_d) and kt >= 4 * B)
                    P.op("pe", MM(ps[sbk][:, c0:512], KT[p0:p0 + 64, pair, kt * 128:(kt + 1) * 128],
                                  QT[p0:p0 + 64, pair, 512 * B + c0:512 * B + 512], True, not extra),
                         reads=[b_KT[pair][kt]] + [b_QT[pair][4 * B + j] for j in range(j0, 4)], writes=[psb[sbk]])
                    if is_d and extra:
                        if kt >= 4 * B:
                            a0 = c0
                            wd = min(256, 512 - a0)
                            rhs = TTb[:, h_, 0:wd]
                        else:
                            a0, wd = 0, 128
                            rhs = TTb[:, h_, 128:256]
                        P.op("pe", MM(ps[sbk][:, a0:a0 + wd], jb[:], rhs, False, True), reads=[b_misc], writes=[psb[sbk]])
                    if (not is_d) and extra:
                        P.op("pe", MM(ps[sbk][:, c0:c0 + 128], identb[:], mtrib[:], False, True), reads=[b_misc], writes=[psb[sbk]])
                    if is_d:
                        P.op("act", ACTV(PT[slot][:, kt, c0:512], ps[sbk][:, c0:512], AF.Exp, bias=vect[:, V_B31 + h_:V_B31 + h_ + 1]),
                             reads=[psb[sbk], b_vec], writes=[b_PT[slot]])
                    else:
                        for j in range(j0, 4):
                            qt = 4 * B + j
                            P.op("act", ACTV(PT[slot][:, kt, j * 128:(j + 1) * 128], ps[sbk][:, j * 128:(j + 1) * 128], AF.Exp,
                                             bias=FB[:, kt, qt * 8 + hf:qt * 8 + hf + 1]),
                                 reads=[psb[sbk], b_fb], writes=[b_PT[slot]])

            def PVN(n):
                nonlocal ofc, odc
                B, mp, is_d, h_, m_, hf, pair, half, vh, slot, pset, p0, nk = minfo(n)
                cnt = n + 1
                Wc = 129 if is_d else 65
                V4 = Vd4 if is_d else Vf4
                for j in range(4):
                    qt = 4 * B + j
                    if is_d:
                        bank, off = pset[j // 2], (j % 2) * 130
                    else:
                        bank, off = pset[0], j * 66
                    for kt in range(qt + 1):
                        P.op("pe", MM(ps[bank][:, off:off + Wc], PT[slot][:, kt, j * 128:(j + 1) * 128], V4[:, kt, vh, 0:Wc], kt == 0, kt == qt),
                             reads=[b_PT[slot], b_V[kt]], writes=[psb[bank]])
                if not is_d:
                    bank = pset[0]
                    pv = ps[bank][:, 0:264].rearrange("p (j e) -> p j e", e=66)
                    u = ofc % 2
                    r = rcs[cnt % 2]
                    br = b_rc[cnt % 2]
                    P.op("dve", RCP(r[:, 0:4], pv[:, :, 64]), reads=[psb[bank]], writes=[br])
                    P.op("dve", TT(ofs[u][:, :, half * 64:(half + 1) * 64], pv[:, :, 0:64], r[:, 0:4].unsqueeze(2).to_broadcast([128, 4, 64]), ALU.mult),
                         reads=[psb[bank], br], writes=[b_ofs[u]])
                    if half == 1:
                        for j in range(4):
                            P.op("pe", TR(pT7[:, j * 128:(j + 1) * 128], ofs[u][:, j, :]), reads=[b_ofs[u], b_misc], writes=[psb[7]])
                        P.op("act", ACTV(QT[:, pair, 512 * B:512 * B + 512], pT7[:, 0:512], AF.Copy), reads=[psb[7]],
                             writes=[b_QT[pair][4 * B + j] for j in range(4)])
                        ofc += 1
                else:
                    r = rcs[cnt % 2]
                    br = b_rc[cnt % 2]
                    dst, bdst = (o1s, b_o1) if m_ == 0 else (t2s, b_t2)
                    for bi_, jj in ((0, 0), (1, 2)):
                        bank = pset[bi_]
                        pvx = ps[bank][:, 0:260].rearrange("p (j e) -> p j e", e=130)
                        P.op("dve", RCP(r[:, jj:jj + 2], pvx[:, :, 128]), reads=[psb[bank]], writes=[br])
                        P.op("dve", TT(dst[:, jj:jj + 2, :], pvx[:, :, 0:128], r[:, jj:jj + 2].unsqueeze(2).to_broadcast([128, 2, 128]), ALU.mult),
                             reads=[psb[bank], br], writes=[bdst])
                    if m_ == 1:
                        fl = lambda a: a.rearrange("p a b -> p (a b)")
                        P.op("dve", lambda h: h.scalar_tensor_tensor(out=fl(ods), in0=fl(t2s), scalar=neglam, in1=fl(o1s), op0=ALU.mult, op1=ALU.add),
                             reads=[b_t2, b_o1, b_misc], writes=[b_ods])
                        P.op("act", ACTV(fl(sqs), fl(ods), AF.Square), reads=[b_ods], writes=[b_sqs])
                        P.op("dve", RED(r[:, 4:8], sqs), reads=[b_sqs], writes=[br])
                        P.op("act", ACTV(r[:, 8:12], r[:, 4:8], AF.Sqrt, scale=1.0 / 128, bias=float(EPS)), reads=[br], writes=[br])
                        P.op("dve", RCP(r[:, 12:16], r[:, 8:12]), reads=[br], writes=[br])
                        P.op("dve", TT(ods, ods, r[:, 12:16].unsqueeze(2).to_broadcast([128, 4, 128]), ALU.mult), reads=[b_ods, br], writes=[b_ods])
                        u = odc % 2
                        P.op("dve", TT(odb[u], ods, sgs[:].unsqueeze(1).to_broadcast([128, 4, 128]), ALU.mult), reads=[b_ods, b_misc], writes=[b_odb[u]])
                        for j in range(4):
                            P.op("pe", TR(pT7[:, j * 128:(j + 1) * 128], odb[u][:, j, :]), reads=[b_odb[u], b_misc], writes=[psb[7]])
                        P.op("act", ACTV(QT[:, pair, 512 * B:512 * B + 512], pT7[:, 0:512], AF.Copy), reads=[psb[7]],
                             writes=[b_QT[pair][4 * B + j] for j in range(4)])
                        odc += 1

            NM = 64
            SE(0)
            for n in range(NM):
                if n + 1 < NM:
                    SE(n + 1)
                PVN(n)
            if DBG_OUT:
                b_dbg = Buf("dbg")
                P.dma(DMA(dbg_odt, QT.rearrange("p a b -> p (a b)")), b_dbg, reads=[b_QT[p_][i_] for p_ in range(8) for i_ in range(NT)])
            P.barrier()
            AR.release()
        ODT = QT
        AR.top = 8192

        if "S" in phases:
            AR.top = 8192
            saux = AR.alloc([1280], F32)
            pio = AR.alloc([2], F32)
            ptb = AR.alloc([NS * NPG], I32)
            idx = AR.alloc([NS * NPG], I32)
            selr = AR.alloc([NS, 128], BF16)
            selrf = [AR.alloc([128], F32) for _ in range(2)]
            BIASD = AR.alloc([NPG, 8], F32)
            pself = AR.alloc([32], F32)
            accd = AR.alloc([512], F32)
            accf = AR.alloc([512], F32)
            NSL = 3
            PGt = [AR.alloc([2056], F32) for _ in range(NSL)]
            b_PG = [Buf("PG%d" % i_) for i_ in range(NSL)]
            LFp = [AR.alloc([NPG * 8], F32) for _ in range(2)]
            Vdb = [AR.alloc([NPG, 512], BF16) for _ in range(2)]
            Vfb = [AR.alloc([NPG, 512], BF16) for _ in range(2)]
            prod = [AR.alloc([512], F32) for _ in range(2)]
            SC = [AR.alloc([2, NPG * 8], F32) for _ in range(2)]
            PR = [AR.alloc([2, NPG * 8], F32) for _ in range(2)]
            DEC = AR.alloc([NPG * 8], F32)
            SUF = AR.alloc([NPG * 8], F32)
            TBs = AR.alloc([NPG * 8], F32)
            sm = [AR.alloc([64], F32) for _ in range(2)]
            ADb = [AR.alloc([NPG + 1, 4], BF16) for _ in range(2)]
            AFb = [AR.alloc([NPG + 1, 8], BF16) for _ in range(2)]
            mdt = AR.alloc([512], F32)
            b_saux = Buf("saux")
            b_idx = Buf("idx")
            b_selr = Buf("selr")
            b_selrf = [Buf("selrf0"), Buf("selrf1")]
            b_biasd = Buf("biasd")
            b_pself = Buf("pself")
            b_acc = Buf("acc")
            b_Kd = [Buf("Kd0"), Buf("Kd1")]
            b_Kf = [Buf("Kf0"), Buf("Kf1")]
            b_Vd = [Buf("Vd0"), Buf("Vd1")]
            b_Vf = [Buf("Vf0"), Buf("Vf1")]
            b_LFp = [Buf("LFp0"), Buf("LFp1")]
            b_Vdb = [Buf("Vdb0"), Buf("Vdb1")]
            b_Vfb = [Buf("Vfb0"), Buf("Vfb1")]
            b_prod = [Buf("prod0"), Buf("prod1")]
            b_SC = [Buf("SC0"), Buf("SC1")]
            b_PR = [Buf("PR0"), Buf("PR1")]
            b_dec = Buf("dec")
            b_sm = [Buf("sm0"), Buf("sm1")]
            b_AD = [Buf("AD0"), Buf("AD1")]
            b_AF = [Buf("AF0"), Buf("AF1")]
            b_md = Buf("md")
            DM8 = saux[0:8, 0:512]
            DM4 = saux[0:4, 512:1024]
            ESv = saux[:, 1024:1280]
            P.dma(DMA(saux, cst[:, C_DM8:C_DM8 + 1280]), b_saux, writes=[b_saux])
            P.dma(DMA(pio[:, 0:2], cst[:, C_PIO:C_PIO + 2]), b_saux, writes=[b_saux])
            P.dma(DMA(ptb, ptab.partition_broadcast(128)), b_idx, writes=[b_idx])
            P.op("dve", lambda h: h.tensor_scalar(out=idx, in0=ptb, scalar1=128.0, scalar2=pio[:, 0:1], op0=ALU.mult, op1=ALU.add), reads=[b_idx, b_saux], writes=[b_idx])
            for s in range(NS):
                P.op("dve", CP(selr[:, s, :], cstt[:, C_ID + s:C_ID + s + 1].to_broadcast([128, 128])), reads=[b_cst], writes=[b_selr])
            P.op("pool", MSET(accd, 0.0), writes=[b_acc])
            P.op("pool", MSET(accf, 0.0), writes=[b_acc])
            B3 = BIASD.rearrange("p g (h m) -> p g h m", m=2)
            for m in range(2):
                P.op("dve", CP(B3[:, :, :, m], vect[:, V_B31:V_B31 + 4].unsqueeze(1).to_broadcast([128, NPG, 4])), reads=[b_vec], writes=[b_biasd])
            P.op("pe", MM(ps[4][:, 0:4], jb[:], TTb[:, :, 128], True, True), reads=[b_misc], writes=[psb[4]])
            for m in range(2):
                P.op("dve", TT(B3[:, NPG - 1, :, m], ps[4][:, 0:4], vect[:, V_B31:V_B31 + 4], ALU.add), reads=[psb[4], b_vec, b_biasd], writes=[b_biasd])
            for c0, dstc in ((0, 0), (512, 8)):
                P.op("dve", TT(prod[0], k16[:, c0:c0 + 512], q16[:, c0:c0 + 512], ALU.mult), reads=[b_16], writes=[b_prod[0]])
                P.op("dve", RED(pself[:, 16 + dstc:16 + dstc + 8], prod[0].rearrange("p (a b) -> p a b", b=64)), reads=[b_prod[0]], writes=[b_pself])
            P.op("dve", TT(pself[:, 16:24].rearrange("p (h m) -> p h m", m=2), pself[:, 16:24].rearrange("p (h m) -> p h m", m=2),
                           vect[:, V_B0:V_B0 + 4].unsqueeze(2).to_broadcast([128, 4, 2]), ALU.add), reads=[b_pself, b_vec], writes=[b_pself])
            P.op("act", ACTV(pself[:, 0:16], pself[:, 16:32], AF.Exp), reads=[b_pself], writes=[b_pself])

            gcn = 0

            def pages(s, gen):
                nonlocal gcn
                u = s % 2
                qd_bank, qf_bank = (0, 1) if u == 0 else (2, 3)
                P.op("pe", MM(ps[qd_bank][:, :], selr[:, s, :], q16[:, 0:512], True, True), reads=[b_selr, b_16], writes=[psb[qd_bank]])
                P.op("pe", MM(ps[qf_bank][:, :], selr[:, s, :], q16[:, 512:1024], True, True), reads=[b_selr, b_16], writes=[psb[qf_bank]])
                P.op("dve", CP(selrf[u], cstt[:, C_ID + s:C_ID + s + 1].to_broadcast([128, 128])), reads=[b_cst], writes=[b_selrf[u]])
                msk = cstt[:, C_ID + s:C_ID + s + 1]
                for pg in range(NPG):
                    j = s * NPG + pg
                    g = gcn % NSL
                    gcn += 1
                    off = bass.IndirectOffsetOnAxis(ap=idx[:, j:j + 1], axis=0)
                    P.dma(lambda h, dst=PGt[g], off=off: h.indirect_dma_start(out=dst, out_offset=None, in_=c_all, in_offset=off),
                          b_PG[g], reads=[b_idx], writes=[b_PG[g]], E="pool")
                    for col, qb_ in ((0, qd_bank), (1, qf_bank)):
                        P.op("dve", TT(prod[col], PGt[g][:, col * 512:(col + 1) * 512], ps[qb_][:, :], ALU.mult), reads=[b_PG[g], psb[qb_]], writes=[b_prod[col]])
                        P.op("dve", RED(SC[u][:, col, pg * 8:(pg + 1) * 8], prod[col].rearrange("p (a b) -> p a b", b=64)), reads=[b_prod[col]], writes=[b_SC[u]])
                    P.op("act", ACTV(Vdb[u][:, pg, :], PGt[g][:, 1024:1536], AF.Copy), reads=[b_PG[g]], writes=[b_Vdb[u]])
                    P.op("act", ACTV(Vfb[u][:, pg, :], PGt[g][:, 1536:2048], AF.Copy), reads=[b_PG[g]], writes=[b_Vfb[u]])
                    P.op("act", ACTV(LFp[u][:, pg * 8:(pg + 1) * 8], PGt[g][:, 2048:2056], AF.Copy), reads=[b_PG[g]], writes=[b_LFp[u]])
                    if gen is not None:
                        next(gen, None)
            def post(s):
                u = s % 2
                msk = cstt[:, C_ID + s:C_ID + s + 1]
                P.op("pe", MM(ps[4][:, 0:128], cstt[:, C_ONES:C_ONES + 128], LFp[u], True, True), reads=[b_cst, b_LFp[u]], writes=[psb[4]])
                P.op("dve", CP(TBs, ps[4][:, 0:128]), reads=[psb[4]], writes=[b_dec])
                P.op("pe", MM(ps[4][:, 0:8], selrf[u], LF[:, NT, :], True, True), reads=[b_selrf[u], b_LF[NT]], writes=[psb[4]])
                P.op("dve", CP(SUF[:, (NPG - 1) * 8:NPG * 8], ps[4][:, 0:8]), reads=[psb[4]], writes=[b_dec])
                yield
                for pg in range(NPG - 2, -1, -1):
                    P.op("dve", TT(SUF[:, pg * 8:(pg + 1) * 8], SUF[:, (pg + 1) * 8:(pg + 2) * 8], TBs[:, (pg + 1) * 8:(pg + 2) * 8], ALU.add), reads=[b_dec], writes=[b_dec])
                yield
                P.op("pe", MM(ps[4][:, 0:128], cstt[:, C_UTRI:C_UTRI + 128], LFp[u], True, True), reads=[b_cst, b_LFp[u]], writes=[psb[4]])
                P.op("dve", TT(DEC, ps[4][:, 0:128], SUF, ALU.add), reads=[psb[4], b_dec], writes=[b_dec])
                yield
                P.op("dve", TT(SC[u][:, 0, :], SC[u][:, 0, :], BIASD.rearrange("p g c -> p (g c)"), ALU.add), reads=[b_SC[u], b_biasd], writes=[b_SC[u]])
                P.op("dve", TT(SC[u][:, 1, :], SC[u][:, 1, :], DEC, ALU.add), reads=[b_SC[u], b_dec], writes=[b_SC[u]])
                P.op("act", ACTV(PR[u].rearrange("p a b -> p (a b)"), SC[u].rearrange("p a b -> p (a b)"), AF.Exp), reads=[b_SC[u]], writes=[b_PR[u]])
                yield
                t_ = sm[u]
                for a_ in range(2):
                    P.op("dve", RED(t_[:, a_ * 8:(a_ + 1) * 8], PR[u][:, a_, :].rearrange("p (g c) -> p c g", c=8)), reads=[b_PR[u]], writes=[b_sm[u]])
                P.op("dve", lambda h, t_=t_, msk=msk: h.scalar_tensor_tensor(out=t_[:, 0:16], in0=pself[:, 0:16], scalar=msk, in1=t_[:, 0:16], op0=ALU.mult, op1=ALU.add),
                     reads=[b_pself, b_cst, b_sm[u]], writes=[b_sm[u]])
                P.op("pe", MM(ps[4][:, 0:16], cstt[:, C_ONES:C_ONES + 128], t_[:, 0:16], True, True), reads=[b_cst, b_sm[u]], writes=[psb[4]])
                P.op("dve", RCP(t_[:, 16:32], ps[4][:, 0:16]), reads=[psb[4]], writes=[b_sm[u]])
                yield
                P.op("dve", lambda h, t_=t_, msk=msk: h.scalar_tensor_tensor(out=t_[:, 32:48], in0=pself[:, 0:16], scalar=msk, in1=t_[:, 16:32], op0=ALU.mult, op1=ALU.mult),
                     reads=[b_pself, b_cst, b_sm[u]], writes=[b_sm[u]])
                yield
                PRd = PR[u][:, 0, :].rearrange("p (g c) -> p g c", c=8)
                PRf = PR[u][:, 1, :].rearrange("p (g c) -> p g c", c=8)
                P.op("dve", TT(PRd, PRd, t_[:, 16:24].unsqueeze(1).to_broadcast([128, NPG, 8]), ALU.mult), reads=[b_PR[u], b_sm[u]], writes=[b_PR[u]])
                P.op("dve", TT(AFb[u][:, 0:NPG, :], PRf, t_[:, 24:32].unsqueeze(1).to_broadcast([128, NPG, 8]), ALU.mult), reads=[b_PR[u], b_sm[u]], writes=[b_AF[u]])
                P.op("dve", CP(AFb[u][:, NPG, :], t_[:, 40:48]), reads=[b_sm[u]], writes=[b_AF[u]])
                PR4 = PR[u][:, 0, :].rearrange("p (g h m) -> p g h m", h=4, m=2)
                P.op("dve", lambda h, u=u, PR4=PR4: h.scalar_tensor_tensor(out=ADb[u][:, 0:NPG, :], in0=PR4[:, :, :, 1], scalar=neglam, in1=PR4[:, :, :, 0], op0=ALU.mult, op1=ALU.add),
                     reads=[b_PR[u], b_misc], writes=[b_AD[u]])
                s4 = t_[:, 32:40].rearrange("p (h m) -> p h m", m=2)
                P.op("dve", lambda h, u=u, s4=s4: h.scalar_tensor_tensor(out=ADb[u][:, NPG, :], in0=s4[:, :, 1], scalar=neglam, in1=s4[:, :, 0], op0=ALU.mult, op1=ALU.add),
                     reads=[b_sm[u], b_misc], writes=[b_AD[u]])
                yield
                for pg in range(NPG + 1):
                    rd = Vdb[u][:, pg, :] if pg < NPG else v16[:, 0:512]
                    rf = Vfb[u][:, pg, :] if pg < NPG else v16[:, 512:1024]
                    P.op("pe", MM(ps[5][0:4, :], ADb[u][:, pg, :], rd, pg == 0, pg == NPG), reads=[b_AD[u], b_Vdb[u], b_16], writes=[psb[5]])
                    P.op("pe", MM(ps[6][0:8, :], AFb[u][:, pg, :], rf, pg == 0, pg == NPG), reads=[b_AF[u], b_Vfb[u], b_16], writes=[psb[6]])
                yield
                for bank, nh, DMm, acc in ((5, 4, DM4, accd), (6, 8, DM8, accf)):
                    P.op("dve", TT(mdt[0:nh, :], ps[bank][0:nh, :], DMm, ALU.mult), reads=[psb[bank], b_saux], writes=[b_md])
                    P.op("pe", MM(ps[7][0:16, :], ESv[0:nh, s * 16:(s + 1) * 16], mdt[0:nh, :], True, True), reads=[b_saux, b_md], writes=[psb[7]])
                    P.op("dve", TT(acc[0:16, :], ps[7][0:16, :], acc[0:16, :], ALU.add), reads=[psb[7], b_acc], writes=[b_acc])

            prev = None
            for s in range(NS):
                pages(s, prev)
                if prev is not None:
                    for _ in prev:
                        pass
                prev = post(s)
            for _ in prev:
                pass
            r = sm[0]
            a3 = accd.rearrange("p (a b) -> p a b", b=128)
            P.op("act", ACTV(prod[0], accd, AF.Square), reads=[b_acc], writes=[b_prod[0]])
            P.op("dve", RED(r[:, 4:8], prod[0].rearrange("p (a b) -> p a b", b=128)), reads=[b_prod[0]], writes=[b_sm[0]])
            P.op("act", ACTV(r[:, 8:12], r[:, 4:8], AF.Sqrt, scale=1.0 / 128, bias=float(EPS)), reads=[b_sm[0]], writes=[b_sm[0]])
            P.op("dve", RCP(r[:, 12:16], r[:, 8:12]), reads=[b_sm[0]], writes=[b_sm[0]])
            P.op("dve", TT(a3, a3, r[:, 12:16].unsqueeze(2).to_broadcast([128, 4, 128]), ALU.mult), reads=[b_acc, b_sm[0]], writes=[b_acc])
            ob = Vdb[0][:, 0, :]
            fb_ = Vdb[0][:, 1, :]
            P.op("dve", TT(ob.rearrange("p (a b) -> p a b", b=128), a3, sgs[:].unsqueeze(1).to_broadcast([128, 4, 128]), ALU.mult), reads=[b_acc, b_misc], writes=[b_Vdb[0]])
            P.op("dve", CP(fb_, accf), reads=[b_acc], writes=[b_Vdb[0]])
            pT7s = ps[7][:].bitcast(BF16)
            for c in range(8):
                src = ob[:, c * 128:(c + 1) * 128] if c < 4 else fb_[:, (c - 4) * 128:(c - 3) * 128]
                P.op("pe", TR(pT7s[:, c * 128:(c + 1) * 128], src), reads=[b_Vdb[0], b_misc], writes=[psb[7]])
            P.op("act", ACTV(odt16[:], pT7s.rearrange("p (a b) -> p a b", b=128), AF.Copy), reads=[psb[7]], writes=[b_odt16])
            P.barrier()
        AR.top = 8192

        lw_cnt = [0]

        def load_w(dst3, src, nch, c0, ncols, scale_t, wst, b_wst, b_dst):
            srcv = src.rearrange("(c p) n -> p c n", p=128)
            nsl = len(wst)
            for c in range(nch):
                for a in range(0, ncols, 1024):
                    w = min(1024, ncols - a)
                    k = lw_cnt[0]
                    lw_cnt[0] += 1
                    s_ = k % nsl
                    P.dma(DMA(wst[s_][:, 0:w], srcv[:, c, c0 + a:c0 + a + w]), b_wst[s_], writes=[b_wst[s_]])
                    eng = ("dve", "act")[k % 2]
                    if eng == "act":
                        if scale_t is not None:
                            P.op(eng, ACTV(dst3[:, c, a:a + w], wst[s_][:, 0:w], AF.Copy, scale=scale_t[:, c:c + 1]), reads=[b_wst[s_], b_vec], writes=[b_dst])
                        else:
                            P.op(eng, ACTV(dst3[:, c, a:a + w], wst[s_][:, 0:w], AF.Copy), reads=[b_wst[s_]], writes=[b_dst])
                    elif scale_t is not None:
                        P.op(eng, TS(dst3[:, c, a:a + w], wst[s_][:, 0:w], scale_t[:, c:c + 1], ALU.mult), reads=[b_wst[s_], b_vec], writes=[b_dst])
                    else:
                        P.op(eng, CP(dst3[:, c, a:a + w], wst[s_][:, 0:w]), reads=[b_wst[s_]], writes=[b_dst])

        def norm_T(xs, b_x, xn, b_xn_, st, b_st_, hT3, b_hT_, bank):
            P.op("act", ACTV(xn, xs, AF.Square, accum_out=st[:, 0:1]), reads=[b_x], writes=[b_xn_, b_st_])
            P.op("act", ACTV(st[:, 1:2], st[:, 0:1], AF.Sqrt, scale=1.0 / D, bias=float(EPS)), reads=[b_st_], writes=[b_st_])
            P.op("dve", RCP(st[:, 2:3], st[:, 1:2]), reads=[b_st_], writes=[b_st_])
            P.op("dve", TS(xn, xs, st[:, 2:3], ALU.mult), reads=[b_x, b_st_], writes=[b_xn_])
            pT = ps[bank][:].bitcast(BF16)
            for kc in range(8):
                P.op("pe", TR(pT[:, kc * 128:(kc + 1) * 128], xn[:, kc * 128:(kc + 1) * 128]), reads=[b_xn_, b_misc], writes=[psb[bank]])
            P.op("act", ACTV(hT3, pT.rearrange("p (a b) -> p a b", b=128), AF.Copy), reads=[psb[bank]], writes=[b_hT_])

        if "C" in phases:
            AR.mark()
            Wg = AR.alloc([8, 2048], BF16)
            Wa = AR.alloc([4, 1024], BF16)
            Wb = AR.alloc([4, 1024], BF16)
            Wo = AR.alloc([8, 1024], BF16)
            gbf = AR.alloc([2048], F32)
            gbb = AR.alloc([2048], BF16)
            wst_c = [AR.alloc([1024], F32) for _ in range(4)]
            b_wst_c = [Buf("cwst%d" % i_) for i_ in range(4)]
            b_Wc = Buf("Wc")
            b_gb = Buf("gb")
            load_w(Wg, w_in, 8, O_G, 2048, g1t, wst_c, b_wst_c, b_Wc)
            load_w(Wa, w_a, 4, 0, 1024, None, wst_c, b_wst_c, b_Wc)
            load_w(Wb, w_b, 4, 0, 1024, None, wst_c, b_wst_c, b_Wc)
            load_w(Wo, w_out, 8, 0, 1024, None, wst_c, b_wst_c, b_Wc)
            P.op("pool", MSET(gbf, 0.0), writes=[b_gb])
            P.dma(DMA(gbf[64:65, :], gate_b), b_gb, writes=[b_gb])
            P.op("dve", CP(gbb, gbf), reads=[b_gb], writes=[b_Wc])
            xs_t_c = [AR.alloc([D], F32) for _ in range(2)]
            xn_t_c = [AR.alloc([D], BF16) for _ in range(2)]
            hT_t_c = [AR.alloc([8, 128], BF16) for _ in range(2)]
            st4_c = [AR.alloc([4], F32) for _ in range(2)]
            G = AR.alloc([2048], F32)
            t0 = AR.alloc([D], F32)
            t1 = AR.alloc([D], F32)
            mb = AR.alloc([D], BF16)
            mT = AR.alloc([8, 128], BF16)
            x1t = [AR.alloc([D], F32) for _ in range(2)]
            b_xs_c = [Buf("cxs0"), Buf("cxs1")]
            b_xn_c = [Buf("cxn0"), Buf("cxn1")]
            b_hT_c = [Buf("chT0"), Buf("chT1")]
            b_st4_c = [Buf("cst40"), Buf("cst41")]
            b_G = Buf("G")
            b_t0 = Buf("t0")
            b_t1 = Buf("t1")
            b_mb = Buf("mb")
            b_mT = Buf("mT")
            b_x1 = [Buf("x1t0"), Buf("x1t1")]
            def c_stage1(i):
                s_ = i % 2
                if i == NT:
                    P.op("pool", MSET(xs_t_c[s_], 0.0), writes=[b_xs_c[s_]])
                    P.dma(DMA(xs_t_c[s_][0:NS, :], xsm), b_xs_c[s_], writes=[b_xs_c[s_]])
                else:
                    P.dma(DMA(xs_t_c[s_], xp[i * 128:(i + 1) * 128, :]), b_xs_c[s_], writes=[b_xs_c[s_]])
                norm_T(xs_t_c[s_], b_xs_c[s_], xn_t_c[s_], b_xn_c[s_], st4_c[s_], b_st4_c[s_], hT_t_c[s_], b_hT_c[s_], 7)

            def c_stage2(i):
                s_ = i % 2
                for blk in range(4):
                    for kc in range(8):
                        P.op("pe", MM(ps[blk][:, :], hT_t_c[s_][:, kc, :], Wg[:, kc, blk * 512:(blk + 1) * 512], kc == 0, False),
                             reads=[b_hT_c[s_], b_Wc], writes=[psb[blk]])
                    P.op("pe", MM(ps[blk][:, :], sel64b[:], gbb[:, blk * 512:(blk + 1) * 512], False, True), reads=[b_misc, b_Wc], writes=[psb[blk]])
                    P.op("act", ACTV(G[:, blk * 512:(blk + 1) * 512], ps[blk][:, :], AF.Sigmoid), reads=[psb[blk]], writes=[b_G])
                for blk in range(2):
                    for c in range(4):
                        if i < NT:
                            la, lb = ODT[:, c, i * 128:(i + 1) * 128], ODT[:, 4 + c, i * 128:(i + 1) * 128]
                            ra, rb_ = [b_QT[c][i]], [b_QT[4 + c][i]]
                        else:
                            la, lb = odt16[:, c, :], odt16[:, 4 + c, :]
                            ra = rb_ = [b_odt16]
                        P.op("pe", MM(ps[4 + blk][:, :], la, Wa[:, c, blk * 512:(blk + 1) * 512], c == 0, c == 3), reads=ra + [b_Wc], writes=[psb[4 + blk]])
                        P.op("pe", MM(ps[6 + blk][:, :], lb, Wb[:, c, blk * 512:(blk + 1) * 512], c == 0, c == 3), reads=rb_ + [b_Wc], writes=[psb[6 + blk]])
                    P.op("dve", TT(t0[:, blk * 512:(blk + 1) * 512], ps[4 + blk][:, :], G[:, blk * 512:(blk + 1) * 512], ALU.mult), reads=[psb[4 + blk], b_G], writes=[b_t0])
                    P.op("dve", TT(t1[:, blk * 512:(blk + 1) * 512], ps[6 + blk][:, :], G[:, 1024 + blk * 512:1024 + (blk + 1) * 512], ALU.mult), reads=[psb[6 + blk], b_G], writes=[b_t1])
                P.op("pool", TT(mb, t0, t1, ALU.add), reads=[b_t0, b_t1], writes=[b_mb])
                pT0 = ps[0][:].bitcast(BF16)
                for kc in range(8):
                    P.op("pe", TR(pT0[:, kc * 128:(kc + 1) * 128], mb[:, kc * 128:(kc + 1) * 128]), reads=[b_mb, b_misc], writes=[psb[0]])
                P.op("act", ACTV(mT, pT0.rearrange("p (a b) -> p a b", b=128), AF.Copy), reads=[psb[0]], writes=[b_mT])
                for blk in range(2):
                    for kc in range(8):
                        P.op("pe", MM(ps[1 + blk][:, :], mT[:, kc, :], Wo[:, kc, blk * 512:(blk + 1) * 512], kc == 0, kc == 7), reads=[b_mT, b_Wc], writes=[psb[1 + blk]])
                    P.op("dve", TT(x1t[s_][:, blk * 512:(blk + 1) * 512], ps[1 + blk][:, :], xs_t_c[s_][:, blk * 512:(blk + 1) * 512], ALU.add),
                         reads=[psb[1 + blk], b_xs_c[s_]], writes=[b_x1[s_]])
                P.dma(DMA(x1s[i * 128:(i + 1) * 128, :], x1t[s_]), b_x1[s_], reads=[b_x1[s_]], writes=[b_x1s[i]])
            c_stage1(0)
            for i in range(NT + 1):
                if i + 1 <= NT:
                    c_stage1(i + 1)
                c_stage2(i)
            P.barrier()
            AR.release()

        if "D" in phases:
            AR.top = 0
            Wgu = AR.alloc([8, 2 * DFF], BF16)
            Wdn = AR.alloc([NFC, 1024], BF16)
            wst_d = [AR.alloc([1024], F32) for _ in range(2)]
            b_wst_d = [Buf("dwst%d" % i_) for i_ in range(5)]
            b_Wd = Buf("Wd")
            top_d = AR.top
            wst_d = wst_d + [AR.alloc([1024], F32) for _ in range(3)]
            load_w(Wgu, w_gu, 8, 0, 2 * DFF, g2t, wst_d, b_wst_d, b_Wd)
            load_w(Wdn, w_dn, NFC, 0, 1024, None, wst_d, b_wst_d, b_Wd)
            P.barrier()
            AR.top = top_d
            x1b = [[AR.alloc([D], F32) for _ in range(2)] for _ in range(2)]
            b_x1b = [[Buf("x1b%d%d" % (a, b)) for b in range(2)] for a in range(2)]
            xn_t_d = [AR.alloc([D], BF16) for _ in range(2)]
            b_xn_d = [Buf("dxn0"), Buf("dxn1")]
            st4_d = [AR.alloc([4], F32) for _ in range(2)]
            b_st4_d = [Buf("dst40"), Buf("dst41")]
            h2T = [AR.alloc([8, 256], BF16) for _ in range(2)]
            b_h2 = [[Buf("h2T%d%d" % (a, b)) for b in range(2)] for a in range(2)]
            actT = AR.alloc([NFC, 256], BF16)
            b_act = Buf("actT")
            sgt = [AR.alloc([256], F32) for _ in range(2)]
            b_sg = [Buf("sg0"), Buf("sg1")]
            blocks = [[2 * b, 2 * b + 1] for b in range(8)] + [[NT]]
            gcnt = 0
            ncn = 0
            for bi, T in enumerate(blocks):
                sl = bi % 2
                for ti, i in enumerate(T):
                    xt = x1b[sl][ti]
                    P.dma(DMA(xt, x1s[i * 128:(i + 1) * 128, :]), b_x1b[sl][ti], reads=[b_x1s[i]], writes=[b_x1b[sl][ti]])
                    u = ncn % 2
                    ncn += 1
                    norm_T(xt, b_x1b[sl][ti], xn_t_d[u], b_xn_d[u], st4_d[u], b_st4_d[u], h2T[sl][:, :, ti * 128:(ti + 1) * 128], b_h2[sl][ti], 7)
                ntok = 128 * len(T)
                rh = [b_h2[sl][ti] for ti in range(len(T))]
                for fc in range(NFC):
                    a, b = ((0, 1), (2, 3), (4, 5))[gcnt % 3]
                    u = gcnt % 2
                    gcnt += 1
                    for kc in range(8):
                        P.op("pe", MM(ps[a][:, 0:ntok], Wgu[:, kc, fc * 128:(fc + 1) * 128], h2T[sl][:, kc, 0:ntok], kc == 0, kc == 7), reads=rh + [b_Wd], writes=[psb[a]])
                    for kc in range(8):
                        P.op("pe", MM(ps[b][:, 0:ntok], Wgu[:, kc, DFF + fc * 128:DFF + (fc + 1) * 128], h2T[sl][:, kc, 0:ntok], kc == 0, kc == 7), reads=rh + [b_Wd], writes=[psb[b]])
                    P.op("act", ACTV(sgt[u][:, 0:ntok], ps[a][:, 0:ntok], AF.Silu), reads=[psb[a]], writes=[b_sg[u]])
                    P.op("dve", TT(actT[:, fc, 0:ntok], ps[b][:, 0:ntok], sgt[u][:, 0:ntok], ALU.mult), reads=[psb[b], b_sg[u]], writes=[b_act])
                for ti, i in enumerate(T):
                    xt = x1b[sl][ti]
                    for blk in range(2):
                        bank = 6 + blk
                        for fc in range(NFC):
                            P.op("pe", MM(ps[bank][:, :], actT[:, fc, ti * 128:(ti + 1) * 128], Wdn[:, fc, blk * 512:(blk + 1) * 512], fc == 0, fc == NFC - 1),
                                 reads=[b_act, b_Wd], writes=[psb[bank]])
                        P.op("dve", TT(xt[:, blk * 512:(blk + 1) * 512], ps[bank][:, :], xt[:, blk * 512:(blk + 1) * 512], ALU.add),
                             reads=[psb[bank], b_x1b[sl][ti]], writes=[b_x1b[sl][ti]])
                    if i < NT:
                        P.dma(DMA(y_p[i * 128:(i + 1) * 128, :], xt), b_x1b[sl][ti], reads=[b_x1b[sl][ti]])
                    else:
                        P.dma(DMA(y_s, xt[0:NS, :]), b_x1b[sl][ti], reads=[b_x1b[sl][ti]])
            P.barrier()

        P.finish()
        P.emit()
    return nc, P


def _host_inputs(inp):
    f = lambda a: np.ascontiguousarray(np.asarray(a, dtype=np.float32))
    common = {
        "w_in": f(inp["w_in"][0]), "w_a": f(inp["w_branch_a"][0]), "w_b": f(inp["w_branch_b"][0]),
        "w_out": f(inp["w_out"][0]), "w_gu": f(inp["w_gate_up"][0]), "w_dn": f(inp["w_down"][0]),
        "cst": make_consts(),
        "g1T": f(np.asarray(inp["norm1_g"])[0].reshape(8, 128).T),
        "g2T": f(np.asarray(inp["norm2_g"])[0].reshape(8, 128).T),
        "vec": f(np.concatenate([np.asarray(inp["diff_q_g"])[0], np.asarray(inp["diff_k_g"])[0], np.asarray(inp["fox_q_g"])[0],
                                 np.asarray(inp["fox_k_g"])[0], np.asarray(inp["diff_subln_g"])[0], np.asarray(inp["fox_f_b"])[0],
                                 np.asarray(inp["rel_bias"])[31], np.asarray(inp["rel_bias"])[0],
                                 np.asarray(inp["diff_lambda"])[0].reshape(-1)])[None, :]),
        "gate_b": f(np.asarray(inp["gate_b"])[0].reshape(1, -1)),
        "rel_bias": f(inp["rel_bias"]),
    }
    return common


OUT_NAMES = ["y_p", "y_s", "o_dk", "o_dv", "o_fk", "o_fv", "o_lf", "s_dk", "s_dv", "s_fk", "s_fv", "s_lf"]


def run(inp, phases="ABCDS"):
    nc, P = build(phases)
    common = _host_inputs(inp)
    if "C" not in phases:
        for k in ("w_a", "w_b", "w_out", "gate_b"):
            common.pop(k)
    if "D" not in phases:
        for k in ("w_gu", "w_dn"):
            common.pop(k)
    xpr = np.asarray(inp["x_prompt"], dtype=np.float32)
    xsa = np.asarray(inp["x_sample"], dtype=np.float32)
    in_maps = []
    if "S" in phases:
        r2 = lambda k, w: np.asarray(inp[k], dtype=np.float32).reshape(NPHYS * 128, w)
        c_all_host = np.concatenate([r2("cache_diff_k", 512), r2("cache_fox_k", 512), r2("cache_diff_v", 512),
                                     r2("cache_fox_v", 512), r2("cache_fox_logf", 8)], axis=1)
    for c in range(NCORES):
        m = dict(common)
        m["xp"] = np.ascontiguousarray(xpr[c])
        m["xs"] = np.ascontiguousarray(xsa[c * NS:(c + 1) * NS, 0])
        if "S" in phases:
            m["c_all"] = c_all_host
            m["ptab"] = np.ascontiguousarray(np.asarray(inp["page_table"], dtype=np.int32)[c * NS:(c + 1) * NS].reshape(1, -1))
        in_maps.append(m)
    res = run_bass_kernel_spmd(nc, in_maps, core_ids=list(range(NCORES)))
    R = res.results
    if DBG_OUT:
        global DBG_RES
        DBG_RES = R
    cat = lambda k: np.concatenate([np.asarray(r[k]) for r in R], axis=0)
    yp = cat("y_p").reshape(8, SEQ, D)
    ys = cat("y_s").reshape(128, 1, D)
    outs = [yp, ys,
            cat("o_dk").reshape(1, 8, SEQ, 4, 2, 64), cat("o_dv").reshape(1, 8, SEQ, 4, 128),
            cat("o_fk").reshape(1, 8, SEQ, 8, 64), cat("o_fv").reshape(1, 8, SEQ, 8, 64), cat("o_lf").reshape(1, 8, SEQ, 8),
            cat("s_dk").reshape(1, 128, 1, 4, 2, 64), cat("s_dv").reshape(1, 128, 1, 4, 128),
            cat("s_fk").reshape(1, 128, 1, 8, 64), cat("s_fv").reshape(1, 128, 1, 8, 64), cat("s_lf").reshape(1, 128, 1, 8)]
    return tuple(np.ascontiguousarray(o, dtype=np.float32) for o in outs)


def kernel(**inputs):
    return run(inputs, "ABCDS")
```
